# Optimizing a Trainium2 kernel written in Bass

```python
import math
import jax, jax.numpy as jnp
from jax import lax
import numpy as np

D_MODEL = 1024
BATCH = 8
SEQ = 2048
DEPTH = 2
DEC_BATCH = 16
DEC_SEQ = 4096
PAST_LEN = 128

N_HEADS = 16
N_KV_HEADS = 4
HEAD_DIM = 64
Q_PER_KV = N_HEADS // N_KV_HEADS
ATTN_WIDTH = N_HEADS * HEAD_DIM
KV_WIDTH = N_KV_HEADS * HEAD_DIM
WINDOW = 128
BLOCK = 128
NUM_BUCKETS = 32
MAX_DISTANCE = 128
SSD_EXPAND = 2
D_INNER = SSD_EXPAND * D_MODEL
SSD_HEAD_DIM = 64
SSD_HEADS = D_INNER // SSD_HEAD_DIM
SSD_GROUPS = 4
HEADS_PER_GROUP = SSD_HEADS // SSD_GROUPS
D_STATE = 128
CONV_K = 5
CONV_CH = D_INNER + 2 * SSD_GROUPS * D_STATE
CHUNK = 128
MEM_LEN = 256
CROSS_HEADS = 4
CROSS_HEAD_DIM = D_MODEL // CROSS_HEADS
D_FF = ((8 * D_MODEL // 3 + 255) // 256) * 256
IN_WIDTH = ATTN_WIDTH + 2 * KV_WIDTH + D_INNER + CONV_CH + 2 * SSD_HEADS + 2 * D_MODEL
SPLITS = (ATTN_WIDTH,
          ATTN_WIDTH + KV_WIDTH,
          ATTN_WIDTH + 2 * KV_WIDTH,
          ATTN_WIDTH + 2 * KV_WIDTH + D_INNER,
          ATTN_WIDTH + 2 * KV_WIDTH + D_INNER + CONV_CH,
          ATTN_WIDTH + 2 * KV_WIDTH + D_INNER + CONV_CH + 2 * SSD_HEADS)
EPS = 1e-6

kernel_name = 'hybrid_swa_ssd_encoder'


def rmsnorm(x, g):
    xf = x.astype(jnp.float32)
    y = xf * lax.rsqrt(jnp.mean(xf * xf, axis=-1, keepdims=True) + EPS)
    return (y * g.astype(jnp.float32)).astype(x.dtype)


def t5_bucket(rel):
    nb = NUM_BUCKETS // 2
    max_exact = nb // 2
    ret = jnp.where(rel > 0, nb, 0)
    n = jnp.abs(rel)
    nf = jnp.maximum(n, 1).astype(jnp.float32)
    large = max_exact + (jnp.log(nf / max_exact) / math.log(MAX_DISTANCE / max_exact)
                         * (nb - max_exact)).astype(jnp.int32)
    large = jnp.minimum(large, nb - 1)
    return ret + jnp.where(n < max_exact, n, large)


def window_attention(q, k, v, sink, rel_bias):
    b, s = q.shape[0], q.shape[1]
    nblk = s // BLOCK
    span = BLOCK + 2 * WINDOW
    kp = jnp.pad(k, ((0, 0), (WINDOW, WINDOW), (0, 0), (0, 0)))
    vp = jnp.pad(v, ((0, 0), (WINDOW, WINDOW), (0, 0), (0, 0)))
    rel = jnp.arange(span)[None, :] - WINDOW - jnp.arange(BLOCK)[:, None]
    in_window = jnp.abs(rel) <= WINDOW
    bias = jnp.transpose(rel_bias[t5_bucket(rel)], (2, 0, 1))
    bias = bias.reshape(N_KV_HEADS, Q_PER_KV, BLOCK, span).astype(jnp.float32)
    sink_l = sink.reshape(N_KV_HEADS, Q_PER_KV, 1, 1).astype(jnp.float32)
    scale = HEAD_DIM ** -0.5

    def block_fn(i):
        start = i * BLOCK
        qb = lax.dynamic_slice_in_dim(q, start, BLOCK, axis=1)
        kb = lax.dynamic_slice_in_dim(kp, start, span, axis=1)
        vb = lax.dynamic_slice_in_dim(vp, start, span, axis=1)
        key_pos = start - WINDOW + jnp.arange(span)
        valid = in_window & ((key_pos >= 0) & (key_pos < s))[None, :]
        logits = jnp.einsum('bqgrd,bkgd->bgrqk', qb, kb).astype(jnp.float32) * scale + bias
        logits = jnp.where(valid, logits, -jnp.inf)
        m = jnp.maximum(jnp.max(logits, axis=-1, keepdims=True), sink_l)
        e = jnp.exp(logits - m)
        denom = jnp.sum(e, axis=-1, keepdims=True) + jnp.exp(sink_l - m)
        p = (e / denom).astype(v.dtype)
        return jnp.einsum('bgrqk,bkgd->bqgrd', p, vb)

    out = lax.map(block_fn, jnp.arange(nblk))
    return jnp.moveaxis(out, 0, 1).reshape(b, s, ATTN_WIDTH)


def centred_dwconv(u, w, bias):
    pad = CONV_K // 2
    out = lax.conv_general_dilated(u, w[:, None, :].astype(u.dtype), window_strides=(1,),
                                   padding=[(pad, pad)], dimension_numbers=('NWC', 'WIO', 'NWC'),
                                   feature_group_count=u.shape[-1])
    return out + bias.astype(u.dtype)


def segsum(a):
    t = a.shape[-1]
    cs = jnp.cumsum(a, axis=-1)
    diff = cs[..., :, None] - cs[..., None, :]
    mask = jnp.tril(jnp.ones((t, t), dtype=bool))
    return jnp.where(mask, diff, -jnp.inf)


def ssd_scan(xh, dt, a, bm, cm):
    b, l, g, r, p = xh.shape
    n = bm.shape[-1]
    c = l // CHUNK
    xd = (xh * dt[..., None]).reshape(b, c, CHUNK, g, r, p)
    adt = jnp.moveaxis((dt * a).reshape(b, c, CHUNK, g, r), 2, -1)
    bc = bm.reshape(b, c, CHUNK, g, n)
    cc = cm.reshape(b, c, CHUNK, g, n)
    a_cum = jnp.cumsum(adt, axis=-1)
    lmat = jnp.exp(segsum(adt))
    y_diag = jnp.einsum('bclgn,bcsgn,bcgrls,bcsgrp->bclgrp', cc, bc, lmat, xd)
    decay_states = jnp.exp(a_cum[..., -1:] - a_cum)
    states = jnp.einsum('bclgn,bcgrl,bclgrp->bcgrpn', bc, decay_states, xd)
    chunk_a = jnp.moveaxis(a_cum[..., -1], 1, -1)
    decay_chunk = jnp.exp(segsum(jnp.pad(chunk_a, ((0, 0), (0, 0), (0, 0), (1, 0)))))
    states = jnp.pad(states, ((0, 0), (1, 0), (0, 0), (0, 0), (0, 0), (0, 0)))
    states = jnp.einsum('bgrzc,bcgrpn->bzgrpn', decay_chunk, states)[:, :-1]
    y_off = jnp.einsum('bclgn,bcgrpn,bcgrl->bclgrp', cc, states, jnp.exp(a_cum))
    return (y_diag + y_off).reshape(b, l, g, r, p)


def ssd_mixer(xbc, z, dt_raw, conv_w, conv_b, dt_bias, a_log, d_skip, g_norm):
    b, l = xbc.shape[0], xbc.shape[1]
    xbc = jax.nn.silu(centred_dwconv(xbc, conv_w, conv_b))
    xs = xbc[..., :D_INNER]
    bm = xbc[..., D_INNER:D_INNER + SSD_GROUPS * D_STATE].reshape(b, l, SSD_GROUPS, D_STATE)
    cm = xbc[..., D_INNER + SSD_GROUPS * D_STATE:].reshape(b, l, SSD_GROUPS, D_STATE)
    xh = xs.reshape(b, l, SSD_GROUPS, HEADS_PER_GROUP, SSD_HEAD_DIM)
    dt = jax.nn.softplus(dt_raw.reshape(b, l, 2, SSD_HEADS).astype(jnp.float32)
                         + dt_bias.astype(jnp.float32))
    a = -jnp.exp(a_log.astype(jnp.float32))
    dt_f = dt[:, :, 0].reshape(b, l, SSD_GROUPS, HEADS_PER_GROUP)
    dt_b = dt[:, :, 1].reshape(b, l, SSD_GROUPS, HEADS_PER_GROUP)
    a_f = a[0].reshape(SSD_GROUPS, HEADS_PER_GROUP)
    a_b = a[1].reshape(SSD_GROUPS, HEADS_PER_GROUP)
    y_f = ssd_scan(xh, dt_f, a_f, bm, cm)
    y_b = jnp.flip(ssd_scan(jnp.flip(xh, 1), jnp.flip(dt_b, 1), a_b,
                            jnp.flip(bm, 1), jnp.flip(cm, 1)), 1)
    y = y_f + y_b + xh * d_skip.reshape(SSD_GROUPS, HEADS_PER_GROUP, 1)
    y = y.reshape(b, l, D_INNER).astype(xbc.dtype)
    return rmsnorm(y * jax.nn.silu(z), g_norm)


def memory_cross_attention(h, mem, g_mem, w_q, w_kv, w_o):
    b, s = h.shape[0], h.shape[1]
    mh = rmsnorm(mem, g_mem)
    q = (h @ w_q).reshape(b, s, CROSS_HEADS, CROSS_HEAD_DIM)
    kv = mh @ w_kv
    k = kv[..., :D_MODEL].reshape(b, -1, CROSS_HEADS, CROSS_HEAD_DIM)
    v = kv[..., D_MODEL:].reshape(b, -1, CROSS_HEADS, CROSS_HEAD_DIM)
    logits = jnp.einsum('bshd,bmhd->bhsm', q, k).astype(jnp.float32) * CROSS_HEAD_DIM ** -0.5
    p = jax.nn.softmax(logits, axis=-1).astype(v.dtype)
    o = jnp.einsum('bhsm,bmhd->bshd', p, v).reshape(b, s, D_MODEL)
    return o @ w_o


def encoder_layer(x, mem, rel_bias, norm_mix, w_in, attn_sink, conv_w, conv_b, dt_bias, a_log,
                  d_skip, norm_ssd, w_proj_attn, w_proj_ssd, w_out, norm_cross, norm_mem,
                  w_q_cross, w_kv_cross, w_o_cross, norm_ffn, w_gate_up, w_down):
    b, s = x.shape[0], x.shape[1]
    h = rmsnorm(x, norm_mix)
    proj = h @ w_in
    q, k, v, z, xbc, dt_raw, gates = jnp.split(proj, SPLITS, axis=-1)
    q = q.reshape(b, s, N_KV_HEADS, Q_PER_KV, HEAD_DIM)
    k = k.reshape(b, s, N_KV_HEADS, HEAD_DIM)
    v = v.reshape(b, s, N_KV_HEADS, HEAD_DIM)
    attn_o = window_attention(q, k, v, attn_sink, rel_bias)
    ssd_o = ssd_mixer(xbc, z, dt_raw, conv_w, conv_b, dt_bias, a_log, d_skip, norm_ssd)
    gates = jax.nn.sigmoid(gates)
    mixed = gates[..., :D_MODEL] * (attn_o @ w_proj_attn) + gates[..., D_MODEL:] * (ssd_o @ w_proj_ssd)
    x = x + mixed @ w_out
    x = x + memory_cross_attention(rmsnorm(x, norm_cross), mem, norm_mem, w_q_cross, w_kv_cross, w_o_cross)
    gu = rmsnorm(x, norm_ffn) @ w_gate_up
    x = x + (jax.nn.silu(gu[..., :D_FF]) * gu[..., D_FF:]) @ w_down
    return x


def encoder(x, mem, rel_bias, norm_mix, w_in, attn_sink, conv_w, conv_b, dt_bias, a_log, d_skip,
            norm_ssd, w_proj_attn, w_proj_ssd, w_out, norm_cross, norm_mem, w_q_cross, w_kv_cross,
            w_o_cross, norm_ffn, w_gate_up, w_down, norm_final):
    for l in range(DEPTH):
        x = encoder_layer(x, mem, rel_bias, norm_mix[l], w_in[l], attn_sink[l], conv_w[l], conv_b[l],
                          dt_bias[l], a_log[l], d_skip[l], norm_ssd[l], w_proj_attn[l], w_proj_ssd[l],
                          w_out[l], norm_cross[l], norm_mem[l], w_q_cross[l], w_kv_cross[l],
                          w_o_cross[l], norm_ffn[l], w_gate_up[l], w_down[l])
    return rmsnorm(x, norm_final)


def setup_inputs(seed: int = 0) -> dict:
    key = jax.random.key(seed)
    ks = jax.random.split(key, 32)
    f32 = jnp.float32

    def nrm(k, shape, scale):
        return jax.random.normal(k, shape, f32) * scale

    def gain(k, shape):
        return 1.0 + 0.02 * jax.random.normal(k, shape, f32)

    dt0 = jnp.exp(jax.random.uniform(ks[10], (DEPTH, 2, SSD_HEADS), f32,
                                     minval=math.log(1e-3), maxval=math.log(1e-1)))
    return {
        'x_prompt': nrm(ks[0], (BATCH, SEQ, D_MODEL), 1.0),
        'x_sample': nrm(ks[1], (DEC_BATCH, DEC_SEQ, D_MODEL), 1.0),
        'mem_prompt': nrm(ks[2], (BATCH, MEM_LEN, D_MODEL), 1.0),
        'mem_sample': nrm(ks[3], (DEC_BATCH, MEM_LEN, D_MODEL), 1.0),
        'rel_bias': nrm(ks[4], (NUM_BUCKETS, N_HEADS), 0.5),
        'norm_mix': gain(ks[5], (DEPTH, D_MODEL)),
        'w_in': nrm(ks[6], (DEPTH, D_MODEL, IN_WIDTH), D_MODEL ** -0.5),
        'attn_sink': nrm(ks[7], (DEPTH, N_HEADS), 0.5),
        'conv_w': nrm(ks[8], (DEPTH, CONV_K, CONV_CH), CONV_K ** -0.5),
        'conv_b': nrm(ks[9], (DEPTH, CONV_CH), 0.02),
        'dt_bias': dt0 + jnp.log(-jnp.expm1(-dt0)),
        'a_log': jnp.log(jax.random.uniform(ks[11], (DEPTH, 2, SSD_HEADS), f32, minval=1.0, maxval=16.0)),
        'd_skip': gain(ks[12], (DEPTH, SSD_HEADS)),
        'norm_ssd': gain(ks[13], (DEPTH, D_INNER)),
        'w_proj_attn': nrm(ks[14], (DEPTH, ATTN_WIDTH, D_MODEL), ATTN_WIDTH ** -0.5),
        'w_proj_ssd': nrm(ks[15], (DEPTH, D_INNER, D_MODEL), D_INNER ** -0.5),
        'w_out': nrm(ks[16], (DEPTH, D_MODEL, D_MODEL), D_MODEL ** -0.5),
        'norm_cross': gain(ks[17], (DEPTH, D_MODEL)),
        'norm_mem': gain(ks[18], (DEPTH, D_MODEL)),
        'w_q_cross': nrm(ks[19], (DEPTH, D_MODEL, D_MODEL), D_MODEL ** -0.5),
        'w_kv_cross': nrm(ks[20], (DEPTH, D_MODEL, 2 * D_MODEL), D_MODEL ** -0.5),
        'w_o_cross': nrm(ks[21], (DEPTH, D_MODEL, D_MODEL), D_MODEL ** -0.5),
        'norm_ffn': gain(ks[22], (DEPTH, D_MODEL)),
        'w_gate_up': nrm(ks[23], (DEPTH, D_MODEL, 2 * D_FF), D_MODEL ** -0.5),
        'w_down': nrm(ks[24], (DEPTH, D_FF, D_MODEL), D_FF ** -0.5),
        'norm_final': gain(ks[25], (D_MODEL,)),
    }


def reference(x_prompt, x_sample, mem_prompt, mem_sample, rel_bias, norm_mix, w_in, attn_sink,
              conv_w, conv_b, dt_bias, a_log, d_skip, norm_ssd, w_proj_attn, w_proj_ssd, w_out,
              norm_cross, norm_mem, w_q_cross, w_kv_cross, w_o_cross, norm_ffn, w_gate_up, w_down,
              norm_final):
    y_prompt = encoder(x_prompt, mem_prompt, rel_bias, norm_mix, w_in, attn_sink, conv_w, conv_b,
                       dt_bias, a_log, d_skip, norm_ssd, w_proj_attn, w_proj_ssd, w_out, norm_cross,
                       norm_mem, w_q_cross, w_kv_cross, w_o_cross, norm_ffn, w_gate_up, w_down, norm_final)
    y_sample = encoder(x_sample, mem_sample, rel_bias, norm_mix, w_in, attn_sink, conv_w, conv_b,
                       dt_bias, a_log, d_skip, norm_ssd, w_proj_attn, w_proj_ssd, w_out, norm_cross,
                       norm_mem, w_q_cross, w_kv_cross, w_o_cross, norm_ffn, w_gate_up, w_down, norm_final)
    return (y_prompt, y_sample)
```

```python
import math
import os
from contextlib import ExitStack
import numpy as np
import concourse.bass as bass
import concourse.mybir as mybir
from concourse.bass_utils import run_bass_kernel_spmd

F32 = mybir.dt.float32
BF16 = mybir.dt.bfloat16
AF = mybir.ActivationFunctionType
ALU = mybir.AluOpType

D = 1024
DEPTH = 2
EPS = 1e-6
W_SHAPES = {
    "w_in": (1024, 8768), "w_proj_attn": (1024, 1024), "w_proj_ssd": (2048, 1024), "w_out": (1024, 1024),
    "w_q_cross": (1024, 1024), "w_kv_cross": (1024, 2048), "w_o_cross": (1024, 1024),
    "w_gate_up": (1024, 5632), "w_down": (2816, 1024),
}
W_ORDER = ["w_in", "w_proj_attn", "w_proj_ssd", "w_out", "w_kv_cross", "w_q_cross", "w_o_cross", "w_gate_up", "w_down"]
SMALL = {
    "rel_bias": (32, 16), "norm_mix": (2, 1024), "attn_sink": (2, 16), "conv_w": (2, 5, 3072), "conv_b": (2, 3072),
    "dt_bias": (2, 2, 32), "a_log": (2, 2, 32), "d_skip": (2, 32), "norm_ssd": (2, 2048), "norm_cross": (2, 1024),
    "norm_mem": (2, 1024), "norm_ffn": (2, 1024), "norm_final": (1024,),
}


def _bucket(rel):
    nb = 16
    max_exact = 8
    ret = np.where(rel > 0, nb, 0)
    n = np.abs(rel)
    nf = np.maximum(n, 1).astype(np.float32)
    large = max_exact + (np.log(nf / np.float32(max_exact)) / np.float32(math.log(16.0))
                         * np.float32(nb - max_exact)).astype(np.int32)
    large = np.minimum(large, nb - 1)
    return ret + np.where(n < max_exact, n, large)


def host_consts():
    i = np.arange(128)
    ident = np.eye(128, dtype=np.float32)
    triu = (i[:, None] <= i[None, :]).astype(np.float32)
    tril = (i[:, None] >= i[None, :]).astype(np.float32)
    sl = (i[:, None] > i[None, :]).astype(np.float32)
    su = (i[:, None] < i[None, :]).astype(np.float32)
    ones = np.ones((128, 128), np.float32)
    m = np.arange(512)
    rel = 255 - m
    win = ((np.abs(rel) <= 128) & (m < 511)).astype(np.float32)
    cst = np.concatenate([ident, triu, tril, sl, su, ones, np.tile(win[None, :], (128, 1))], axis=1)
    b = _bucket(rel.astype(np.int32))
    onehot = np.zeros((32, 512), np.float32)
    for mm in range(511):
        onehot[b[mm], mm] = 1.0
    return np.ascontiguousarray(cst), onehot


class Tok:
    __slots__ = ("key", "sem", "val")

    def __init__(self, key, sem, val):
        self.key, self.sem, self.val = key, sem, val


class Buf:
    def __init__(self, t=None, name=""):
        self.t = t
        self.name = name
        self.w = {}
        self.r = {}
        self.sems = {}
        self.pending = False


class Ctx:
    def __init__(self, nc, es):
        self.nc = nc
        self.es = es
        self.E = {"pe": nc.tensor, "act": nc.scalar, "dve": nc.vector, "pool": nc.gpsimd, "sp": nc.sync}
        self.sem = {}
        self.cnt = {}
        self.seen = {}
        self.last = {}
        for e in self.E:
            self.sem[e] = es.enter_context(nc.semaphore("sem_" + e))
            self.cnt[e] = 0
            self.seen[e] = {}
        self.dma_last = {}
        self.nsem = 5
        self.rings = {}
        self.ps_ptr = 0
        self.flip = 0
        self.sempool = []
        self.phase_sems = []
        self.phase_open = False

    def begin_phase(self):
        self.phase_open = True

    def end_phase(self):
        self.barrier()
        self.sempool.extend(self.phase_sems)
        self.phase_sems = []
        self.phase_open = False

    def wait(self, e, tok):
        if tok is None:
            return
        if tok.key == e and e == "pe":
            return
        seen = self.seen[e]
        if seen.get(tok.key, 0) >= tok.val:
            return
        self.E[e].wait_ge(tok.sem, tok.val)
        seen[tok.key] = tok.val

    def pre(self, e, reads, writes):
        for b in reads:
            for t in b.w.values():
                self.wait(e, t)
        for b in writes:
            for t in b.w.values():
                self.wait(e, t)
            for t in b.r.values():
                self.wait(e, t)

    def _book(self, tok, reads, writes):
        for b in reads:
            b.r[tok.key] = tok
            if tok.key != "pe":
                b.pending = False
        for b in writes:
            if b.r:
                b.w = {}
                b.r = {}
            b.w[tok.key] = tok
            if tok.key == "pe":
                b.pending = True

    def post(self, e, ins, reads, writes):
        self.cnt[e] += 1
        ins.then_inc(self.sem[e], 1)
        tok = Tok(e, self.sem[e], self.cnt[e])
        self.last[e] = tok
        self._book(tok, reads, writes)
        return tok

    def op(self, e, reads, writes, fn):
        self.pre(e, reads, writes)
        ins = fn(self.E[e])
        self.post(e, ins, reads, writes)
        return ins

    def dma(self, q, out, in_, reads, writes, semb, kind="ld", **kw):
        self.pre(q, reads, writes)
        ins = self.E[q].dma_start(out=out, in_=in_, **kw)
        if kind not in semb.sems:
            if self.sempool:
                ent = self.sempool.pop()
            else:
                ent = [self.es.enter_context(self.nc.semaphore("d%d" % self.nsem)), 0, self.nsem]
                self.nsem += 1
            semb.sems[kind] = ent
            if self.phase_open:
                self.phase_sems.append(ent)
        rec = semb.sems[kind]
        rec[1] += 16
        ins.then_inc(rec[0], 16)
        tok = Tok(("d", rec[2]), rec[0], rec[1])
        self.dma_last[tok.key] = tok
        self._book(tok, reads, writes)
        return tok

    def barrier(self):
        nobar = getattr(self, "nobar", set())
        toks = list(self.last.values()) + [t for k, t in self.dma_last.items() if k not in nobar]
        for e in ("pe", "act", "dve", "pool", "sp"):
            for t in toks:
                if t.key == e:
                    continue
                self.wait(e, t)

    def tile(self, name, shape, dt):
        t = self.es.enter_context(self.nc.sbuf_tensor(name, list(shape), dt))
        return Buf(t, name)

    def mkring(self, es, name, n, shape, dt):
        self.uid = getattr(self, "uid", 0) + 1
        self.rings[name] = [[Buf(es.enter_context(self.nc.sbuf_tensor("%s_%d_%d" % (name, self.uid, i), list(shape), dt)),
                                 "%s%d" % (name, i)) for i in range(n)], 0]

    def ring(self, name):
        r = self.rings[name]
        b = r[0][r[1] % len(r[0])]
        r[1] += 1
        return b

    def psum(self, n=1):
        if not hasattr(self, "lru"):
            self.lru = list(range(8))
        for st in self.lru:
            if st + n <= 8 and all(not self.PSB[st + i].pending for i in range(n)):
                out = [self.PSB[st + i] for i in range(n)]
                for i in range(n):
                    self.lru.remove(st + i)
                    self.lru.append(st + i)
                    self.PSB[st + i].pending = True
                return out
        raise AssertionError("no free psum banks for n=%d: pending=%s" % (n, [b.pending for b in self.PSB]))

    def alt(self):
        self.flip ^= 1
        return "act" if self.flip else "dve"


_UQ = [0]


def uq(name):
    _UQ[0] += 1
    return "%s_u%d" % (name, _UQ[0])


def bc(ap2, m):
    p, n = ap2.shape
    return ap2.unsqueeze(2).broadcast_to([p, n, m])


def build(seq_lens, dbg=False, stop=None):
    nc = bass.Bass("TRN2", target_bir_lowering=False)
    NS = len(seq_lens)
    xin = [nc.dram_tensor("x%d" % s, [seq_lens[s], D], F32, kind="ExternalInput").ap() for s in range(NS)]
    memin = [nc.dram_tensor("mem%d" % s, [256, D], F32, kind="ExternalInput").ap() for s in range(NS)]
    Wf = {k: nc.dram_tensor(k, [DEPTH] + list(v), F32, kind="ExternalInput").ap() for k, v in W_SHAPES.items()}
    Sm = {k: nc.dram_tensor(k, list(v), F32, kind="ExternalInput").ap() for k, v in SMALL.items()}
    cst_d = nc.dram_tensor("cst", [128, 1280], F32, kind="ExternalInput").ap()
    oh_d = nc.dram_tensor("onehot", [32, 512], F32, kind="ExternalInput").ap()
    yout = [nc.dram_tensor("y%d" % s, [seq_lens[s], D], F32, kind="ExternalOutput").ap() for s in range(NS)]

    def scr(name, shape, dt):
        return nc.dram_tensor(name, list(shape), dt, kind=("ExternalOutput" if (dbg and not name.startswith("wb_")) else "Internal")).ap()

    Wb = {k: [scr("wb_%s%d" % (k, l), v, BF16) for l in range(DEPTH)] for k, v in W_SHAPES.items()}
    Rd = scr("Rd", [16, 129 * 512], BF16)
    S = {}
    for s in range(NS):
        L = seq_lens[s]
        for l in range(DEPTH):
            S[s, l] = dict(
                qT=scr("qT%d%d" % (s, l), [1024, L], BF16), kT=scr("kT%d%d" % (s, l), [256, L], BF16),
                v=scr("v%d%d" % (s, l), [L, 256], BF16), sz=scr("sz%d%d" % (s, l), [L, 2048], BF16),
                xbc=scr("xbc%d%d" % (s, l), [3072, L + 4], BF16), dt=scr("dt%d%d" % (s, l), [L, 64], F32),
                gat=scr("gat%d%d" % (s, l), [2048, L], BF16), xs=scr("xs%d%d" % (s, l), [L, 2048], BF16),
                B=scr("B%d%d" % (s, l), [L, 512], BF16), BT=scr("BT%d%d" % (s, l), [512, L], BF16),
                CT=scr("CT%d%d" % (s, l), [512, L], BF16), yb=scr("yb%d%d" % (s, l), [L, 2048], BF16),
                ssdT=scr("ssdT%d%d" % (s, l), [2048, L], BF16), OT=scr("OT%d%d" % (s, l), [1024, L], BF16),
                xn=scr("xn%d%d" % (s, l), [L, D], F32),
            )

    with ExitStack() as es:
        cx = Ctx(nc, es)
        PS = es.enter_context(nc.psum_tensor("psall", [128, 8, 512], F32))
        cx.PSB = [Buf(None, "ps%d" % i) for i in range(8)]

        def psap(b):
            return PS[:, cx.PSB.index(b), :]

        def psbf(b):
            return PS[:, cx.PSB.index(b), :].bitcast(BF16)

        cstf = cx.tile("cstf", [128, 1280], F32)
        cstb = cx.tile("cstb", [128, 768], BF16)
        ident_f = cstf.t[:, 0:128]
        triu_f = cstf.t[:, 128:256]
        tril_f = cstf.t[:, 256:384]
        ones_f = cstf.t[:, 640:768]
        win_f = cstf.t[:, 768:1280]
        ident_b = cstb.t[:, 0:128]
        sl_b = cstb.t[:, 384:512]
        su_b = cstb.t[:, 512:640]
        ones_b = cstb.t[:, 640:768]
        gmix = cx.tile("gmix", [128, 2, 8], F32)
        gcross = cx.tile("gcross", [128, 2, 8], F32)
        gmem = cx.tile("gmem", [128, 2, 8], F32)
        gffn = cx.tile("gffn", [128, 2, 8], F32)
        gssd = cx.tile("gssd", [128, 2, 16], F32)
        gfin = cx.tile("gfin", [128, 1024], F32)
        dtb = cx.tile("dtb", [128, 128], F32)
        aneg = cx.tile("aneg", [128, 128], F32)
        dsk = cx.tile("dsk", [128, 64], F32)
        esink = cx.tile("esink", [128, 32], F32)
        cw = cx.tile("cw", [128, 2, 5, 24], F32)
        cb = cx.tile("cb", [128, 2, 24], F32)
        epsT = cx.tile("epsT", [128, 1], F32)
        oneT = cx.tile("oneT", [128, 1], F32)

        pl = Buf(None, "pl")
        with nc.allow_non_contiguous_dma(reason="small parameter loads"):
            cx.dma("sp", cstf.t[:], cst_d[:, :], [], [cstf], pl)
            for tl, nm in ((gmix, "norm_mix"), (gcross, "norm_cross"), (gmem, "norm_mem"), (gffn, "norm_ffn")):
                for l in range(DEPTH):
                    cx.dma("sp", tl.t[:, l, :], Sm[nm][l, :].rearrange("(c p) -> p c", p=128), [], [tl], pl)
            for l in range(DEPTH):
                cx.dma("sp", gssd.t[:, l, :], Sm["norm_ssd"][l, :].rearrange("(c p) -> p c", p=128), [], [gssd], pl)
                for k in range(5):
                    cx.dma("sp", cw.t[:, l, k, :], Sm["conv_w"][l, k, :].rearrange("(c p) -> p c", p=128), [], [cw], pl)
                cx.dma("sp", cb.t[:, l, :], Sm["conv_b"][l, :].rearrange("(c p) -> p c", p=128), [], [cb], pl)
            cx.dma("sp", gfin.t[:], Sm["norm_final"].partition_broadcast(128), [], [gfin], pl)
            cx.dma("sp", dtb.t[:], Sm["dt_bias"].rearrange("l a h -> (l a h)").partition_broadcast(128), [], [dtb], pl)
            cx.dma("sp", aneg.t[:], Sm["a_log"].rearrange("l a h -> (l a h)").partition_broadcast(128), [], [aneg], pl)
            cx.dma("sp", dsk.t[:], Sm["d_skip"].rearrange("l h -> (l h)").partition_broadcast(128), [], [dsk], pl)
            cx.dma("sp", esink.t[:], Sm["attn_sink"].rearrange("l h -> (l h)").partition_broadcast(128), [], [esink], pl)

        cx.barrier()
        WB = {}

        def cast_weights(l):
            for k in W_ORDER:
                K, N = W_SHAPES[k]
                b = N
                while b > 2048 or N % b:
                    b -= 1
                    while N % b:
                        b -= 1
                wbuf = Buf(None, "wb_%s%d" % (k, l))
                WB[k, l] = wbuf
                for r0 in range(0, K, 128):
                    r1 = min(K, r0 + 128)
                    tk = cx.dma("pool", Wb[k][l][r0:r1, :].rearrange("r (a b) -> r a b", b=b),
                                Wf[k][l, r0:r1, :].rearrange("r (a b) -> r a b", b=b), [], [wbuf], wbuf)
                    if not hasattr(cx, "nobar"):
                        cx.nobar = set()
                    cx.nobar.add(tk.key)

        cast_weights(0)

        cx.op("dve", [cstf], [cstb], lambda e: e.tensor_copy(out=cstb.t[:], in_=cstf.t[:, 0:768]))
        cx.op("dve", [], [epsT], lambda e: e.memset(epsT.t[:], EPS))
        cx.op("dve", [], [oneT], lambda e: e.memset(oneT.t[:], 1.0))
        cx.op("act", [aneg], [aneg], lambda e: e.activation(out=aneg.t[:], in_=aneg.t[:], func=AF.Exp))
        cx.op("dve", [aneg], [aneg], lambda e: e.tensor_scalar(out=aneg.t[:], in0=aneg.t[:], scalar1=-1.0, scalar2=None, op0=ALU.mult))
        cx.op("act", [esink], [esink], lambda e: e.activation(out=esink.t[:], in_=esink.t[:], func=AF.Exp))

        with ExitStack() as pes:
            cx.mkring(pes, "Rt", 2, [128, 512], BF16)
            win8 = Buf(pes.enter_context(nc.sbuf_tensor(uq("win8"), [128, 512], F32)))
            negm = Buf(pes.enter_context(nc.sbuf_tensor(uq("negm"), [128, 512], F32)))
            rtmp = Buf(pes.enter_context(nc.sbuf_tensor(uq("rtmp"), [128, 512], F32)))
            cx.op("dve", [cstf], [win8], lambda e: e.tensor_scalar(out=win8.t[:], in0=win_f, scalar1=8.0, scalar2=None, op0=ALU.mult))
            cx.op("dve", [cstf], [negm], lambda e: e.tensor_scalar(out=negm.t[:], in0=win_f, scalar1=-1.0, scalar2=30000.0, op0=ALU.add, op1=ALU.mult))
            oh = Buf(pes.enter_context(nc.sbuf_tensor(uq("oh"), [32, 512], F32)))
            rb = Buf(pes.enter_context(nc.sbuf_tensor(uq("rb"), [32, 16], F32)))
            rbrep = Buf(pes.enter_context(nc.sbuf_tensor(uq("rbrep"), [32, 16, 128], F32)))
            cx.dma("sp", oh.t[:], oh_d[:, :], [], [oh], pl)
            cx.dma("sp", rb.t[:], Sm["rel_bias"][:, :], [], [rb], pl)
            cx.barrier()
            cx.op("dve", [rb], [rbrep], lambda e: e.tensor_copy(out=rbrep.t[:], in_=bc(rb.t[:, :], 128)))
            for h in range(16):
                (bk,) = cx.psum(1)
                cx.op("pe", [rbrep, oh], [bk], lambda e: e.matmul(psap(bk), lhsT=rbrep.t[:, h, :], rhs=oh.t[:, :], start=True, stop=True))
                Rt = cx.ring("Rt")
                cx.op("dve", [bk, win8], [rtmp], lambda e: e.tensor_tensor(out=rtmp.t[:], in0=psap(bk), in1=win8.t[:], op=ALU.mult))
                cx.op("dve", [rtmp, negm], [Rt], lambda e: e.tensor_tensor(out=Rt.t[:], in0=rtmp.t[:], in1=negm.t[:], op=ALU.add))
                cx.dma("pool", Rd[h, 0:65536].rearrange("(p m) -> p m", m=512), Rt.t[:], [Rt], [], Rt, "st")
            cx.barrier()

        def load_tatt(Tatt):
            for h in range(16):
                for j in range(3):
                    off = 383 - 128 * j
                    src = Rd[h, off:off + 128 * 511].rearrange("(p m) -> p m", m=511)[:, 0:128]
                    cx.dma("sp", Tatt.t[:, h, j, :], src, [], [Tatt], Tatt)

        def normT(X, nts, g_ap, hT, ss, rstd, xnr, junk):
            for ts in range(nts):
                cx.op("act", [X], [junk, ss], lambda e: e.activation(out=junk.t[:], in_=X.t[:, ts, :], func=AF.Square,
                                                                      accum_out=ss.t[:, ts:ts + 1]))
            cx.op("act", [ss, epsT], [ss], lambda e: e.activation(out=ss.t[:, 0:nts], in_=ss.t[:, 0:nts], func=AF.Ln,
                                                                   scale=1.0 / 1024, bias=epsT.t[:, 0:1]))
            cx.op("act", [ss], [rstd], lambda e: e.activation(out=rstd.t[:, 0:nts], in_=ss.t[:, 0:nts], func=AF.Exp, scale=-0.5))
            for ts in range(nts):
                xn = cx.ring(xnr)
                cx.op("dve", [X, rstd], [xn], lambda e: e.tensor_scalar(out=xn.t[:], in0=X.t[:, ts, :], scalar1=rstd.t[:, ts:ts + 1],
                                                                         scalar2=None, op0=ALU.mult))
                (bk,) = cx.psum(1)
                pv = psbf(bk).rearrange("p (k j) -> p k j", j=128)
                cx.pre("pe", [xn, cstb], [bk])
                for kc in range(8):
                    ins = nc.tensor.transpose(out=pv[:, kc, :], in_=xn.t[:, kc * 128:(kc + 1) * 128], identity=ident_b)
                cx.post("pe", ins, [xn, cstb], [bk])
                cx.op("dve", [bk], [hT], lambda e: e.tensor_tensor(out=hT.t[:, 0:8, ts * 128:(ts + 1) * 128], in0=pv[:, 0:8, :],
                                                                    in1=bc(g_ap, 128), op=ALU.mult))

        def gemm(A, KC, kp, wkey, l, c0, nw, mode, T, evac):
            W = Wb[wkey][l]
            Ab = getattr(A, "buf", A)
            segs = [(k0, min(k0 + 8, KC)) for k0 in range(0, KC, 8)]
            nout = nw // 128 if mode == "FM" else T // 128
            banks = cx.psum(nout)
            for (k0, k1) in segs:
                wt = cx.ring("wslot")
                cx.dma("sp", wt.t[0:kp, 0:k1 - k0, 0:nw], W[k0 * kp:k1 * kp, c0:c0 + nw].rearrange("(k p) n -> p k n", p=kp),
                       [WB[wkey, l]], [wt], wt)
                for o in range(nout):
                    cx.pre("pe", [wt, Ab], [banks[o]])
                    for k in range(k0, k1):
                        if mode == "FM":
                            ins = nc.tensor.matmul(psap(banks[o])[:, 0:T], lhsT=wt.t[0:kp, k - k0, o * 128:(o + 1) * 128],
                                                   rhs=A.t[0:kp, k, 0:T], start=(k == 0), stop=(k == KC - 1))
                        else:
                            ins = nc.tensor.matmul(psap(banks[o])[:, 0:nw], lhsT=A.t[0:kp, k, o * 128:(o + 1) * 128],
                                                   rhs=wt.t[0:kp, k - k0, 0:nw], start=(k == 0), stop=(k == KC - 1))
                    cx.post("pe", ins, [wt, Ab], [banks[o]])
            for o in range(nout):
                evac(o, banks[o])

        def evac_store(bk, ncols, dst, func=None, ring="stb", eng=None):
            st = cx.ring(ring)
            if func is None:
                e = eng or cx.alt()
                if e == "act":
                    cx.op("act", [bk], [st], lambda e_: e_.activation(out=st.t[:, 0:ncols], in_=psap(bk)[:, 0:ncols], func=AF.Copy))
                else:
                    cx.op("dve", [bk], [st], lambda e_: e_.tensor_copy(out=st.t[:, 0:ncols], in_=psap(bk)[:, 0:ncols]))
            else:
                cx.op("act", [bk], [st], lambda e_: e_.activation(out=st.t[:, 0:ncols], in_=psap(bk)[:, 0:ncols], func=func))
            cx.dma("pool", dst, st.t[:, 0:ncols], [st], [], st, "st")

        def phaseA(s, l):
            L = seq_lens[s]
            sc = S[s, l]
            xsrc = xin[s] if l == 0 else S[s, l - 1]["xn"]
            with ExitStack() as pes:
                cx.mkring(pes, "X", 2, [128, 4, 1024], F32)
                cx.mkring(pes, "hT", 2, [128, 8, 512], BF16)
                cx.mkring(pes, "xnr", 2, [128, 1024], BF16)
                cx.mkring(pes, "wslot", 7, [128, 8, 512], BF16)
                cx.mkring(pes, "stb", 8, [128, 512], BF16)
                cx.mkring(pes, "stf", 3, [128, 64], F32)
                ss = Buf(pes.enter_context(nc.sbuf_tensor(uq("ssA"), [128, 4], F32)))
                rstd = Buf(pes.enter_context(nc.sbuf_tensor(uq("rstdA"), [128, 4], F32)))
                junk = Buf(pes.enter_context(nc.sbuf_tensor(uq("junkA"), [128, 1024], BF16)))
                zt = Buf(pes.enter_context(nc.sbuf_tensor(uq("zt"), [128, 24, 2], BF16)))
                cx.op("dve", [], [zt], lambda e: e.memset(zt.t[:], 0.0))
                with nc.allow_non_contiguous_dma(reason="zero halo"):
                    cx.dma("pool", sc["xbc"][:, 0:2].rearrange("(c p) t -> p c t", p=128), zt.t[:], [zt], [], zt, "st")
                    cx.dma("pool", sc["xbc"][:, L + 2:L + 4].rearrange("(c p) t -> p c t", p=128), zt.t[:], [zt], [], zt, "st")
                def load_norm(st_j):
                    tj = st_j * 512
                    Xj = cx.ring("X")
                    cx.dma("sp", Xj.t[:], xsrc[tj:tj + 512, :].rearrange("(ts p) d -> p ts d", p=128), [], [Xj], Xj)
                    hTj = cx.ring("hT")
                    normT(Xj, 4, gmix.t[:, l, :], hTj, ss, rstd, "xnr", junk)
                    return hTj
                hT_next = load_norm(0)
                for st_i in range(L // 512):
                    t0 = st_i * 512
                    hT = hT_next
                    for cg in range(2):
                        gemm(hT, 8, 128, "w_in", l, cg * 512, 512, "FM", 512,
                             lambda o, bk, cg=cg: evac_store(bk, 512, sc["qT"][(cg * 4 + o) * 128:(cg * 4 + o + 1) * 128, t0:t0 + 512]))
                    gemm(hT, 8, 128, "w_in", l, 1024, 256, "FM", 512,
                         lambda o, bk: evac_store(bk, 512, sc["kT"][o * 128:(o + 1) * 128, t0:t0 + 512]))
                    gemm(hT, 8, 128, "w_in", l, 1280, 256, "TM", 512,
                         lambda o, bk: evac_store(bk, 256, sc["v"][t0 + o * 128:t0 + (o + 1) * 128, :]))
                    for cg in range(4):
                        gemm(hT, 8, 128, "w_in", l, 1536 + cg * 512, 512, "TM", 512,
                             lambda o, bk, cg=cg: evac_store(bk, 512, sc["sz"][t0 + o * 128:t0 + (o + 1) * 128, cg * 512:(cg + 1) * 512], func=AF.Silu))
                    for cg in range(6):
                        gemm(hT, 8, 128, "w_in", l, 3584 + cg * 512, 512, "FM", 512,
                             lambda o, bk, cg=cg: evac_store(bk, 512, sc["xbc"][(cg * 4 + o) * 128:(cg * 4 + o + 1) * 128, 2 + t0:2 + t0 + 512]))

                    def evac_dt(o, bk):
                        st = cx.ring("stf")
                        cx.op("dve", [bk, dtb], [st], lambda e: e.tensor_tensor(out=st.t[:], in0=psap(bk)[:, 0:64], in1=dtb.t[:, l * 64:(l + 1) * 64], op=ALU.add))
                        cx.op("act", [st], [st], lambda e: e.activation(out=st.t[:], in_=st.t[:], func=AF.Exp))
                        cx.op("act", [st, oneT], [st], lambda e: e.activation(out=st.t[:], in_=st.t[:], func=AF.Ln, bias=oneT.t[:, 0:1], scale=1.0))
                        cx.dma("pool", sc["dt"][t0 + o * 128:t0 + (o + 1) * 128, :], st.t[:], [st], [], st, "st")
                    if st_i + 1 < L // 512:
                        hT_next = load_norm(st_i + 1)
                    gemm(hT, 8, 128, "w_in", l, 6656, 64, "TM", 512, evac_dt)
                    for cg in range(4):
                        gemm(hT, 8, 128, "w_in", l, 6720 + cg * 512, 512, "FM", 512,
                             lambda o, bk, cg=cg: evac_store(bk, 512, sc["gat"][(cg * 4 + o) * 128:(cg * 4 + o + 1) * 128, t0:t0 + 512], func=AF.Sigmoid))
                cx.barrier()

        def phaseA2(s, l):
            L = seq_lens[s]
            sc = S[s, l]
            with ExitStack() as pes:
                dg = Buf(pes.enter_context(nc.sbuf_tensor(uq("dg"), [128, 24, 5, 128], BF16)))
                cx.mkring(pes, "xci", 2, [128, 24, 516], BF16)
                cx.mkring(pes, "cv", 6, [128, 512], BF16)
                cx.mkring(pes, "xstm", 2, [128, 4, 2048], BF16)
                cx.mkring(pes, "btm", 2, [128, 4, 512], BF16)
                for cc in range(24):
                    for k in range(5):
                        cx.op("dve", [cstf, cw], [dg], lambda e: e.tensor_scalar(out=dg.t[:, cc, k, :], in0=ident_f, scalar1=cw.t[:, l, k, cc:cc + 1],
                                                                                 scalar2=None, op0=ALU.mult))
                for st_i in range(L // 512):
                    t0 = st_i * 512
                    xi = cx.ring("xci")
                    cx.dma("sp", xi.t[:], sc["xbc"][:, t0:t0 + 516].rearrange("(c p) t -> p c t", p=128), [], [xi], xi)
                    xstm = cx.ring("xstm")
                    btm = cx.ring("btm")
                    def conv(cc):
                        (bk,) = cx.psum(1)
                        cx.pre("pe", [dg, xi], [bk])
                        for k in range(5):
                            ins = nc.tensor.matmul(psap(bk), lhsT=dg.t[:, cc, k, :], rhs=xi.t[:, cc, k:k + 512], start=(k == 0), stop=(k == 4))
                        cx.post("pe", ins, [dg, xi], [bk])
                        cv = cx.ring("cv")
                        cx.op("act", [bk, cb], [cv], lambda e: e.activation(out=cv.t[:], in_=psap(bk), func=AF.Silu, bias=cb.t[:, l, cc:cc + 1], scale=1.0))
                        if cc >= 16:
                            dst = sc["BT"] if cc < 20 else sc["CT"]
                            r0 = (cc - 16) * 128 if cc < 20 else (cc - 20) * 128
                            cx.dma("pool", dst[r0:r0 + 128, t0:t0 + 512], cv.t[:], [cv], [], cv, "st")
                        return cv

                    def trans(cc, cv):
                        if cc >= 20:
                            return
                        (bt,) = cx.psum(1)
                        pv = psbf(bt).rearrange("p (k j) -> p k j", j=128)
                        cx.pre("pe", [cv, cstb], [bt])
                        for ts in range(4):
                            ins = nc.tensor.transpose(out=pv[:, ts, :], in_=cv.t[:, ts * 128:(ts + 1) * 128], identity=ident_b)
                        cx.post("pe", ins, [cv, cstb], [bt])
                        if cc < 16:
                            cx.op("dve", [bt], [xstm], lambda e: e.tensor_copy(out=xstm.t[:, :, cc * 128:(cc + 1) * 128], in_=pv[:, 0:4, :]))
                        else:
                            cx.op("dve", [bt], [btm], lambda e: e.tensor_copy(out=btm.t[:, :, (cc - 16) * 128:(cc - 15) * 128], in_=pv[:, 0:4, :]))
                    cvs = {0: conv(0), 1: conv(1)}
                    for cc in range(24):
                        if cc + 2 < 24:
                            cvs[cc + 2] = conv(cc + 2)
                        trans(cc, cvs.pop(cc))
                    cx.dma("pool", sc["xs"][t0:t0 + 512, :].rearrange("(ts p) c -> p ts c", p=128), xstm.t[:], [xstm], [], xstm, "st")
                    cx.dma("pool", sc["B"][t0:t0 + 512, :].rearrange("(ts p) c -> p ts c", p=128), btm.t[:], [btm], [], btm, "st")
                cx.barrier()

        def phaseA2B(s, l):
            L = seq_lens[s]
            sc = S[s, l]
            nst = L // 512
            with ExitStack() as pes:
                dg = Buf(pes.enter_context(nc.sbuf_tensor(uq("dg"), [128, 24, 5, 128], BF16)))
                cx.mkring(pes, "xci", 2, [128, 24, 516], BF16)
                cx.mkring(pes, "cv", 6, [128, 512], BF16)
                cx.mkring(pes, "xstm", 2, [128, 4, 2048], BF16)
                cx.mkring(pes, "btm", 2, [128, 4, 512], BF16)
                cx.mkring(pes, "CTt", 2, [128, 4, 512], BF16)
                cx.mkring(pes, "dt", 3, [128, 64], F32)
                cx.mkring(pes, "xdd", 2, [128, 2048], BF16)
                cx.mkring(pes, "ybt", 2, [128, 2048], BF16)
                S32 = [Buf(pes.enter_context(nc.sbuf_tensor(uq("S32b"), [128, 512], F32))) for _ in range(4)]
                Sbf = [Buf(pes.enter_context(nc.sbuf_tensor(uq("Sbfb"), [128, 512], BF16))) for _ in range(4)]
                sms = [{k: Buf(pes.enter_context(nc.sbuf_tensor(uq("smb_" + k), [128, n], F32))) for k, n in
                        (("adt", 64), ("cs", 64), ("ee", 64), ("dd", 32), ("w", 32))} for _ in range(2)]
                for g in range(4):
                    cx.op("dve", [], [S32[g]], lambda e: e.memset(S32[g].t[:], 0.0))
                    cx.op("dve", [], [Sbf[g]], lambda e: e.memset(Sbf[g].t[:], 0.0))
                for cc in range(24):
                    for k in range(5):
                        cx.op("dve", [cstf, cw], [dg], lambda e: e.tensor_scalar(out=dg.t[:, cc, k, :], in0=ident_f, scalar1=cw.t[:, l, k, cc:cc + 1],
                                                                                 scalar2=None, op0=ALU.mult))

                def scan_chunk(c, xstm, btm, CTt):
                    t0 = c * 128
                    ci = c % 4
                    tc = slice(ci * 128, (ci + 1) * 128)
                    sm = sms[c % 2]
                    dtt = cx.ring("dt")
                    cx.dma("sp", dtt.t[:], sc["dt"][t0:t0 + 128, :], [], [dtt], dtt)
                    cx.op("dve", [dtt, aneg], [sm["adt"]], lambda e: e.tensor_tensor(out=sm["adt"].t[:], in0=dtt.t[:], in1=aneg.t[:, l * 64:(l + 1) * 64], op=ALU.mult))
                    ssd_decay(dtt, l, 1, sm)
                    xdd = cx.ring("xdd")
                    cx.op("dve", [xstm, sm["w"]], [xdd], lambda e: e.tensor_tensor(out=v3(xdd.t[:]), in0=v3(xstm.t[:, ci, :]), in1=bc(sm["w"].t[:, :], 64), op=ALU.mult))
                    ybt = cx.ring("ybt")
                    for g in range(4):
                        (bk,) = cx.psum(1)
                        cx.op("pe", [CTt, Sbf[g]], [bk], lambda e: e.matmul(psap(bk), lhsT=CTt.t[:, g, tc], rhs=Sbf[g].t[:], start=True, stop=True))
                        cx.op("dve", [bk, sm["ee"]], [ybt], lambda e: e.tensor_tensor(out=v3(ybt.t[:, g * 512:(g + 1) * 512]), in0=v3(psap(bk)),
                                                                                       in1=bc(sm["ee"].t[:, 8 * g:8 * g + 8], 64), op=ALU.mult))
                        (bs,) = cx.psum(1)
                        cx.op("pe", [btm, xdd], [bs], lambda e: e.matmul(psap(bs), lhsT=btm.t[:, ci, g * 128:(g + 1) * 128], rhs=xdd.t[:, g * 512:(g + 1) * 512], start=True, stop=True))
                        cx.op("dve", [S32[g], sm["ee"]], [S32[g]], lambda e: e.tensor_tensor(out=v3(S32[g].t[:]), in0=v3(S32[g].t[:]),
                                                                                          in1=bc(sm["ee"].t[:, 32 + 8 * g:32 + 8 * g + 8], 64), op=ALU.mult))
                        cx.op("dve", [S32[g], bs], [S32[g]], lambda e: e.tensor_tensor(out=S32[g].t[:], in0=S32[g].t[:], in1=psap(bs), op=ALU.add))
                        cx.op("act", [S32[g]], [Sbf[g]], lambda e: e.activation(out=Sbf[g].t[:], in_=S32[g].t[:], func=AF.Copy))
                    cx.dma("pool", sc["yb"][t0:t0 + 128, :], ybt.t[:], [ybt], [], ybt, "st")

                pending = []
                for st_i in range(nst - 1, -1, -1):
                    t0 = st_i * 512
                    xi = cx.ring("xci")
                    cx.dma("sp", xi.t[:], sc["xbc"][:, t0:t0 + 516].rearrange("(c p) t -> p c t", p=128), [], [xi], xi)
                    xstm = cx.ring("xstm")
                    btm = cx.ring("btm")
                    CTt = cx.ring("CTt")

                    def conv(cc):
                        (bk,) = cx.psum(1)
                        cx.pre("pe", [dg, xi], [bk])
                        for k in range(5):
                            ins = nc.tensor.matmul(psap(bk), lhsT=dg.t[:, cc, k, :], rhs=xi.t[:, cc, k:k + 512], start=(k == 0), stop=(k == 4))
                        cx.post("pe", ins, [dg, xi], [bk])
                        if cc >= 20:
                            cx.op("act", [bk, cb], [CTt], lambda e: e.activation(out=CTt.t[:, cc - 20, :], in_=psap(bk), func=AF.Silu, bias=cb.t[:, l, cc:cc + 1], scale=1.0))
                            return None
                        cv = cx.ring("cv")
                        cx.op("act", [bk, cb], [cv], lambda e: e.activation(out=cv.t[:], in_=psap(bk), func=AF.Silu, bias=cb.t[:, l, cc:cc + 1], scale=1.0))
                        if cc >= 16:
                            r0 = (cc - 16) * 128
                            cx.dma("pool", sc["BT"][r0:r0 + 128, t0:t0 + 512], cv.t[:], [cv], [], cv, "st")
                        return cv

                    def trans(cc, cv):
                        if cc >= 20:
                            return
                        (bt,) = cx.psum(1)
                        pv = psbf(bt).rearrange("p (k j) -> p k j", j=128)
                        cx.pre("pe", [cv, cstb], [bt])
                        for ts in range(4):
                            ins = nc.tensor.transpose(out=pv[:, ts, :], in_=cv.t[:, ts * 128:(ts + 1) * 128], identity=ident_b)
                        cx.post("pe", ins, [cv, cstb], [bt])
                        if cc < 16:
                            cx.op("dve", [bt], [xstm], lambda e: e.tensor_copy(out=xstm.t[:, :, cc * 128:(cc + 1) * 128], in_=pv[:, 0:4, :]))
                        else:
                            cx.op("dve", [bt], [btm], lambda e: e.tensor_copy(out=btm.t[:, :, (cc - 16) * 128:(cc - 15) * 128], in_=pv[:, 0:4, :]))
                    cvs = {0: conv(0), 1: conv(1)}
                    for cc in range(24):
                        if cc + 2 < 24:
                            cvs[cc + 2] = conv(cc + 2)
                        trans(cc, cvs.pop(cc))
                        if cc in (3, 9, 15, 21) and pending:
                            pending.pop(0)()
                    while pending:
                        pending.pop(0)()
                    cx.dma("pool", sc["xs"][t0:t0 + 512, :].rearrange("(ts p) c -> p ts c", p=128), xstm.t[:], [xstm], [], xstm, "st")
                    cx.dma("pool", sc["B"][t0:t0 + 512, :].rearrange("(ts p) c -> p ts c", p=128), btm.t[:], [btm], [], btm, "st")
                    cx.dma("pool", sc["CT"][:, t0:t0 + 512].rearrange("(g n) t -> n g t", n=128), CTt.t[:], [CTt], [], CTt, "st")
                    for ci in (3, 2, 1, 0):
                        pending.append(lambda c=st_i * 4 + ci, a=xstm, b_=btm, d=CTt: scan_chunk(c, a, b_, d))
                while pending:
                    pending.pop(0)()
                cx.barrier()

        def ssd_decay(dtt, l, direction, sm):
            d0 = direction * 32
            tri = triu_f if direction == 0 else tril_f
            (bk,) = cx.psum(1)
            cx.pre("pe", [sm["adt"], cstf], [bk])
            nc.tensor.matmul(psap(bk)[:, 0:32], lhsT=tri, rhs=sm["adt"].t[:, d0:d0 + 32], start=True, stop=True)
            ins = nc.tensor.matmul(psap(bk)[:, 32:64], lhsT=ones_f, rhs=sm["adt"].t[:, d0:d0 + 32], start=True, stop=True)
            cx.post("pe", ins, [sm["adt"], cstf], [bk])
            cs, ee, dd, w = sm["cs"], sm["ee"], sm["dd"], sm["w"]
            cx.op("act", [bk], [cs], lambda e: e.activation(out=cs.t[:], in_=psap(bk)[:, 0:64], func=AF.Copy))
            cx.op("act", [cs], [ee], lambda e: e.activation(out=ee.t[:], in_=cs.t[:], func=AF.Exp))
            cx.op("dve", [cs], [dd], lambda e: e.tensor_tensor(out=dd.t[:], in0=cs.t[:, 32:64], in1=cs.t[:, 0:32], op=ALU.subtract))
            cx.op("act", [dd], [dd], lambda e: e.activation(out=dd.t[:], in_=dd.t[:], func=AF.Exp))
            cx.op("dve", [dd, dtt], [w], lambda e: e.tensor_tensor(out=w.t[:], in0=dd.t[:], in1=dtt.t[:, d0:d0 + 32], op=ALU.mult))

        def v3(ap, n=64):
            return ap.rearrange("p (h d) -> p h d", d=n)

        def phaseB(s, l):
            L = seq_lens[s]
            sc = S[s, l]
            C = L // 128
            with ExitStack() as pes:
                cx.mkring(pes, "xs", 3, [128, 2048], BF16)
                cx.mkring(pes, "btm", 3, [128, 512], BF16)
                cx.mkring(pes, "dt", 3, [128, 64], F32)
                cx.mkring(pes, "CT", 2, [128, 4, 512], BF16)
                cx.mkring(pes, "xdd", 2, [128, 2048], BF16)
                cx.mkring(pes, "ybt", 2, [128, 2048], BF16)
                S32 = [Buf(pes.enter_context(nc.sbuf_tensor(uq("S32b"), [128, 512], F32))) for _ in range(4)]
                Sbf = [Buf(pes.enter_context(nc.sbuf_tensor(uq("Sbfb"), [128, 512], BF16))) for _ in range(4)]
                sms = [{k: Buf(pes.enter_context(nc.sbuf_tensor(uq("smb_" + k), [128, n], F32))) for k, n in
                        (("adt", 64), ("cs", 64), ("ee", 64), ("dd", 32), ("w", 32))} for _ in range(2)]
                for g in range(4):
                    cx.op("dve", [], [S32[g]], lambda e: e.memset(S32[g].t[:], 0.0))
                    cx.op("dve", [], [Sbf[g]], lambda e: e.memset(Sbf[g].t[:], 0.0))
                state = {"CT": None}

                def prep(c):
                    t0 = c * 128
                    sm = sms[c % 2]
                    if state["CT"] is None or c % 4 == 3:
                        CTn = cx.ring("CT")
                        s0 = (c // 4) * 512
                        cx.dma("sp", CTn.t[:], sc["CT"][:, s0:s0 + 512].rearrange("(g n) t -> n g t", n=128), [], [CTn], CTn)
                        state["CT"] = CTn
                    xs = cx.ring("xs")
                    cx.dma("sp", xs.t[:], sc["xs"][t0:t0 + 128, :], [], [xs], xs)
                    btm = cx.ring("btm")
                    cx.dma("sp", btm.t[:], sc["B"][t0:t0 + 128, :], [], [btm], btm)
                    dtt = cx.ring("dt")
                    cx.dma("sp", dtt.t[:], sc["dt"][t0:t0 + 128, :], [], [dtt], dtt)
                    cx.op("dve", [dtt, aneg], [sm["adt"]], lambda e: e.tensor_tensor(out=sm["adt"].t[:], in0=dtt.t[:], in1=aneg.t[:, l * 64:(l + 1) * 64], op=ALU.mult))
                    ssd_decay(dtt, l, 1, sm)
                    xdd = cx.ring("xdd")
                    cx.op("dve", [xs, sm["w"]], [xdd], lambda e: e.tensor_tensor(out=v3(xdd.t[:]), in0=v3(xs.t[:]), in1=bc(sm["w"].t[:, :], 64), op=ALU.mult))
                    return dict(CT=state["CT"], btm=btm, sm=sm, xdd=xdd)
                nxt = prep(C - 1)
                for c in range(C - 1, -1, -1):
                    t0 = c * 128
                    P = nxt
                    CT, btm, sm, xdd = P["CT"], P["btm"], P["sm"], P["xdd"]
                    tc = slice((c % 4) * 128, (c % 4 + 1) * 128)
                    ybt = cx.ring("ybt")
                    for g in range(4):
                        (bk,) = cx.psum(1)
                        cx.op("pe", [CT, Sbf[g]], [bk], lambda e: e.matmul(psap(bk), lhsT=CT.t[:, g, tc], rhs=Sbf[g].t[:], start=True, stop=True))
                        cx.op("dve", [bk, sm["ee"]], [ybt], lambda e: e.tensor_tensor(out=v3(ybt.t[:, g * 512:(g + 1) * 512]), in0=v3(psap(bk)),
                                                                                       in1=bc(sm["ee"].t[:, 8 * g:8 * g + 8], 64), op=ALU.mult))
                        (bs,) = cx.psum(1)
                        cx.op("pe", [btm, xdd], [bs], lambda e: e.matmul(psap(bs), lhsT=btm.t[:, g * 128:(g + 1) * 128], rhs=xdd.t[:, g * 512:(g + 1) * 512], start=True, stop=True))
                        cx.op("dve", [S32[g], sm["ee"]], [S32[g]], lambda e: e.tensor_tensor(out=v3(S32[g].t[:]), in0=v3(S32[g].t[:]),
                                                                                          in1=bc(sm["ee"].t[:, 32 + 8 * g:32 + 8 * g + 8], 64), op=ALU.mult))
                        cx.op("dve", [S32[g], bs], [S32[g]], lambda e: e.tensor_tensor(out=S32[g].t[:], in0=S32[g].t[:], in1=psap(bs), op=ALU.add))
                        cx.op("act", [S32[g]], [Sbf[g]], lambda e: e.activation(out=Sbf[g].t[:], in_=S32[g].t[:], func=AF.Copy))
                        if g == 1 and c > 0:
                            nxt = prep(c - 1)
                    cx.dma("pool", sc["yb"][t0:t0 + 128, :], ybt.t[:], [ybt], [], ybt, "st")
                cx.barrier()

        def phaseC(s, l):
            L = seq_lens[s]
            sc = S[s, l]
            C = L // 128
            scale = 64 ** -0.5
            with ExitStack() as pes:
                cx.mkring(pes, "xs", 2, [128, 2048], BF16)
                cx.mkring(pes, "btm", 2, [128, 512], BF16)
                cx.mkring(pes, "dt", 2, [128, 64], F32)
                cx.mkring(pes, "CT", 1, [128, 4, 512], BF16)
                cx.mkring(pes, "BT", 1, [128, 4, 512], BF16)
                cx.mkring(pes, "ybl", 2, [128, 2048], BF16)
                cx.mkring(pes, "szl", 2, [128, 2048], BF16)
                cx.mkring(pes, "xdd", 2, [128, 2048], BF16)
                cx.mkring(pes, "xdf", 2, [128, 2048], BF16)
                cx.mkring(pes, "xdb", 2, [128, 2048], BF16)
                cx.mkring(pes, "rhsf", 2, [128, 8, 128], BF16)
                cx.mkring(pes, "rhsb", 2, [128, 8, 128], BF16)
                cx.mkring(pes, "E", 2, [128, 8, 128], BF16)
                cx.mkring(pes, "Mf", 2, [128, 8, 128], BF16)
                cx.mkring(pes, "Mb", 2, [128, 8, 128], BF16)
                cx.mkring(pes, "yn", 1, [128, 2048], BF16)
                cx.mkring(pes, "ssdT", 2, [128, 16, 128], BF16)
                cx.mkring(pes, "qT", 1, [128, 8, 512], BF16)
                kTz = [Buf(pes.enter_context(nc.sbuf_tensor(uq("kTz"), [128, 4, 768], BF16))) for _ in range(2)]
                for kz in kTz:
                    cx.op("dve", [], [kz], lambda e: e.memset(kz.t[:], 0.0))
                cx.mkring(pes, "V", 1, [128, 6, 256], BF16)
                cx.mkring(pes, "Pa", 2, [128, 3, 512], BF16)
                cx.mkring(pes, "den", 2, [64, 512], F32)
                cx.mkring(pes, "OT", 2, [64, 16, 128], BF16)
                Tatt = Buf(pes.enter_context(nc.sbuf_tensor(uq("Tatt"), [128, 16, 3, 128], BF16)))
                load_tatt(Tatt)
                Dg = Buf(pes.enter_context(nc.sbuf_tensor(uq("Dg"), [128, 32, 128], BF16)))
                for h in range(32):
                    cx.op("dve", [cstf, dsk], [Dg], lambda e: e.tensor_scalar(out=Dg.t[:, h, :], in0=ident_f, scalar1=dsk.t[:, l * 32 + h:l * 32 + h + 1],
                                                                               scalar2=None, op0=ALU.mult))
                Gmf = Buf(pes.enter_context(nc.sbuf_tensor(uq("Gmf"), [128, 4, 128], BF16)))
                Gmb = Buf(pes.enter_context(nc.sbuf_tensor(uq("Gmb"), [128, 4, 128], BF16)))
                S32 = [Buf(pes.enter_context(nc.sbuf_tensor(uq("S32f"), [128, 512], F32))) for _ in range(4)]
                Sbf = [Buf(pes.enter_context(nc.sbuf_tensor(uq("Sbff"), [128, 512], BF16))) for _ in range(4)]
                y32 = [Buf(pes.enter_context(nc.sbuf_tensor(uq("y32g"), [128, 512], F32))) for _ in range(4)]
                ssn4 = Buf(pes.enter_context(nc.sbuf_tensor(uq("ssn4"), [128, 4], F32)))
                ssn = Buf(pes.enter_context(nc.sbuf_tensor(uq("ssn"), [128, 2], F32)))
                smfs = [{k: Buf(pes.enter_context(nc.sbuf_tensor(uq("smf_" + k), [128, n], F32))) for k, n in
                         (("adt", 64), ("cs", 64), ("ee", 64), ("dd", 32), ("w", 32))} for _ in range(2)]
                for g in range(4):
                    cx.op("dve", [], [S32[g]], lambda e: e.memset(S32[g].t[:], 0.0))
                    cx.op("dve", [], [Sbf[g]], lambda e: e.memset(Sbf[g].t[:], 0.0))
                def prepC(cn):
                    tn = cn * 128
                    smf_ = smfs[cn % 2]
                    xs_ = cx.ring("xs")
                    cx.dma("sp", xs_.t[:], sc["xs"][tn:tn + 128, :], [], [xs_], xs_)
                    btm_ = cx.ring("btm")
                    cx.dma("sp", btm_.t[:], sc["B"][tn:tn + 128, :], [], [btm_], btm_)
                    dtt_ = cx.ring("dt")
                    cx.dma("sp", dtt_.t[:], sc["dt"][tn:tn + 128, :], [], [dtt_], dtt_)
                    ybl_ = cx.ring("ybl")
                    cx.dma("sp", ybl_.t[:], sc["yb"][tn:tn + 128, :], [], [ybl_], ybl_)
                    szl_ = cx.ring("szl")
                    cx.dma("sp", szl_.t[:], sc["sz"][tn:tn + 128, :], [], [szl_], szl_)
                    cx.op("dve", [dtt_, aneg], [smf_["adt"]], lambda e: e.tensor_tensor(out=smf_["adt"].t[:], in0=dtt_.t[:], in1=aneg.t[:, l * 64:(l + 1) * 64], op=ALU.mult))
                    ssd_decay(dtt_, l, 0, smf_)
                    xdd_ = cx.ring("xdd")
                    xdf_ = cx.ring("xdf")
                    xdb_ = cx.ring("xdb")
                    cx.op("dve", [xs_, smf_["w"]], [xdd_], lambda e: e.tensor_tensor(out=v3(xdd_.t[:]), in0=v3(xs_.t[:]), in1=bc(smf_["w"].t[:, :], 64), op=ALU.mult))
                    cx.op("dve", [xs_, dtt_], [xdf_], lambda e: e.tensor_tensor(out=v3(xdf_.t[:]), in0=v3(xs_.t[:]), in1=bc(dtt_.t[:, 0:32], 64), op=ALU.mult))
                    cx.op("dve", [xs_, dtt_], [xdb_], lambda e: e.tensor_tensor(out=v3(xdb_.t[:]), in0=v3(xs_.t[:]), in1=bc(dtt_.t[:, 32:64], 64), op=ALU.mult))
                    return dict(xs=xs_, btm=btm_, dtt=dtt_, ybl=ybl_, szl=szl_, smf=smf_, xdd=xdd_, xdf=xdf_, xdb=xdb_)
                nxtP = None
                for c in range(C):
                    t0 = c * 128
                    I = c // 4
                    ci = c % 4
                    tc = slice(ci * 128, (ci + 1) * 128)
                    if ci == 0:
                        s0 = I * 512
                        CT = cx.ring("CT")
                        cx.dma("sp", CT.t[:], sc["CT"][:, s0:s0 + 512].rearrange("(g n) t -> n g t", n=128), [], [CT], CT)
                        BT = cx.ring("BT")
                        cx.dma("sp", BT.t[:], sc["BT"][:, s0:s0 + 512].rearrange("(g n) t -> n g t", n=128), [], [BT], BT)
                        qT = cx.ring("qT")
                        cx.dma("sp", qT.t[:], sc["qT"][:, s0:s0 + 512].rearrange("(c p) t -> p c t", p=128), [], [qT], qT)
                        blo = max(0, 4 * I - 1)
                        bhi = min(C, 4 * I + 5)
                        sl0 = blo - (4 * I - 1)
                        nb = bhi - blo
                        for g in range(4):
                            for half in range(2):
                                r0 = (g // 2) * 128 + (g % 2) * 64
                                cx.dma("sp", kTz[half].t[half * 64:(half + 1) * 64, g, sl0 * 128:(sl0 + nb) * 128],
                                       sc["kT"][r0:r0 + 64, blo * 128:bhi * 128], [], [kTz[half]], kTz[half])
                        V = cx.ring("V")
                        cx.dma("sp", V.t[:, sl0:sl0 + nb, :], sc["v"][blo * 128:bhi * 128, :].rearrange("(b p) c -> p b c", p=128), [], [V], V)
                    ssdT = cx.ring("ssdT")
                    OT = cx.ring("OT")
                    tq = slice(0, 128)
                    if c == 0:
                        nxtP = prepC(0)
                    P = nxtP
                    xs, btm, dtt, ybl, szl, smf, xdd, xdf, xdb = (P[k] for k in ("xs", "btm", "dtt", "ybl", "szl", "smf", "xdd", "xdf", "xdb"))

                    jv = [j for j in range(3) if 0 <= c + j - 1 < C]
                    j0, j1 = jv[0], jv[-1] + 1
                    att = {}

                    def A1(g):
                        bks = [None, None, None]
                        for j in jv:
                            (bks[j],) = cx.psum(1)
                        bsel = bks[j0:j1]
                        cx.pre("pe", [kTz[0], kTz[1], qT, Tatt, cstb], bsel)
                        for j in jv:
                            slot = ci + j
                            nc.tensor.matmul(psap(bks[j]), lhsT=ident_b, rhs=Tatt.t[:, 4 * g:4 * g + 4, j, :], start=True, stop=False)
                            for hl in range(4):
                                h = 4 * g + hl
                                e2 = h % 2
                                ins = nc.tensor.matmul(psap(bks[j])[:, hl * 128:(hl + 1) * 128],
                                                       lhsT=kTz[e2].t[:, g, slot * 128:(slot + 1) * 128],
                                                       rhs=qT.t[:, h // 2, tc], start=False, stop=(hl == 3))
                        cx.post("pe", ins, [kTz[0], kTz[1], qT, Tatt, cstb], bsel)
                        att[g] = (bks, bsel)

                    def A2(g):
                        bks, bsel = att[g]
                        Pa = cx.ring("Pa")
                        for j in jv:
                            cx.op("act", [bks[j]], [Pa], lambda e: e.activation(out=Pa.t[:, j, :], in_=psap(bks[j]), func=AF.Exp, scale=scale))
                        att[g] = Pa

                    def A3(g):
                        Pa = att[g]
                        (bo,) = cx.psum(1)
                        (bd,) = cx.psum(1)
                        cx.pre("pe", [V, Pa, cstb], [bo, bd])
                        for j in jv:
                            slot = ci + j
                            nc.tensor.matmul(psap(bo)[0:64, :], lhsT=V.t[:, slot, g * 64:(g + 1) * 64], rhs=Pa.t[:, j, :], start=(j == j0), stop=(j == j1 - 1))
                        for j in jv:
                            ins = nc.tensor.matmul(psap(bd)[0:64, :], lhsT=ones_b[:, 0:64], rhs=Pa.t[:, j, :], start=(j == j0), stop=(j == j1 - 1))
                        cx.post("pe", ins, [V, Pa, cstb], [bo, bd])
                        att[g] = (bo, bd)

                    def A4a(g):
                        bo, bd = att[g]
                        den = cx.ring("den")
                        cx.op("dve", [bd, esink], [den], lambda e: e.tensor_tensor(out=den.t[:, :].rearrange("p (h q) -> p h q", q=128),
                                                                                    in0=psap(bd)[0:64, :].rearrange("p (h q) -> p h q", q=128),
                                                                                    in1=bc(esink.t[0:64, l * 16 + 4 * g:l * 16 + 4 * g + 4], 128), op=ALU.add))
                        cx.op("act", [den], [den], lambda e: e.activation(out=den.t[:], in_=den.t[:], func=AF.Ln))
                        cx.op("act", [den], [den], lambda e: e.activation(out=den.t[:], in_=den.t[:], func=AF.Exp, scale=-1.0))
                        att[g] = (bo, den)

                    def A4b(g):
                        bo, den = att[g]
                        cx.op("dve", [bo, den], [OT], lambda e: e.tensor_tensor(out=OT.t[:, 4 * g:4 * g + 4, tq],
                                                                                 in0=psap(bo)[0:64, :].rearrange("p (h q) -> p h q", q=128),
                                                                                 in1=den.t[:, :].rearrange("p (h q) -> p h q", q=128), op=ALU.mult))

                    adt = smf["adt"]
                    A1(0)
                    (bG,) = cx.psum(1)
                    cx.pre("pe", [BT, CT], [bG])
                    for g in range(4):
                        ins = nc.tensor.matmul(psap(bG)[:, g * 128:(g + 1) * 128], lhsT=BT.t[:, g, tc], rhs=CT.t[:, g, tc], start=True, stop=True)
                    cx.post("pe", ins, [BT, CT], [bG])
                    pG = psap(bG).rearrange("p (g q) -> p g q", q=128)
                    cx.op("dve", [bG, cstf], [Gmf], lambda e: e.tensor_tensor(out=Gmf.t[:], in0=pG, in1=triu_f.unsqueeze(1).broadcast_to([128, 4, 128]), op=ALU.mult))
                    bG.pending = True
                    cx.op("dve", [bG, cstf], [Gmb], lambda e: e.tensor_tensor(out=Gmb.t[:], in0=pG, in1=tril_f.unsqueeze(1).broadcast_to([128, 4, 128]), op=ALU.mult))
                    ssd = {}

                    def R(g):
                        rhsf = cx.ring("rhsf")
                        rhsb = cx.ring("rhsb")
                        cx.op("dve", [adt, cstb], [rhsf], lambda e: e.tensor_tensor(out=rhsf.t[:], in0=cstb.t[:, 128:256].unsqueeze(1).broadcast_to([128, 8, 128]),
                                                                                     in1=bc(adt.t[:, 8 * g:8 * g + 8], 128), op=ALU.mult))
                        cx.op("dve", [adt, cstb], [rhsb], lambda e: e.tensor_tensor(out=rhsb.t[:], in0=cstb.t[:, 256:384].unsqueeze(1).broadcast_to([128, 8, 128]),
                                                                                     in1=bc(adt.t[:, 32 + 8 * g:32 + 8 * g + 8], 128), op=ALU.mult))
                        ssd["rhs", g] = (rhsf, rhsb)

                    def O(g):
                        gs = slice(g * 512, (g + 1) * 512)
                        (bo,) = cx.psum(1)
                        cx.op("pe", [CT, Sbf[g]], [bo], lambda e: e.matmul(psap(bo), lhsT=CT.t[:, g, tc], rhs=Sbf[g].t[:], start=True, stop=True))
                        cx.op("dve", [bo, smf["ee"]], [y32[g]], lambda e: e.tensor_tensor(out=v3(y32[g].t[:]), in0=v3(psap(bo)), in1=bc(smf["ee"].t[:, 8 * g:8 * g + 8], 64), op=ALU.mult))

                    def St(g):
                        gs = slice(g * 512, (g + 1) * 512)
                        (bs,) = cx.psum(1)
                        cx.op("pe", [btm, xdd], [bs], lambda e: e.matmul(psap(bs), lhsT=btm.t[:, g * 128:(g + 1) * 128], rhs=xdd.t[:, gs], start=True, stop=True))
                        cx.op("dve", [S32[g], smf["ee"]], [S32[g]], lambda e: e.tensor_tensor(out=v3(S32[g].t[:]), in0=v3(S32[g].t[:]),
                                                                                           in1=bc(smf["ee"].t[:, 32 + 8 * g:32 + 8 * g + 8], 64), op=ALU.mult))
                        cx.op("dve", [S32[g], bs], [S32[g]], lambda e: e.tensor_tensor(out=S32[g].t[:], in0=S32[g].t[:], in1=psap(bs), op=ALU.add))
                        cx.op("act", [S32[g]], [Sbf[g]], lambda e: e.activation(out=Sbf[g].t[:], in_=S32[g].t[:], func=AF.Copy))

                    def Y0(g):
                        gs = slice(g * 512, (g + 1) * 512)
                        (bY,) = cx.psum(1)
                        cx.op("pe", [ybl, cstb], [bY], lambda e: e.matmul(psap(bY), lhsT=ident_b, rhs=ybl.t[:, gs], start=True, stop=False))
                        ssd["bY", g] = bY

                    def Dm(g):
                        rhsf, rhsb = ssd["rhs", g]
                        bDs = cx.psum(1) + cx.psum(1)
                        cx.pre("pe", [rhsf, rhsb, cstb], bDs)
                        for hf in range(2):
                            nc.tensor.matmul(psap(bDs[hf]), lhsT=sl_b, rhs=rhsf.t[:, 4 * hf:4 * hf + 4, :], start=True, stop=False)
                            ins = nc.tensor.matmul(psap(bDs[hf]), lhsT=su_b, rhs=rhsb.t[:, 4 * hf:4 * hf + 4, :], start=False, stop=True)
                        cx.post("pe", ins, [rhsf, rhsb, cstb], bDs)
                        ssd["bD", g] = bDs

                    def EM(g):
                        bDs = ssd["bD", g]
                        E = cx.ring("E")
                        for hf in range(2):
                            cx.op("act", [bDs[hf]], [E], lambda e: e.activation(out=E.t[:, 4 * hf:4 * hf + 4, :], in_=psap(bDs[hf]).rearrange("p (h q) -> p h q", q=128), func=AF.Exp))
                        Mf = cx.ring("Mf")
                        Mb = cx.ring("Mb")
                        cx.op("dve", [E, Gmf], [Mf], lambda e: e.tensor_tensor(out=Mf.t[:], in0=E.t[:], in1=Gmf.t[:, g, :].unsqueeze(1).broadcast_to([128, 8, 128]), op=ALU.mult))
                        cx.op("dve", [E, Gmb], [Mb], lambda e: e.tensor_tensor(out=Mb.t[:], in0=E.t[:], in1=Gmb.t[:, g, :].unsqueeze(1).broadcast_to([128, 8, 128]), op=ALU.mult))
                        ssd["M", g] = (Mf, Mb)

                    def Y(g):
                        Mf, Mb = ssd["M", g]
                        bY = ssd["bY", g]
                        cx.pre("pe", [Mf, Mb, xdf, xdb, Dg, xs], [bY])
                        for hl in range(8):
                            h = 8 * g + hl
                            hc = hl * 64
                            nc.tensor.matmul(psap(bY)[:, hc:hc + 64], lhsT=Mf.t[:, hl, :], rhs=xdf.t[:, h * 64:(h + 1) * 64], start=False, stop=False)
                            nc.tensor.matmul(psap(bY)[:, hc:hc + 64], lhsT=Mb.t[:, hl, :], rhs=xdb.t[:, h * 64:(h + 1) * 64], start=False, stop=False)
                            ins = nc.tensor.matmul(psap(bY)[:, hc:hc + 64], lhsT=Dg.t[:, h, :], rhs=xs.t[:, h * 64:(h + 1) * 64], start=False, stop=(hl == 7))
                        cx.post("pe", ins, [Mf, Mb, xdf, xdb, Dg, xs], [bY])

                    def Cmb(g):
                        gs = slice(g * 512, (g + 1) * 512)
                        bY = ssd["bY", g]
                        cx.op("dve", [y32[g], bY], [y32[g]], lambda e: e.tensor_tensor(out=y32[g].t[:], in0=y32[g].t[:], in1=psap(bY), op=ALU.add))
                        cx.op("dve", [y32[g], szl], [y32[g]], lambda e: e.tensor_tensor(out=y32[g].t[:], in0=y32[g].t[:], in1=szl.t[:, gs], op=ALU.mult))
                        cx.op("act", [y32[g]], [yn, ssn4], lambda e: e.activation(out=yn.t[:, gs], in_=y32[g].t[:], func=AF.Square, accum_out=ssn4.t[:, g:g + 1]))

                    yn = cx.ring("yn")
                    R(0)
                    Dm(0)
                    for g in range(4):
                        if g < 3:
                            A1(g + 1)
                        A2(g)
                        O(g)
                        Y0(g)
                        EM(g)
                        A3(g)
                        St(g)
                        if g < 3:
                            R(g + 1)
                            Dm(g + 1)
                        A4a(g)
                        Y(g)
                        A4b(g)
                        Cmb(g)
                    if c + 1 < C:
                        nxtP = prepC(c + 1)
                    cx.op("dve", [ssn4], [ssn], lambda e: e.tensor_reduce(out=ssn.t[:, 0:1], in_=ssn4.t[:, 0:4], op=ALU.add, axis=mybir.AxisListType.X))
                    cx.op("act", [ssn, epsT], [ssn], lambda e: e.activation(out=ssn.t[:, 0:1], in_=ssn.t[:, 0:1], func=AF.Ln, scale=1.0 / 2048, bias=epsT.t[:, 0:1]))
                    cx.op("act", [ssn], [ssn], lambda e: e.activation(out=ssn.t[:, 1:2], in_=ssn.t[:, 0:1], func=AF.Exp, scale=-0.5))
                    for g in range(4):
                        cx.op("act", [y32[g], ssn], [yn], lambda e: e.activation(out=yn.t[:, g * 512:(g + 1) * 512], in_=y32[g].t[:], func=AF.Copy, scale=ssn.t[:, 1:2]))
                    for half in range(2):
                        (bt,) = cx.psum(1)
                        pv = psbf(bt).rearrange("p (k j) -> p k j", j=128)
                        cx.pre("pe", [yn, cstb], [bt])
                        for kc in range(8):
                            k2 = half * 8 + kc
                            ins = nc.tensor.transpose(out=pv[:, kc, :], in_=yn.t[:, k2 * 128:(k2 + 1) * 128], identity=ident_b)
                        cx.post("pe", ins, [yn, cstb], [bt])
                        cx.op("dve", [bt, gssd], [ssdT], lambda e: e.tensor_tensor(out=ssdT.t[:, half * 8:half * 8 + 8, :], in0=pv[:, 0:8, :],
                                                                                   in1=bc(gssd.t[:, l, half * 8:half * 8 + 8], 128), op=ALU.mult))
                    cx.dma("pool", sc["ssdT"][:, t0:t0 + 128].rearrange("(k p) t -> p k t", p=128), ssdT.t[:], [ssdT], [], ssdT, "st")
                    cx.dma("pool", sc["OT"][:, t0:t0 + 128].rearrange("(h d) t -> d h t", d=64), OT.t[:], [OT], [], OT, "st")
                cx.barrier()

        def phaseD(s, l):
            L = seq_lens[s]
            sc = S[s, l]
            last = (l == DEPTH - 1)
            with ExitStack() as pes:
                cx.mkring(pes, "wslot", 7, [128, 8, 512], BF16)
                cx.mkring(pes, "X", 1, [128, 4, 1024], F32)
                cx.mkring(pes, "xnr", 2, [128, 1024], BF16)
                cx.mkring(pes, "hT", 1, [128, 8, 512], BF16)
                cx.mkring(pes, "Pc", 3, [128, 2, 512], BF16)
                cx.mkring(pes, "rec", 3, [128, 512], F32)
                cx.mkring(pes, "sg", 4, [128, 512], BF16)
                cx.mkring(pes, "tmp", 1, [128, 512], F32)
                cx.mkring(pes, "outst", 1, [128, 1024], F32)
                cx.mkring(pes, "OTl", 1, [64, 16, 512], BF16)
                bufA = Buf(pes.enter_context(nc.sbuf_tensor(uq("bufA"), [128, 24, 512], BF16)))
                bufB = Buf(pes.enter_context(nc.sbuf_tensor(uq("bufB"), [128, 16, 512], BF16)))
                bufC = Buf(pes.enter_context(nc.sbuf_tensor(uq("bufC"), [128, 16, 512], BF16)))
                KcT = Buf(pes.enter_context(nc.sbuf_tensor(uq("KcT"), [128, 8, 256], BF16)))
                Vc = Buf(pes.enter_context(nc.sbuf_tensor(uq("Vc"), [128, 2, 1024], BF16)))
                ss = Buf(pes.enter_context(nc.sbuf_tensor(uq("ssD"), [128, 4], F32)))
                rstd = Buf(pes.enter_context(nc.sbuf_tensor(uq("rstdD"), [128, 4], F32)))
                junk = Buf(pes.enter_context(nc.sbuf_tensor(uq("junkD"), [128, 1024], BF16)))

                class View:
                    def __init__(self, buf, ap):
                        self.buf = buf
                        self.t = ap
                Xm = cx.ring("X")
                cx.dma("sp", Xm.t[:, 0:2, :], memin[s][:, :].rearrange("(ts p) d -> p ts d", p=128), [], [Xm], Xm)
                normT(Xm, 2, gmem.t[:, l, :], bufB, ss, rstd, "xnr", junk)

                def evac_k(o, bk, cg):
                    e = cx.alt()
                    if e == "act":
                        cx.op("act", [bk], [KcT], lambda e_: e_.activation(out=KcT.t[:, cg * 4 + o, :], in_=psap(bk)[:, 0:256], func=AF.Copy))
                    else:
                        cx.op("dve", [bk], [KcT], lambda e_: e_.tensor_copy(out=KcT.t[:, cg * 4 + o, :], in_=psap(bk)[:, 0:256]))

                def evac_v(o, bk, cg):
                    cx.op("dve", [bk], [Vc], lambda e_: e_.tensor_copy(out=Vc.t[:, o, cg * 512:(cg + 1) * 512], in_=psap(bk)))
                for cg in range(2):
                    gemm(bufB, 8, 128, "w_kv_cross", l, cg * 512, 512, "FM", 256, lambda o, bk, cg=cg: evac_k(o, bk, cg))
                for cg in range(2):
                    gemm(bufB, 8, 128, "w_kv_cross", l, 1024 + cg * 512, 512, "TM", 256, lambda o, bk, cg=cg: evac_v(o, bk, cg))

                for st_i in range(L // 512):
                    t0 = st_i * 512
                    X = cx.ring("X")
                    xsrc = xin[s] if l == 0 else S[s, l - 1]["xn"]
                    cx.dma("sp", X.t[:], xsrc[t0:t0 + 512, :].rearrange("(ts p) d -> p ts d", p=128), [], [X], X)
                    OTl = cx.ring("OTl")
                    cx.dma("sp", OTl.t[:], sc["OT"][:, t0:t0 + 512].rearrange("(h d) t -> d h t", d=64), [], [OTl], OTl)
                    cx.dma("sp", bufB.t[:], sc["ssdT"][:, t0:t0 + 512].rearrange("(k p) t -> p k t", p=128), [], [bufB], bufB)
                    cx.dma("sp", bufC.t[:], sc["gat"][:, t0:t0 + 512].rearrange("(k p) t -> p k t", p=128), [], [bufC], bufC)

                    def evac_pa(o, bk, cg):
                        oc = cg * 4 + o
                        cx.op("dve", [bk, bufC, bufA], [bufA], lambda e: e.tensor_tensor(out=bufA.t[:, 16 + oc, :], in0=psap(bk), in1=bufC.t[:, oc, :], op=ALU.mult))

                    def evac_ps(o, bk, cg):
                        oc = cg * 4 + o
                        tmp = cx.ring("tmp")
                        cx.op("dve", [bk, bufC], [tmp], lambda e: e.tensor_tensor(out=tmp.t[:], in0=psap(bk), in1=bufC.t[:, 8 + oc, :], op=ALU.mult))
                        cx.op("dve", [tmp, bufA], [bufA], lambda e: e.tensor_tensor(out=bufA.t[:, oc, :], in0=tmp.t[:], in1=bufA.t[:, 16 + oc, :], op=ALU.add))
                    for cg in range(2):
                        gemm(OTl, 16, 64, "w_proj_attn", l, cg * 512, 512, "FM", 512, lambda o, bk, cg=cg: evac_pa(o, bk, cg))
                    for cg in range(2):
                        gemm(bufB, 16, 128, "w_proj_ssd", l, cg * 512, 512, "FM", 512, lambda o, bk, cg=cg: evac_ps(o, bk, cg))

                    def evac_res(o, bk, cg):
                        cx.op("dve", [bk, X], [X], lambda e: e.tensor_tensor(out=X.t[:, o, cg * 512:(cg + 1) * 512], in0=psap(bk), in1=X.t[:, o, cg * 512:(cg + 1) * 512], op=ALU.add))
                    for cg in range(2):
                        gemm(bufA, 8, 128, "w_out", l, cg * 512, 512, "TM", 512, lambda o, bk, cg=cg: evac_res(o, bk, cg))
                    hT = cx.ring("hT")
                    normT(X, 4, gcross.t[:, l, :], hT, ss, rstd, "xnr", junk)

                    def evac_q(o, bk, cg):
                        e = cx.alt()
                        if e == "act":
                            cx.op("act", [bk], [bufB], lambda e_: e_.activation(out=bufB.t[:, cg * 4 + o, :], in_=psap(bk), func=AF.Copy))
                        else:
                            cx.op("dve", [bk], [bufB], lambda e_: e_.tensor_copy(out=bufB.t[:, cg * 4 + o, :], in_=psap(bk)))
                    for cg in range(2):
                        gemm(hT, 8, 128, "w_q_cross", l, cg * 512, 512, "FM", 512, lambda o, bk, cg=cg: evac_q(o, bk, cg))
                    xa = {}

                    def XL(hc):
                        Pc = cx.ring("Pc")
                        for mch in range(2):
                            (bk,) = cx.psum(1)
                            cx.pre("pe", [KcT, bufB], [bk])
                            for dc in range(2):
                                ins = nc.tensor.matmul(psap(bk), lhsT=KcT.t[:, hc * 2 + dc, mch * 128:(mch + 1) * 128], rhs=bufB.t[:, hc * 2 + dc, :],
                                                       start=(dc == 0), stop=(dc == 1))
                            cx.post("pe", ins, [KcT, bufB], [bk])
                            cx.op("act", [bk], [Pc], lambda e: e.activation(out=Pc.t[:, mch, :], in_=psap(bk), func=AF.Exp, scale=1.0 / 16.0))
                        xa[hc] = Pc

                    def XD(hc):
                        Pc = xa[hc]
                        (bd,) = cx.psum(1)
                        cx.pre("pe", [Pc, cstb], [bd])
                        for mch in range(2):
                            ins = nc.tensor.matmul(psap(bd), lhsT=ones_b, rhs=Pc.t[:, mch, :], start=(mch == 0), stop=(mch == 1))
                        cx.post("pe", ins, [Pc, cstb], [bd])
                        rec = cx.ring("rec")
                        cx.op("act", [bd], [rec], lambda e: e.activation(out=rec.t[:], in_=psap(bd), func=AF.Ln))
                        cx.op("act", [rec], [rec], lambda e: e.activation(out=rec.t[:], in_=rec.t[:], func=AF.Exp, scale=-1.0))
                        bks = []
                        for dc in range(2):
                            (bk,) = cx.psum(1)
                            cx.pre("pe", [Vc, Pc], [bk])
                            for mch in range(2):
                                ins = nc.tensor.matmul(psap(bk), lhsT=Vc.t[:, mch, (hc * 2 + dc) * 128:(hc * 2 + dc + 1) * 128], rhs=Pc.t[:, mch, :],
                                                       start=(mch == 0), stop=(mch == 1))
                            cx.post("pe", ins, [Vc, Pc], [bk])
                            bks.append(bk)
                        xa[hc] = (rec, bks)

                    def XE(hc):
                        rec, bks = xa[hc]
                        for dc in range(2):
                            bk = bks[dc]
                            cx.op("dve", [bk, rec, bufB], [bufB], lambda e: e.tensor_tensor(out=bufB.t[:, 8 + hc * 2 + dc, :], in0=psap(bk), in1=rec.t[:], op=ALU.mult))
                    XL(0)
                    for hc in range(4):
                        if hc < 3:
                            XL(hc + 1)
                        XD(hc)
                        if hc > 0:
                            XE(hc - 1)
                    XE(3)
                    ocv = View(bufB, bufB.t[:, 8:16, :])
                    for cg in range(2):
                        gemm(ocv, 8, 128, "w_o_cross", l, cg * 512, 512, "TM", 512, lambda o, bk, cg=cg: evac_res(o, bk, cg))
                    hT = cx.ring("hT")
                    normT(X, 4, gffn.t[:, l, :], hT, ss, rstd, "xnr", junk)
                    for cg in range(6):
                        nw = 512 if cg < 5 else 256
                        held = {}

                        def evac_g(o, bk, held=held):
                            sg = cx.ring("sg")
                            cx.op("act", [bk], [sg], lambda e: e.activation(out=sg.t[:], in_=psap(bk), func=AF.Silu))
                            held[o] = sg

                        def evac_u(o, bk, cg=cg, held=held):
                            sg = held[o]
                            cx.op("dve", [bk, sg, bufA], [bufA], lambda e: e.tensor_tensor(out=bufA.t[:, cg * 4 + o, :], in0=psap(bk), in1=sg.t[:], op=ALU.mult))
                        gemm(hT, 8, 128, "w_gate_up", l, cg * 512, nw, "FM", 512, evac_g)
                        gemm(hT, 8, 128, "w_gate_up", l, 2816 + cg * 512, nw, "FM", 512, evac_u)
                    for cg in range(2):
                        gemm(bufA, 22, 128, "w_down", l, cg * 512, 512, "TM", 512, lambda o, bk, cg=cg: evac_res(o, bk, cg))
                    if last:
                        for ts in range(4):
                            cx.op("act", [X], [junk, ss], lambda e: e.activation(out=junk.t[:], in_=X.t[:, ts, :], func=AF.Square, accum_out=ss.t[:, ts:ts + 1]))
                        cx.op("act", [ss, epsT], [ss], lambda e: e.activation(out=ss.t[:, 0:4], in_=ss.t[:, 0:4], func=AF.Ln, scale=1.0 / 1024, bias=epsT.t[:, 0:1]))
                        cx.op("act", [ss], [rstd], lambda e: e.activation(out=rstd.t[:, 0:4], in_=ss.t[:, 0:4], func=AF.Exp, scale=-0.5))
                        for ts in range(4):
                            ot = cx.ring("outst")
                            cx.op("dve", [X, rstd, gfin], [ot], lambda e: e.scalar_tensor_tensor(out=ot.t[:], in0=X.t[:, ts, :], scalar=rstd.t[:, ts:ts + 1], in1=gfin.t[:],
                                                                                                op0=ALU.mult, op1=ALU.mult))
                            cx.dma("pool", yout[s][t0 + ts * 128:t0 + (ts + 1) * 128, :], ot.t[:], [ot], [], ot, "st")
                    else:
                        cx.dma("pool", sc["xn"][t0:t0 + 512, :].rearrange("(ts p) d -> p ts d", p=128), X.t[:], [X], [], X, "st")
                cx.barrier()

        first = True
        nph = 0
        for s in range(NS):
            for l in range(DEPTH):
                for ph in ((phaseA, phaseA2, phaseB, phaseC, phaseD) if not os.environ.get("K_MERGE") else (phaseA, phaseA2B, phaseC, phaseD)):
                    if stop is not None and nph >= stop:
                        break
                    cx.begin_phase()
                    ph(s, l)
                    cx.end_phase()
                    nph += 1
                    if first and stop is None:
                        cast_weights(1)
                        first = False
        cx.barrier()
    return nc


_CACHE = {}


def kernel(**inputs):
    seq_lens = [2048, 4096, 4096]
    if "nc" not in _CACHE:
        _CACHE["nc"] = build(seq_lens)
    nc = _CACHE["nc"]
    cst, onehot = host_consts()
    xp = np.asarray(inputs["x_prompt"], dtype=np.float32)
    xsm = np.asarray(inputs["x_sample"], dtype=np.float32)
    mp = np.asarray(inputs["mem_prompt"], dtype=np.float32)
    ms = np.asarray(inputs["mem_sample"], dtype=np.float32)
    shared = {k: np.ascontiguousarray(np.asarray(inputs[k], dtype=np.float32)) for k in list(W_SHAPES) + list(SMALL)}
    shared["cst"] = cst
    shared["onehot"] = onehot
    in_maps = []
    for c in range(8):
        m = dict(shared)
        m["x0"] = np.ascontiguousarray(xp[c])
        m["x1"] = np.ascontiguousarray(xsm[2 * c])
        m["x2"] = np.ascontiguousarray(xsm[2 * c + 1])
        m["mem0"] = np.ascontiguousarray(mp[c])
        m["mem1"] = np.ascontiguousarray(ms[2 * c])
        m["mem2"] = np.ascontiguousarray(ms[2 * c + 1])
        in_maps.append(m)
    res = run_bass_kernel_spmd(nc, in_maps, core_ids=list(range(8)))
    y_prompt = np.stack([np.asarray(res.results[c]["y0"], dtype=np.float32) for c in range(8)], axis=0)
    ys = []
    for c in range(8):
        ys.append(np.asarray(res.results[c]["y1"], dtype=np.float32))
        ys.append(np.asarray(res.results[c]["y2"], dtype=np.float32))
    y_sample = np.stack(ys, axis=0)
    return (y_prompt, y_sample)
```

```python
import math
import os
from contextlib import ExitStack
import numpy as np
import concourse.bass as bass
import concourse.mybir as mybir
from concourse.bass_utils import run_bass_kernel_spmd

F32 = mybir.dt.float32
BF16 = mybir.dt.bfloat16
AF = mybir.ActivationFunctionType
ALU = mybir.AluOpType

D = 1024
DEPTH = 2
EPS = 1e-6
W_SHAPES = {
    "w_in": (1024, 8768), "w_proj_attn": (1024, 1024), "w_proj_ssd": (2048, 1024), "w_out": (1024, 1024),
    "w_q_cross": (1024, 1024), "w_kv_cross": (1024, 2048), "w_o_cross": (1024, 1024),
    "w_gate_up": (1024, 5632), "w_down": (2816, 1024),
}
W_ORDER = ["w_in", "w_proj_attn", "w_proj_ssd", "w_out", "w_kv_cross", "w_q_cross", "w_o_cross", "w_gate_up", "w_down"]
SMALL = {
    "rel_bias": (32, 16), "norm_mix": (2, 1024), "attn_sink": (2, 16), "conv_w": (2, 5, 3072), "conv_b": (2, 3072),
    "dt_bias": (2, 2, 32), "a_log": (2, 2, 32), "d_skip": (2, 32), "norm_ssd": (2, 2048), "norm_cross": (2, 1024),
    "norm_mem": (2, 1024), "norm_ffn": (2, 1024), "norm_final": (1024,),
}


def _bucket(rel):
    nb = 16
    max_exact = 8
    ret = np.where(rel > 0, nb, 0)
    n = np.abs(rel)
    nf = np.maximum(n, 1).astype(np.float32)
    large = max_exact + (np.log(nf / np.float32(max_exact)) / np.float32(math.log(16.0))
                         * np.float32(nb - max_exact)).astype(np.int32)
    large = np.minimum(large, nb - 1)
    return ret + np.where(n < max_exact, n, large)


def host_consts():
    i = np.arange(128)
    ident = np.eye(128, dtype=np.float32)
    triu = (i[:, None] <= i[None, :]).astype(np.float32)
    tril = (i[:, None] >= i[None, :]).astype(np.float32)
    sl = (i[:, None] > i[None, :]).astype(np.float32)
    su = (i[:, None] < i[None, :]).astype(np.float32)
    ones = np.ones((128, 128), np.float32)
    m = np.arange(512)
    rel = 255 - m
    win = ((np.abs(rel) <= 128) & (m < 511)).astype(np.float32)
    cst = np.concatenate([ident, triu, tril, sl, su, ones, np.tile(win[None, :], (128, 1))], axis=1)
    b = _bucket(rel.astype(np.int32))
    onehot = np.zeros((32, 512), np.float32)
    for mm in range(511):
        onehot[b[mm], mm] = 1.0
    return np.ascontiguousarray(cst), onehot


class Tok:
    __slots__ = ("key", "sem", "val")

    def __init__(self, key, sem, val):
        self.key, self.sem, self.val = key, sem, val


class Buf:
    def __init__(self, t=None, name=""):
        self.t = t
        self.name = name
        self.w = {}
        self.r = {}
        self.sems = {}
        self.pending = False


class Ctx:
    def __init__(self, nc, es):
        self.nc = nc
        self.es = es
        self.E = {"pe": nc.tensor, "act": nc.scalar, "dve": nc.vector, "pool": nc.gpsimd, "sp": nc.sync}
        self.sem = {}
        self.cnt = {}
        self.seen = {}
        self.last = {}
        for e in self.E:
            self.sem[e] = es.enter_context(nc.semaphore("sem_" + e))
            self.cnt[e] = 0
            self.seen[e] = {}
        self.dma_last = {}
        self.nsem = 5
        self.rings = {}
        self.ps_ptr = 0
        self.flip = 0
        self.sempool = []
        self.phase_sems = []
        self.phase_open = False

    def begin_phase(self):
        self.phase_open = True

    def end_phase(self):
        self.barrier()
        self.sempool.extend(self.phase_sems)
        self.phase_sems = []
        self.phase_open = False

    def wait(self, e, tok):
        if tok is None:
            return
        if tok.key == e and e == "pe":
            return
        seen = self.seen[e]
        if seen.get(tok.key, 0) >= tok.val:
            return
        self.E[e].wait_ge(tok.sem, tok.val)
        seen[tok.key] = tok.val

    def pre(self, e, reads, writes):
        for b in reads:
            for t in b.w.values():
                self.wait(e, t)
        for b in writes:
            for t in b.w.values():
                self.wait(e, t)
            for t in b.r.values():
                self.wait(e, t)

    def _book(self, tok, reads, writes):
        for b in reads:
            b.r[tok.key] = tok
            if tok.key != "pe":
                b.pending = False
        for b in writes:
            if b.r:
                b.w = {}
                b.r = {}
            b.w[tok.key] = tok
            if tok.key == "pe":
                b.pending = True

    def post(self, e, ins, reads, writes):
        self.cnt[e] += 1
        ins.then_inc(self.sem[e], 1)
        tok = Tok(e, self.sem[e], self.cnt[e])
        self.last[e] = tok
        self._book(tok, reads, writes)
        return tok

    def op(self, e, reads, writes, fn):
        self.pre(e, reads, writes)
        ins = fn(self.E[e])
        self.post(e, ins, reads, writes)
        return ins

    def dma(self, q, out, in_, reads, writes, semb, kind="ld", **kw):
        self.pre(q, reads, writes)
        ins = self.E[q].dma_start(out=out, in_=in_, **kw)
        if kind not in semb.sems:
            if self.sempool:
                ent = self.sempool.pop()
            else:
                ent = [self.es.enter_context(self.nc.semaphore("d%d" % self.nsem)), 0, self.nsem]
                self.nsem += 1
            semb.sems[kind] = ent
            if self.phase_open:
                self.phase_sems.append(ent)
        rec = semb.sems[kind]
        rec[1] += 16
        ins.then_inc(rec[0], 16)
        tok = Tok(("d", rec[2]), rec[0], rec[1])
        self.dma_last[tok.key] = tok
        self._book(tok, reads, writes)
        return tok

    def barrier(self):
        nobar = getattr(self, "nobar", set())
        toks = list(self.last.values()) + [t for k, t in self.dma_last.items() if k not in nobar]
        for e in ("pe", "act", "dve", "pool", "sp"):
            for t in toks:
                if t.key == e:
                    continue
                self.wait(e, t)

    def tile(self, name, shape, dt):
        t = self.es.enter_context(self.nc.sbuf_tensor(name, list(shape), dt))
        return Buf(t, name)

    def mkring(self, es, name, n, shape, dt):
        self.uid = getattr(self, "uid", 0) + 1
        self.rings[name] = [[Buf(es.enter_context(self.nc.sbuf_tensor("%s_%d_%d" % (name, self.uid, i), list(shape), dt)),
                                 "%s%d" % (name, i)) for i in range(n)], 0]

    def ring(self, name):
        r = self.rings[name]
        b = r[0][r[1] % len(r[0])]
        r[1] += 1
        return b

    def psum(self, n=1):
        if not hasattr(self, "lru"):
            self.lru = list(range(8))
        for st in self.lru:
            if st + n <= 8 and all(not self.PSB[st + i].pending for i in range(n)):
                out = [self.PSB[st + i] for i in range(n)]
                for i in range(n):
                    self.lru.remove(st + i)
                    self.lru.append(st + i)
                    self.PSB[st + i].pending = True
                return out
        raise AssertionError("no free psum banks for n=%d: pending=%s" % (n, [b.pending for b in self.PSB]))

    def alt(self):
        self.flip ^= 1
        return "act" if self.flip else "dve"


_UQ = [0]


def uq(name):
    _UQ[0] += 1
    return "%s_u%d" % (name, _UQ[0])


def bc(ap2, m):
    p, n = ap2.shape
    return ap2.unsqueeze(2).broadcast_to([p, n, m])


def build(seq_lens, dbg=False, stop=None):
    nc = bass.Bass("TRN2", target_bir_lowering=False)
    NS = len(seq_lens)
    xin = [nc.dram_tensor("x%d" % s, [seq_lens[s], D], F32, kind="ExternalInput").ap() for s in range(NS)]
    memin = [nc.dram_tensor("mem%d" % s, [256, D], F32, kind="ExternalInput").ap() for s in range(NS)]
    Wf = {k: nc.dram_tensor(k, [DEPTH] + list(v), F32, kind="ExternalInput").ap() for k, v in W_SHAPES.items()}
    Sm = {k: nc.dram_tensor(k, list(v), F32, kind="ExternalInput").ap() for k, v in SMALL.items()}
    cst_d = nc.dram_tensor("cst", [128, 1280], F32, kind="ExternalInput").ap()
    oh_d = nc.dram_tensor("onehot", [32, 512], F32, kind="ExternalInput").ap()
    yout = [nc.dram_tensor("y%d" % s, [seq_lens[s], D], F32, kind="ExternalOutput").ap() for s in range(NS)]

    def scr(name, shape, dt):
        return nc.dram_tensor(name, list(shape), dt, kind=("ExternalOutput" if (dbg and not name.startswith("wb_")) else "Internal")).ap()

    Wb = {k: [scr("wb_%s%d" % (k, l), v, BF16) for l in range(DEPTH)] for k, v in W_SHAPES.items()}
    Rd = scr("Rd", [16, 129 * 512], BF16)
    S = {}
    for s in range(NS):
        L = seq_lens[s]
        for l in range(DEPTH):
            S[s, l] = dict(
                qT=scr("qT%d%d" % (s, l), [1024, L], BF16), kT=scr("kT%d%d" % (s, l), [256, L], BF16),
                v=scr("v%d%d" % (s, l), [L, 256], BF16), sz=scr("sz%d%d" % (s, l), [L, 2048], BF16),
                xbc=scr("xbc%d%d" % (s, l), [3072, L + 4], BF16), dt=scr("dt%d%d" % (s, l), [L, 64], F32),
                gat=scr("gat%d%d" % (s, l), [2048, L], BF16), xs=scr("xs%d%d" % (s, l), [L, 2048], BF16),
                B=scr("B%d%d" % (s, l), [L, 512], BF16), BT=scr("BT%d%d" % (s, l), [512, L], BF16),
                CT=scr("CT%d%d" % (s, l), [512, L], BF16), yb=scr("yb%d%d" % (s, l), [L, 2048], BF16),
                ssdT=scr("ssdT%d%d" % (s, l), [2048, L], BF16), OT=scr("OT%d%d" % (s, l), [1024, L], BF16),
                xn=scr("xn%d%d" % (s, l), [L, D], F32),
            )

    with ExitStack() as es:
        cx = Ctx(nc, es)
        PS = es.enter_context(nc.psum_tensor("psall", [128, 8, 512], F32))
        cx.PSB = [Buf(None, "ps%d" % i) for i in range(8)]

        def psap(b):
            return PS[:, cx.PSB.index(b), :]

        def psbf(b):
            return PS[:, cx.PSB.index(b), :].bitcast(BF16)

        cstf = cx.tile("cstf", [128, 1280], F32)
        cstb = cx.tile("cstb", [128, 768], BF16)
        ident_f = cstf.t[:, 0:128]
        triu_f = cstf.t[:, 128:256]
        tril_f = cstf.t[:, 256:384]
        ones_f = cstf.t[:, 640:768]
        win_f = cstf.t[:, 768:1280]
        ident_b = cstb.t[:, 0:128]
        sl_b = cstb.t[:, 384:512]
        su_b = cstb.t[:, 512:640]
        ones_b = cstb.t[:, 640:768]
        gmix = cx.tile("gmix", [128, 2, 8], F32)
        gcross = cx.tile("gcross", [128, 2, 8], F32)
        gmem = cx.tile("gmem", [128, 2, 8], F32)
        gffn = cx.tile("gffn", [128, 2, 8], F32)
        gssd = cx.tile("gssd", [128, 2, 16], F32)
        gfin = cx.tile("gfin", [128, 1024], F32)
        dtb = cx.tile("dtb", [128, 128], F32)
        aneg = cx.tile("aneg", [128, 128], F32)
        dsk = cx.tile("dsk", [128, 64], F32)
        esink = cx.tile("esink", [128, 32], F32)
        cw = cx.tile("cw", [128, 2, 5, 24], F32)
        cb = cx.tile("cb", [128, 2, 24], F32)
        epsT = cx.tile("epsT", [128, 1], F32)
        oneT = cx.tile("oneT", [128, 1], F32)

        pl = Buf(None, "pl")
        with nc.allow_non_contiguous_dma(reason="small parameter loads"):
            cx.dma("sp", cstf.t[:], cst_d[:, :], [], [cstf], pl)
            for tl, nm in ((gmix, "norm_mix"), (gcross, "norm_cross"), (gmem, "norm_mem"), (gffn, "norm_ffn")):
                for l in range(DEPTH):
                    cx.dma("sp", tl.t[:, l, :], Sm[nm][l, :].rearrange("(c p) -> p c", p=128), [], [tl], pl)
            for l in range(DEPTH):
                cx.dma("sp", gssd.t[:, l, :], Sm["norm_ssd"][l, :].rearrange("(c p) -> p c", p=128), [], [gssd], pl)
                for k in range(5):
                    cx.dma("sp", cw.t[:, l, k, :], Sm["conv_w"][l, k, :].rearrange("(c p) -> p c", p=128), [], [cw], pl)
                cx.dma("sp", cb.t[:, l, :], Sm["conv_b"][l, :].rearrange("(c p) -> p c", p=128), [], [cb], pl)
            cx.dma("sp", gfin.t[:], Sm["norm_final"].partition_broadcast(128), [], [gfin], pl)
            cx.dma("sp", dtb.t[:], Sm["dt_bias"].rearrange("l a h -> (l a h)").partition_broadcast(128), [], [dtb], pl)
            cx.dma("sp", aneg.t[:], Sm["a_log"].rearrange("l a h -> (l a h)").partition_broadcast(128), [], [aneg], pl)
            cx.dma("sp", dsk.t[:], Sm["d_skip"].rearrange("l h -> (l h)").partition_broadcast(128), [], [dsk], pl)
            cx.dma("sp", esink.t[:], Sm["attn_sink"].rearrange("l h -> (l h)").partition_broadcast(128), [], [esink], pl)

        cx.barrier()
        WB = {}

        def cast_weights(l):
            for k in W_ORDER:
                K, N = W_SHAPES[k]
                b = N
                while b > 2048 or N % b:
                    b -= 1
                    while N % b:
                        b -= 1
                wbuf = Buf(None, "wb_%s%d" % (k, l))
                WB[k, l] = wbuf
                for r0 in range(0, K, 128):
                    r1 = min(K, r0 + 128)
                    tk = cx.dma("pool", Wb[k][l][r0:r1, :].rearrange("r (a b) -> r a b", b=b),
                                Wf[k][l, r0:r1, :].rearrange("r (a b) -> r a b", b=b), [], [wbuf], wbuf)
                    if not hasattr(cx, "nobar"):
                        cx.nobar = set()
                    cx.nobar.add(tk.key)

        cast_weights(0)

        cx.op("dve", [cstf], [cstb], lambda e: e.tensor_copy(out=cstb.t[:], in_=cstf.t[:, 0:768]))
        cx.op("dve", [], [epsT], lambda e: e.memset(epsT.t[:], EPS))
        cx.op("dve", [], [oneT], lambda e: e.memset(oneT.t[:], 1.0))
        cx.op("act", [aneg], [aneg], lambda e: e.activation(out=aneg.t[:], in_=aneg.t[:], func=AF.Exp))
        cx.op("dve", [aneg], [aneg], lambda e: e.tensor_scalar(out=aneg.t[:], in0=aneg.t[:], scalar1=-1.0, scalar2=None, op0=ALU.mult))
        cx.op("act", [esink], [esink], lambda e: e.activation(out=esink.t[:], in_=esink.t[:], func=AF.Exp))

        with ExitStack() as pes:
            cx.mkring(pes, "Rt", 2, [128, 512], BF16)
            win8 = Buf(pes.enter_context(nc.sbuf_tensor(uq("win8"), [128, 512], F32)))
            negm = Buf(pes.enter_context(nc.sbuf_tensor(uq("negm"), [128, 512], F32)))
            rtmp = Buf(pes.enter_context(nc.sbuf_tensor(uq("rtmp"), [128, 512], F32)))
            cx.op("dve", [cstf], [win8], lambda e: e.tensor_scalar(out=win8.t[:], in0=win_f, scalar1=8.0, scalar2=None, op0=ALU.mult))
            cx.op("dve", [cstf], [negm], lambda e: e.tensor_scalar(out=negm.t[:], in0=win_f, scalar1=-1.0, scalar2=30000.0, op0=ALU.add, op1=ALU.mult))
            oh = Buf(pes.enter_context(nc.sbuf_tensor(uq("oh"), [32, 512], F32)))
            rb = Buf(pes.enter_context(nc.sbuf_tensor(uq("rb"), [32, 16], F32)))
            rbrep = Buf(pes.enter_context(nc.sbuf_tensor(uq("rbrep"), [32, 16, 128], F32)))
            cx.dma("sp", oh.t[:], oh_d[:, :], [], [oh], pl)
            cx.dma("sp", rb.t[:], Sm["rel_bias"][:, :], [], [rb], pl)
            cx.barrier()
            cx.op("dve", [rb], [rbrep], lambda e: e.tensor_copy(out=rbrep.t[:], in_=bc(rb.t[:, :], 128)))
            for h in range(16):
                (bk,) = cx.psum(1)
                cx.op("pe", [rbrep, oh], [bk], lambda e: e.matmul(psap(bk), lhsT=rbrep.t[:, h, :], rhs=oh.t[:, :], start=True, stop=True))
                Rt = cx.ring("Rt")
                cx.op("dve", [bk, win8], [rtmp], lambda e: e.tensor_tensor(out=rtmp.t[:], in0=psap(bk), in1=win8.t[:], op=ALU.mult))
                cx.op("dve", [rtmp, negm], [Rt], lambda e: e.tensor_tensor(out=Rt.t[:], in0=rtmp.t[:], in1=negm.t[:], op=ALU.add))
                cx.dma("pool", Rd[h, 0:65536].rearrange("(p m) -> p m", m=512), Rt.t[:], [Rt], [], Rt, "st")
            cx.barrier()

        def load_tatt(Tatt):
            for h in range(16):
                for j in range(3):
                    off = 383 - 128 * j
                    src = Rd[h, off:off + 128 * 511].rearrange("(p m) -> p m", m=511)[:, 0:128]
                    cx.dma("sp", Tatt.t[:, h, j, :], src, [], [Tatt], Tatt)

        def normT(X, nts, g_ap, hT, ss, rstd, xnr, junk):
            for ts in range(nts):
                cx.op("act", [X], [junk, ss], lambda e: e.activation(out=junk.t[:], in_=X.t[:, ts, :], func=AF.Square,
                                                                      accum_out=ss.t[:, ts:ts + 1]))
            cx.op("act", [ss, epsT], [ss], lambda e: e.activation(out=ss.t[:, 0:nts], in_=ss.t[:, 0:nts], func=AF.Ln,
                                                                   scale=1.0 / 1024, bias=epsT.t[:, 0:1]))
            cx.op("act", [ss], [rstd], lambda e: e.activation(out=rstd.t[:, 0:nts], in_=ss.t[:, 0:nts], func=AF.Exp, scale=-0.5))
            for ts in range(nts):
                xn = cx.ring(xnr)
                cx.op("dve", [X, rstd], [xn], lambda e: e.tensor_scalar(out=xn.t[:], in0=X.t[:, ts, :], scalar1=rstd.t[:, ts:ts + 1],
                                                                         scalar2=None, op0=ALU.mult))
                (bk,) = cx.psum(1)
                pv = psbf(bk).rearrange("p (k j) -> p k j", j=128)
                cx.pre("pe", [xn, cstb], [bk])
                for kc in range(8):
                    ins = nc.tensor.transpose(out=pv[:, kc, :], in_=xn.t[:, kc * 128:(kc + 1) * 128], identity=ident_b)
                cx.post("pe", ins, [xn, cstb], [bk])
                cx.op("dve", [bk], [hT], lambda e: e.tensor_tensor(out=hT.t[:, 0:8, ts * 128:(ts + 1) * 128], in0=pv[:, 0:8, :],
                                                                    in1=bc(g_ap, 128), op=ALU.mult))

        def gemm(A, KC, kp, wkey, l, c0, nw, mode, T, evac):
            W = Wb[wkey][l]
            Ab = getattr(A, "buf", A)
            segs = [(k0, min(k0 + 8, KC)) for k0 in range(0, KC, 8)]
            nout = nw // 128 if mode == "FM" else T // 128
            banks = cx.psum(nout)
            for (k0, k1) in segs:
                wt = cx.ring("wslot")
                cx.dma("sp", wt.t[0:kp, 0:k1 - k0, 0:nw], W[k0 * kp:k1 * kp, c0:c0 + nw].rearrange("(k p) n -> p k n", p=kp),
                       [WB[wkey, l]], [wt], wt)
                for o in range(nout):
                    cx.pre("pe", [wt, Ab], [banks[o]])
                    for k in range(k0, k1):
                        if mode == "FM":
                            ins = nc.tensor.matmul(psap(banks[o])[:, 0:T], lhsT=wt.t[0:kp, k - k0, o * 128:(o + 1) * 128],
                                                   rhs=A.t[0:kp, k, 0:T], start=(k == 0), stop=(k == KC - 1))
                        else:
                            ins = nc.tensor.matmul(psap(banks[o])[:, 0:nw], lhsT=A.t[0:kp, k, o * 128:(o + 1) * 128],
                                                   rhs=wt.t[0:kp, k - k0, 0:nw], start=(k == 0), stop=(k == KC - 1))
                    cx.post("pe", ins, [wt, Ab], [banks[o]])
            for o in range(nout):
                evac(o, banks[o])

        def evac_store(bk, ncols, dst, func=None, ring="stb", eng=None):
            st = cx.ring(ring)
            if func is None:
                e = eng or cx.alt()
                if e == "act":
                    cx.op("act", [bk], [st], lambda e_: e_.activation(out=st.t[:, 0:ncols], in_=psap(bk)[:, 0:ncols], func=AF.Copy))
                else:
                    cx.op("dve", [bk], [st], lambda e_: e_.tensor_copy(out=st.t[:, 0:ncols], in_=psap(bk)[:, 0:ncols]))
            else:
                cx.op("act", [bk], [st], lambda e_: e_.activation(out=st.t[:, 0:ncols], in_=psap(bk)[:, 0:ncols], func=func))
            cx.dma("pool", dst, st.t[:, 0:ncols], [st], [], st, "st")

        def phaseA(s, l):
            L = seq_lens[s]
            sc = S[s, l]
            xsrc = xin[s] if l == 0 else S[s, l - 1]["xn"]
            with ExitStack() as pes:
                cx.mkring(pes, "X", 2, [128, 4, 1024], F32)
                cx.mkring(pes, "hT", 2, [128, 8, 512], BF16)
                cx.mkring(pes, "xnr", 2, [128, 1024], BF16)
                cx.mkring(pes, "wslot", 10, [128, 8, 512], BF16)
                cx.mkring(pes, "stb", 8, [128, 512], BF16)
                cx.mkring(pes, "stf", 3, [128, 64], F32)
                ss = Buf(pes.enter_context(nc.sbuf_tensor(uq("ssA"), [128, 4], F32)))
                rstd = Buf(pes.enter_context(nc.sbuf_tensor(uq("rstdA"), [128, 4], F32)))
                junk = Buf(pes.enter_context(nc.sbuf_tensor(uq("junkA"), [128, 1024], BF16)))
                zt = Buf(pes.enter_context(nc.sbuf_tensor(uq("zt"), [128, 24, 2], BF16)))
                cx.op("dve", [], [zt], lambda e: e.memset(zt.t[:], 0.0))
                with nc.allow_non_contiguous_dma(reason="zero halo"):
                    cx.dma("pool", sc["xbc"][:, 0:2].rearrange("(c p) t -> p c t", p=128), zt.t[:], [zt], [], zt, "st")
                    cx.dma("pool", sc["xbc"][:, L + 2:L + 4].rearrange("(c p) t -> p c t", p=128), zt.t[:], [zt], [], zt, "st")
                def load_norm(st_j):
                    tj = st_j * 512
                    Xj = cx.ring("X")
                    cx.dma("sp", Xj.t[:], xsrc[tj:tj + 512, :].rearrange("(ts p) d -> p ts d", p=128), [], [Xj], Xj)
                    hTj = cx.ring("hT")
                    normT(Xj, 4, gmix.t[:, l, :], hTj, ss, rstd, "xnr", junk)
                    return hTj
                hT_next = load_norm(0)
                for st_i in range(L // 512):
                    t0 = st_i * 512
                    hT = hT_next
                    for cg in range(2):
                        gemm(hT, 8, 128, "w_in", l, cg * 512, 512, "FM", 512,
                             lambda o, bk, cg=cg: evac_store(bk, 512, sc["qT"][(cg * 4 + o) * 128:(cg * 4 + o + 1) * 128, t0:t0 + 512]))
                    gemm(hT, 8, 128, "w_in", l, 1024, 256, "FM", 512,
                         lambda o, bk: evac_store(bk, 512, sc["kT"][o * 128:(o + 1) * 128, t0:t0 + 512]))
                    gemm(hT, 8, 128, "w_in", l, 1280, 256, "TM", 512,
                         lambda o, bk: evac_store(bk, 256, sc["v"][t0 + o * 128:t0 + (o + 1) * 128, :]))
                    for cg in range(4):
                        gemm(hT, 8, 128, "w_in", l, 1536 + cg * 512, 512, "TM", 512,
                             lambda o, bk, cg=cg: evac_store(bk, 512, sc["sz"][t0 + o * 128:t0 + (o + 1) * 128, cg * 512:(cg + 1) * 512], func=AF.Silu))
                    for cg in range(6):
                        gemm(hT, 8, 128, "w_in", l, 3584 + cg * 512, 512, "FM", 512,
                             lambda o, bk, cg=cg: evac_store(bk, 512, sc["xbc"][(cg * 4 + o) * 128:(cg * 4 + o + 1) * 128, 2 + t0:2 + t0 + 512]))

                    def evac_dt(o, bk):
                        st = cx.ring("stf")
                        cx.op("dve", [bk, dtb], [st], lambda e: e.tensor_tensor(out=st.t[:], in0=psap(bk)[:, 0:64], in1=dtb.t[:, l * 64:(l + 1) * 64], op=ALU.add))
                        cx.op("act", [st], [st], lambda e: e.activation(out=st.t[:], in_=st.t[:], func=AF.Exp))
                        cx.op("act", [st, oneT], [st], lambda e: e.activation(out=st.t[:], in_=st.t[:], func=AF.Ln, bias=oneT.t[:, 0:1], scale=1.0))
                        cx.dma("pool", sc["dt"][t0 + o * 128:t0 + (o + 1) * 128, :], st.t[:], [st], [], st, "st")
                    if st_i + 1 < L // 512:
                        hT_next = load_norm(st_i + 1)
                    gemm(hT, 8, 128, "w_in", l, 6656, 64, "TM", 512, evac_dt)
                    for cg in range(4):
                        gemm(hT, 8, 128, "w_in", l, 6720 + cg * 512, 512, "FM", 512,
                             lambda o, bk, cg=cg: evac_store(bk, 512, sc["gat"][(cg * 4 + o) * 128:(cg * 4 + o + 1) * 128, t0:t0 + 512], func=AF.Sigmoid))
                cx.barrier()

        def phaseA2(s, l):
            L = seq_lens[s]
            sc = S[s, l]
            with ExitStack() as pes:
                dg = Buf(pes.enter_context(nc.sbuf_tensor(uq("dg"), [128, 24, 5, 128], BF16)))
                cx.mkring(pes, "xci", 2, [128, 24, 516], BF16)
                cx.mkring(pes, "cv", 6, [128, 512], BF16)
                cx.mkring(pes, "xstm", 2, [128, 4, 2048], BF16)
                cx.mkring(pes, "btm", 2, [128, 4, 512], BF16)
                for cc in range(24):
                    for k in range(5):
                        cx.op("dve", [cstf, cw], [dg], lambda e: e.tensor_scalar(out=dg.t[:, cc, k, :], in0=ident_f, scalar1=cw.t[:, l, k, cc:cc + 1],
                                                                                 scalar2=None, op0=ALU.mult))
                for st_i in range(L // 512):
                    t0 = st_i * 512
                    xi = cx.ring("xci")
                    cx.dma("sp", xi.t[:], sc["xbc"][:, t0:t0 + 516].rearrange("(c p) t -> p c t", p=128), [], [xi], xi)
                    xstm = cx.ring("xstm")
                    btm = cx.ring("btm")
                    def conv(cc):
                        (bk,) = cx.psum(1)
                        cx.pre("pe", [dg, xi], [bk])
                        for k in range(5):
                            ins = nc.tensor.matmul(psap(bk), lhsT=dg.t[:, cc, k, :], rhs=xi.t[:, cc, k:k + 512], start=(k == 0), stop=(k == 4))
                        cx.post("pe", ins, [dg, xi], [bk])
                        cv = cx.ring("cv")
                        cx.op("act", [bk, cb], [cv], lambda e: e.activation(out=cv.t[:], in_=psap(bk), func=AF.Silu, bias=cb.t[:, l, cc:cc + 1], scale=1.0))
                        if cc >= 16:
                            dst = sc["BT"] if cc < 20 else sc["CT"]
                            r0 = (cc - 16) * 128 if cc < 20 else (cc - 20) * 128
                            cx.dma("pool", dst[r0:r0 + 128, t0:t0 + 512], cv.t[:], [cv], [], cv, "st")
                        return cv

                    def trans(cc, cv):
                        if cc >= 20:
                            return
                        (bt,) = cx.psum(1)
                        pv = psbf(bt).rearrange("p (k j) -> p k j", j=128)
                        cx.pre("pe", [cv, cstb], [bt])
                        for ts in range(4):
                            ins = nc.tensor.transpose(out=pv[:, ts, :], in_=cv.t[:, ts * 128:(ts + 1) * 128], identity=ident_b)
                        cx.post("pe", ins, [cv, cstb], [bt])
                        if cc < 16:
                            cx.op("dve", [bt], [xstm], lambda e: e.tensor_copy(out=xstm.t[:, :, cc * 128:(cc + 1) * 128], in_=pv[:, 0:4, :]))
                        else:
                            cx.op("dve", [bt], [btm], lambda e: e.tensor_copy(out=btm.t[:, :, (cc - 16) * 128:(cc - 15) * 128], in_=pv[:, 0:4, :]))
                    cvs = {0: conv(0), 1: conv(1)}
                    for cc in range(24):
                        if cc + 2 < 24:
                            cvs[cc + 2] = conv(cc + 2)
                        trans(cc, cvs.pop(cc))
                    cx.dma("pool", sc["xs"][t0:t0 + 512, :].rearrange("(ts p) c -> p ts c", p=128), xstm.t[:], [xstm], [], xstm, "st")
                    cx.dma("pool", sc["B"][t0:t0 + 512, :].rearrange("(ts p) c -> p ts c", p=128), btm.t[:], [btm], [], btm, "st")
                cx.barrier()

        def phaseA2B(s, l):
            L = seq_lens[s]
            sc = S[s, l]
            nst = L // 512
            with ExitStack() as pes:
                dg = Buf(pes.enter_context(nc.sbuf_tensor(uq("dg"), [128, 24, 5, 128], BF16)))
                cx.mkring(pes, "xci", 2, [128, 24, 516], BF16)
                cx.mkring(pes, "cv", 6, [128, 512], BF16)
                cx.mkring(pes, "xstm", 2, [128, 4, 2048], BF16)
                cx.mkring(pes, "btm", 2, [128, 4, 512], BF16)
                cx.mkring(pes, "CTt", 2, [128, 4, 512], BF16)
                cx.mkring(pes, "dt", 3, [128, 64], F32)
                cx.mkring(pes, "xdd", 2, [128, 2048], BF16)
                cx.mkring(pes, "ybt", 2, [128, 2048], BF16)
                S32 = [Buf(pes.enter_context(nc.sbuf_tensor(uq("S32b"), [128, 512], F32))) for _ in range(4)]
                Sbf = [Buf(pes.enter_context(nc.sbuf_tensor(uq("Sbfb"), [128, 512], BF16))) for _ in range(4)]
                sms = [{k: Buf(pes.enter_context(nc.sbuf_tensor(uq("smb_" + k), [128, n], F32))) for k, n in
                        (("adt", 64), ("cs", 64), ("ee", 64), ("dd", 32), ("w", 32))} for _ in range(2)]
                for g in range(4):
                    cx.op("dve", [], [S32[g]], lambda e: e.memset(S32[g].t[:], 0.0))
                    cx.op("dve", [], [Sbf[g]], lambda e: e.memset(Sbf[g].t[:], 0.0))
                for cc in range(24):
                    for k in range(5):
                        cx.op("dve", [cstf, cw], [dg], lambda e: e.tensor_scalar(out=dg.t[:, cc, k, :], in0=ident_f, scalar1=cw.t[:, l, k, cc:cc + 1],
                                                                                 scalar2=None, op0=ALU.mult))

                def scan_chunk(c, xstm, btm, CTt):
                    t0 = c * 128
                    ci = c % 4
                    tc = slice(ci * 128, (ci + 1) * 128)
                    sm = sms[c % 2]
                    dtt = cx.ring("dt")
                    cx.dma("sp", dtt.t[:], sc["dt"][t0:t0 + 128, :], [], [dtt], dtt)
                    cx.op("dve", [dtt, aneg], [sm["adt"]], lambda e: e.tensor_tensor(out=sm["adt"].t[:], in0=dtt.t[:], in1=aneg.t[:, l * 64:(l + 1) * 64], op=ALU.mult))
                    ssd_decay(dtt, l, 1, sm)
                    xdd = cx.ring("xdd")
                    cx.op("dve", [xstm, sm["w"]], [xdd], lambda e: e.tensor_tensor(out=v3(xdd.t[:]), in0=v3(xstm.t[:, ci, :]), in1=bc(sm["w"].t[:, :], 64), op=ALU.mult))
                    ybt = cx.ring("ybt")
                    for g in range(4):
                        (bk,) = cx.psum(1)
                        cx.op("pe", [CTt, Sbf[g]], [bk], lambda e: e.matmul(psap(bk), lhsT=CTt.t[:, g, tc], rhs=Sbf[g].t[:], start=True, stop=True))
                        cx.op("dve", [bk, sm["ee"]], [ybt], lambda e: e.tensor_tensor(out=v3(ybt.t[:, g * 512:(g + 1) * 512]), in0=v3(psap(bk)),
                                                                                       in1=bc(sm["ee"].t[:, 8 * g:8 * g + 8], 64), op=ALU.mult))
                        (bs,) = cx.psum(1)
                        cx.op("pe", [btm, xdd], [bs], lambda e: e.matmul(psap(bs), lhsT=btm.t[:, ci, g * 128:(g + 1) * 128], rhs=xdd.t[:, g * 512:(g + 1) * 512], start=True, stop=True))
                        cx.op("dve", [S32[g], sm["ee"]], [S32[g]], lambda e: e.tensor_tensor(out=v3(S32[g].t[:]), in0=v3(S32[g].t[:]),
                                                                                          in1=bc(sm["ee"].t[:, 32 + 8 * g:32 + 8 * g + 8], 64), op=ALU.mult))
                        cx.op("dve", [S32[g], bs], [S32[g]], lambda e: e.tensor_tensor(out=S32[g].t[:], in0=S32[g].t[:], in1=psap(bs), op=ALU.add))
                        cx.op("act", [S32[g]], [Sbf[g]], lambda e: e.activation(out=Sbf[g].t[:], in_=S32[g].t[:], func=AF.Copy))
                    cx.dma("pool", sc["yb"][t0:t0 + 128, :], ybt.t[:], [ybt], [], ybt, "st")

                pending = []
                for st_i in range(nst - 1, -1, -1):
                    t0 = st_i * 512
                    xi = cx.ring("xci")
                    cx.dma("sp", xi.t[:], sc["xbc"][:, t0:t0 + 516].rearrange("(c p) t -> p c t", p=128), [], [xi], xi)
                    xstm = cx.ring("xstm")
                    btm = cx.ring("btm")
                    CTt = cx.ring("CTt")

                    def conv(cc):
                        (bk,) = cx.psum(1)
                        cx.pre("pe", [dg, xi], [bk])
                        for k in range(5):
                            ins = nc.tensor.matmul(psap(bk), lhsT=dg.t[:, cc, k, :], rhs=xi.t[:, cc, k:k + 512], start=(k == 0), stop=(k == 4))
                        cx.post("pe", ins, [dg, xi], [bk])
                        if cc >= 20:
                            cx.op("act", [bk, cb], [CTt], lambda e: e.activation(out=CTt.t[:, cc - 20, :], in_=psap(bk), func=AF.Silu, bias=cb.t[:, l, cc:cc + 1], scale=1.0))
                            return None
                        cv = cx.ring("cv")
                        cx.op("act", [bk, cb], [cv], lambda e: e.activation(out=cv.t[:], in_=psap(bk), func=AF.Silu, bias=cb.t[:, l, cc:cc + 1], scale=1.0))
                        if cc >= 16:
                            r0 = (cc - 16) * 128
                            cx.dma("pool", sc["BT"][r0:r0 + 128, t0:t0 + 512], cv.t[:], [cv], [], cv, "st")
                        return cv

                    def trans(cc, cv):
                        if cc >= 20:
                            return
                        (bt,) = cx.psum(1)
                        pv = psbf(bt).rearrange("p (k j) -> p k j", j=128)
                        cx.pre("pe", [cv, cstb], [bt])
                        for ts in range(4):
                            ins = nc.tensor.transpose(out=pv[:, ts, :], in_=cv.t[:, ts * 128:(ts + 1) * 128], identity=ident_b)
                        cx.post("pe", ins, [cv, cstb], [bt])
                        if cc < 16:
                            cx.op("dve", [bt], [xstm], lambda e: e.tensor_copy(out=xstm.t[:, :, cc * 128:(cc + 1) * 128], in_=pv[:, 0:4, :]))
                        else:
                            cx.op("dve", [bt], [btm], lambda e: e.tensor_copy(out=btm.t[:, :, (cc - 16) * 128:(cc - 15) * 128], in_=pv[:, 0:4, :]))
                    cvs = {0: conv(0), 1: conv(1)}
                    for cc in range(24):
                        if cc + 2 < 24:
                            cvs[cc + 2] = conv(cc + 2)
                        trans(cc, cvs.pop(cc))
                        if cc in (3, 9, 15, 21) and pending:
                            pending.pop(0)()
                    while pending:
                        pending.pop(0)()
                    cx.dma("pool", sc["xs"][t0:t0 + 512, :].rearrange("(ts p) c -> p ts c", p=128), xstm.t[:], [xstm], [], xstm, "st")
                    cx.dma("pool", sc["B"][t0:t0 + 512, :].rearrange("(ts p) c -> p ts c", p=128), btm.t[:], [btm], [], btm, "st")
                    cx.dma("pool", sc["CT"][:, t0:t0 + 512].rearrange("(g n) t -> n g t", n=128), CTt.t[:], [CTt], [], CTt, "st")
                    for ci in (3, 2, 1, 0):
                        pending.append(lambda c=st_i * 4 + ci, a=xstm, b_=btm, d=CTt: scan_chunk(c, a, b_, d))
                while pending:
                    pending.pop(0)()
                cx.barrier()

        def ssd_decay(dtt, l, direction, sm):
            d0 = direction * 32
            tri = triu_f if direction == 0 else tril_f
            (bk,) = cx.psum(1)
            cx.pre("pe", [sm["adt"], cstf], [bk])
            nc.tensor.matmul(psap(bk)[:, 0:32], lhsT=tri, rhs=sm["adt"].t[:, d0:d0 + 32], start=True, stop=True)
            ins = nc.tensor.matmul(psap(bk)[:, 32:64], lhsT=ones_f, rhs=sm["adt"].t[:, d0:d0 + 32], start=True, stop=True)
            cx.post("pe", ins, [sm["adt"], cstf], [bk])
            cs, ee, dd, w = sm["cs"], sm["ee"], sm["dd"], sm["w"]
            cx.op("act", [bk], [cs], lambda e: e.activation(out=cs.t[:], in_=psap(bk)[:, 0:64], func=AF.Copy))
            cx.op("act", [cs], [ee], lambda e: e.activation(out=ee.t[:], in_=cs.t[:], func=AF.Exp))
            cx.op("dve", [cs], [dd], lambda e: e.tensor_tensor(out=dd.t[:], in0=cs.t[:, 32:64], in1=cs.t[:, 0:32], op=ALU.subtract))
            cx.op("act", [dd], [dd], lambda e: e.activation(out=dd.t[:], in_=dd.t[:], func=AF.Exp))
            cx.op("dve", [dd, dtt], [w], lambda e: e.tensor_tensor(out=w.t[:], in0=dd.t[:], in1=dtt.t[:, d0:d0 + 32], op=ALU.mult))

        def v3(ap, n=64):
            return ap.rearrange("p (h d) -> p h d", d=n)

        def phaseB(s, l):
            L = seq_lens[s]
            sc = S[s, l]
            C = L // 128
            with ExitStack() as pes:
                cx.mkring(pes, "xs", 3, [128, 2048], BF16)
                cx.mkring(pes, "btm", 3, [128, 512], BF16)
                cx.mkring(pes, "dt", 3, [128, 64], F32)
                cx.mkring(pes, "CT", 2, [128, 4, 512], BF16)
                cx.mkring(pes, "xdd", 2, [128, 2048], BF16)
                cx.mkring(pes, "ybt", 2, [128, 2048], BF16)
                S32 = [Buf(pes.enter_context(nc.sbuf_tensor(uq("S32b"), [128, 512], F32))) for _ in range(4)]
                Sbf = [Buf(pes.enter_context(nc.sbuf_tensor(uq("Sbfb"), [128, 512], BF16))) for _ in range(4)]
                sms = [{k: Buf(pes.enter_context(nc.sbuf_tensor(uq("smb_" + k), [128, n], F32))) for k, n in
                        (("adt", 64), ("cs", 64), ("ee", 64), ("dd", 32), ("w", 32))} for _ in range(2)]
                for g in range(4):
                    cx.op("dve", [], [S32[g]], lambda e: e.memset(S32[g].t[:], 0.0))
                    cx.op("dve", [], [Sbf[g]], lambda e: e.memset(Sbf[g].t[:], 0.0))
                state = {"CT": None}

                def prep(c):
                    t0 = c * 128
                    sm = sms[c % 2]
                    if state["CT"] is None or c % 4 == 3:
                        CTn = cx.ring("CT")
                        s0 = (c // 4) * 512
                        cx.dma("sp", CTn.t[:], sc["CT"][:, s0:s0 + 512].rearrange("(g n) t -> n g t", n=128), [], [CTn], CTn)
                        state["CT"] = CTn
                    xs = cx.ring("xs")
                    cx.dma("sp", xs.t[:], sc["xs"][t0:t0 + 128, :], [], [xs], xs)
                    btm = cx.ring("btm")
                    cx.dma("sp", btm.t[:], sc["B"][t0:t0 + 128, :], [], [btm], btm)
                    dtt = cx.ring("dt")
                    cx.dma("sp", dtt.t[:], sc["dt"][t0:t0 + 128, :], [], [dtt], dtt)
                    cx.op("dve", [dtt, aneg], [sm["adt"]], lambda e: e.tensor_tensor(out=sm["adt"].t[:], in0=dtt.t[:], in1=aneg.t[:, l * 64:(l + 1) * 64], op=ALU.mult))
                    ssd_decay(dtt, l, 1, sm)
                    xdd = cx.ring("xdd")
                    cx.op("dve", [xs, sm["w"]], [xdd], lambda e: e.tensor_tensor(out=v3(xdd.t[:]), in0=v3(xs.t[:]), in1=bc(sm["w"].t[:, :], 64), op=ALU.mult))
                    return dict(CT=state["CT"], btm=btm, sm=sm, xdd=xdd)
                nxt = prep(C - 1)
                for c in range(C - 1, -1, -1):
                    t0 = c * 128
                    P = nxt
                    CT, btm, sm, xdd = P["CT"], P["btm"], P["sm"], P["xdd"]
                    tc = slice((c % 4) * 128, (c % 4 + 1) * 128)
                    ybt = cx.ring("ybt")
                    for g in range(4):
                        (bk,) = cx.psum(1)
                        cx.op("pe", [CT, Sbf[g]], [bk], lambda e: e.matmul(psap(bk), lhsT=CT.t[:, g, tc], rhs=Sbf[g].t[:], start=True, stop=True))
                        cx.op("dve", [bk, sm["ee"]], [ybt], lambda e: e.tensor_tensor(out=v3(ybt.t[:, g * 512:(g + 1) * 512]), in0=v3(psap(bk)),
                                                                                       in1=bc(sm["ee"].t[:, 8 * g:8 * g + 8], 64), op=ALU.mult))
                        (bs,) = cx.psum(1)
                        cx.op("pe", [btm, xdd], [bs], lambda e: e.matmul(psap(bs), lhsT=btm.t[:, g * 128:(g + 1) * 128], rhs=xdd.t[:, g * 512:(g + 1) * 512], start=True, stop=True))
                        cx.op("dve", [S32[g], sm["ee"]], [S32[g]], lambda e: e.tensor_tensor(out=v3(S32[g].t[:]), in0=v3(S32[g].t[:]),
                                                                                          in1=bc(sm["ee"].t[:, 32 + 8 * g:32 + 8 * g + 8], 64), op=ALU.mult))
                        cx.op("dve", [S32[g], bs], [S32[g]], lambda e: e.tensor_tensor(out=S32[g].t[:], in0=S32[g].t[:], in1=psap(bs), op=ALU.add))
                        cx.op("act", [S32[g]], [Sbf[g]], lambda e: e.activation(out=Sbf[g].t[:], in_=S32[g].t[:], func=AF.Copy))
                        if g == 1 and c > 0:
                            nxt = prep(c - 1)
                    cx.dma("pool", sc["yb"][t0:t0 + 128, :], ybt.t[:], [ybt], [], ybt, "st")
                cx.barrier()

        def phaseC(s, l):
            L = seq_lens[s]
            sc = S[s, l]
            C = L // 128
            scale = 64 ** -0.5
            with ExitStack() as pes:
                cx.mkring(pes, "xs", 2, [128, 2048], BF16)
                cx.mkring(pes, "btm", 2, [128, 512], BF16)
                cx.mkring(pes, "dt", 2, [128, 64], F32)
                cx.mkring(pes, "CT", 1, [128, 4, 512], BF16)
                cx.mkring(pes, "BT", 1, [128, 4, 512], BF16)
                cx.mkring(pes, "ybl", 2, [128, 2048], BF16)
                cx.mkring(pes, "szl", 2, [128, 2048], BF16)
                cx.mkring(pes, "xdd", 2, [128, 2048], BF16)
                cx.mkring(pes, "xdf", 2, [128, 2048], BF16)
                cx.mkring(pes, "xdb", 2, [128, 2048], BF16)
                cx.mkring(pes, "rhsf", 2, [128, 8, 128], BF16)
                cx.mkring(pes, "rhsb", 2, [128, 8, 128], BF16)
                cx.mkring(pes, "E", 2, [128, 8, 128], BF16)
                cx.mkring(pes, "Mf", 2, [128, 8, 128], BF16)
                cx.mkring(pes, "Mb", 2, [128, 8, 128], BF16)
                cx.mkring(pes, "yn", 1, [128, 2048], BF16)
                cx.mkring(pes, "ssdT", 2, [128, 16, 128], BF16)
                cx.mkring(pes, "qT", 1, [128, 8, 512], BF16)
                kTz = [Buf(pes.enter_context(nc.sbuf_tensor(uq("kTz"), [128, 4, 768], BF16))) for _ in range(2)]
                for kz in kTz:
                    cx.op("dve", [], [kz], lambda e: e.memset(kz.t[:], 0.0))
                cx.mkring(pes, "V", 1, [128, 6, 256], BF16)
                cx.mkring(pes, "Pa", 2, [128, 3, 512], BF16)
                cx.mkring(pes, "den", 2, [64, 512], F32)
                cx.mkring(pes, "OT", 2, [64, 16, 128], BF16)
                Tatt = Buf(pes.enter_context(nc.sbuf_tensor(uq("Tatt"), [128, 16, 3, 128], BF16)))
                load_tatt(Tatt)
                Dg = Buf(pes.enter_context(nc.sbuf_tensor(uq("Dg"), [128, 32, 128], BF16)))
                for h in range(32):
                    cx.op("dve", [cstf, dsk], [Dg], lambda e: e.tensor_scalar(out=Dg.t[:, h, :], in0=ident_f, scalar1=dsk.t[:, l * 32 + h:l * 32 + h + 1],
                                                                               scalar2=None, op0=ALU.mult))
                Gmf = Buf(pes.enter_context(nc.sbuf_tensor(uq("Gmf"), [128, 4, 128], BF16)))
                Gmb = Buf(pes.enter_context(nc.sbuf_tensor(uq("Gmb"), [128, 4, 128], BF16)))
                S32 = [Buf(pes.enter_context(nc.sbuf_tensor(uq("S32f"), [128, 512], F32))) for _ in range(4)]
                Sbf = [Buf(pes.enter_context(nc.sbuf_tensor(uq("Sbff"), [128, 512], BF16))) for _ in range(4)]
                y32 = [Buf(pes.enter_context(nc.sbuf_tensor(uq("y32g"), [128, 512], F32))) for _ in range(4)]
                ssn4 = Buf(pes.enter_context(nc.sbuf_tensor(uq("ssn4"), [128, 4], F32)))
                ssn = Buf(pes.enter_context(nc.sbuf_tensor(uq("ssn"), [128, 2], F32)))
                smfs = [{k: Buf(pes.enter_context(nc.sbuf_tensor(uq("smf_" + k), [128, n], F32))) for k, n in
                         (("adt", 64), ("cs", 64), ("ee", 64), ("dd", 32), ("w", 32))} for _ in range(2)]
                for g in range(4):
                    cx.op("dve", [], [S32[g]], lambda e: e.memset(S32[g].t[:], 0.0))
                    cx.op("dve", [], [Sbf[g]], lambda e: e.memset(Sbf[g].t[:], 0.0))
                def prepC(cn):
                    tn = cn * 128
                    smf_ = smfs[cn % 2]
                    xs_ = cx.ring("xs")
                    cx.dma("sp", xs_.t[:], sc["xs"][tn:tn + 128, :], [], [xs_], xs_)
                    btm_ = cx.ring("btm")
                    cx.dma("sp", btm_.t[:], sc["B"][tn:tn + 128, :], [], [btm_], btm_)
                    dtt_ = cx.ring("dt")
                    cx.dma("sp", dtt_.t[:], sc["dt"][tn:tn + 128, :], [], [dtt_], dtt_)
                    ybl_ = cx.ring("ybl")
                    cx.dma("sp", ybl_.t[:], sc["yb"][tn:tn + 128, :], [], [ybl_], ybl_)
                    szl_ = cx.ring("szl")
                    cx.dma("sp", szl_.t[:], sc["sz"][tn:tn + 128, :], [], [szl_], szl_)
                    cx.op("dve", [dtt_, aneg], [smf_["adt"]], lambda e: e.tensor_tensor(out=smf_["adt"].t[:], in0=dtt_.t[:], in1=aneg.t[:, l * 64:(l + 1) * 64], op=ALU.mult))
                    ssd_decay(dtt_, l, 0, smf_)
                    xdd_ = cx.ring("xdd")
                    xdf_ = cx.ring("xdf")
                    xdb_ = cx.ring("xdb")
                    cx.op("dve", [xs_, smf_["w"]], [xdd_], lambda e: e.tensor_tensor(out=v3(xdd_.t[:]), in0=v3(xs_.t[:]), in1=bc(smf_["w"].t[:, :], 64), op=ALU.mult))
                    cx.op("dve", [xs_, dtt_], [xdf_], lambda e: e.tensor_tensor(out=v3(xdf_.t[:]), in0=v3(xs_.t[:]), in1=bc(dtt_.t[:, 0:32], 64), op=ALU.mult))
                    cx.op("dve", [xs_, dtt_], [xdb_], lambda e: e.tensor_tensor(out=v3(xdb_.t[:]), in0=v3(xs_.t[:]), in1=bc(dtt_.t[:, 32:64], 64), op=ALU.mult))
                    return dict(xs=xs_, btm=btm_, dtt=dtt_, ybl=ybl_, szl=szl_, smf=smf_, xdd=xdd_, xdf=xdf_, xdb=xdb_)
                nxtP = None
                for c in range(C):
                    t0 = c * 128
                    I = c // 4
                    ci = c % 4
                    tc = slice(ci * 128, (ci + 1) * 128)
                    if ci == 0:
                        s0 = I * 512
                        CT = cx.ring("CT")
                        cx.dma("sp", CT.t[:], sc["CT"][:, s0:s0 + 512].rearrange("(g n) t -> n g t", n=128), [], [CT], CT)
                        BT = cx.ring("BT")
                        cx.dma("sp", BT.t[:], sc["BT"][:, s0:s0 + 512].rearrange("(g n) t -> n g t", n=128), [], [BT], BT)
                        qT = cx.ring("qT")
                        cx.dma("sp", qT.t[:], sc["qT"][:, s0:s0 + 512].rearrange("(c p) t -> p c t", p=128), [], [qT], qT)
                        blo = max(0, 4 * I - 1)
                        bhi = min(C, 4 * I + 5)
                        sl0 = blo - (4 * I - 1)
                        nb = bhi - blo
                        for g in range(4):
                            for half in range(2):
                                r0 = (g // 2) * 128 + (g % 2) * 64
                                cx.dma("sp", kTz[half].t[half * 64:(half + 1) * 64, g, sl0 * 128:(sl0 + nb) * 128],
                                       sc["kT"][r0:r0 + 64, blo * 128:bhi * 128], [], [kTz[half]], kTz[half])
                        V = cx.ring("V")
                        cx.dma("sp", V.t[:, sl0:sl0 + nb, :], sc["v"][blo * 128:bhi * 128, :].rearrange("(b p) c -> p b c", p=128), [], [V], V)
                    ssdT = cx.ring("ssdT")
                    OT = cx.ring("OT")
                    tq = slice(0, 128)
                    if c == 0:
                        nxtP = prepC(0)
                    P = nxtP
                    xs, btm, dtt, ybl, szl, smf, xdd, xdf, xdb = (P[k] for k in ("xs", "btm", "dtt", "ybl", "szl", "smf", "xdd", "xdf", "xdb"))

                    jv = [j for j in range(3) if 0 <= c + j - 1 < C]
                    j0, j1 = jv[0], jv[-1] + 1
                    att = {}

                    def A1(g):
                        bks = [None, None, None]
                        for j in jv:
                            (bks[j],) = cx.psum(1)
                        bsel = bks[j0:j1]
                        cx.pre("pe", [kTz[0], kTz[1], qT, Tatt, cstb], bsel)
                        for j in jv:
                            slot = ci + j
                            nc.tensor.matmul(psap(bks[j]), lhsT=ident_b, rhs=Tatt.t[:, 4 * g:4 * g + 4, j, :], start=True, stop=False)
                            for hl in range(4):
                                h = 4 * g + hl
                                e2 = h % 2
                                ins = nc.tensor.matmul(psap(bks[j])[:, hl * 128:(hl + 1) * 128],
                                                       lhsT=kTz[e2].t[:, g, slot * 128:(slot + 1) * 128],
                                                       rhs=qT.t[:, h // 2, tc], start=False, stop=(hl == 3))
                        cx.post("pe", ins, [kTz[0], kTz[1], qT, Tatt, cstb], bsel)
                        att[g] = (bks, bsel)

                    def A2(g):
                        bks, bsel = att[g]
                        Pa = cx.ring("Pa")
                        for j in jv:
                            cx.op("act", [bks[j]], [Pa], lambda e: e.activation(out=Pa.t[:, j, :], in_=psap(bks[j]), func=AF.Exp, scale=scale))
                        att[g] = Pa

                    def A3(g):
                        Pa = att[g]
                        (bo,) = cx.psum(1)
                        (bd,) = cx.psum(1)
                        cx.pre("pe", [V, Pa, cstb], [bo, bd])
                        for j in jv:
                            slot = ci + j
                            nc.tensor.matmul(psap(bo)[0:64, :], lhsT=V.t[:, slot, g * 64:(g + 1) * 64], rhs=Pa.t[:, j, :], start=(j == j0), stop=(j == j1 - 1))
                        for j in jv:
                            ins = nc.tensor.matmul(psap(bd)[0:64, :], lhsT=ones_b[:, 0:64], rhs=Pa.t[:, j, :], start=(j == j0), stop=(j == j1 - 1))
                        cx.post("pe", ins, [V, Pa, cstb], [bo, bd])
                        att[g] = (bo, bd)

                    def A4a(g):
                        bo, bd = att[g]
                        den = cx.ring("den")
                        cx.op("dve", [bd, esink], [den], lambda e: e.tensor_tensor(out=den.t[:, :].rearrange("p (h q) -> p h q", q=128),
                                                                                    in0=psap(bd)[0:64, :].rearrange("p (h q) -> p h q", q=128),
                                                                                    in1=bc(esink.t[0:64, l * 16 + 4 * g:l * 16 + 4 * g + 4], 128), op=ALU.add))
                        cx.op("act", [den], [den], lambda e: e.activation(out=den.t[:], in_=den.t[:], func=AF.Ln))
                        cx.op("act", [den], [den], lambda e: e.activation(out=den.t[:], in_=den.t[:], func=AF.Exp, scale=-1.0))
                        att[g] = (bo, den)

                    def A4b(g):
                        bo, den = att[g]
                        cx.op("dve", [bo, den], [OT], lambda e: e.tensor_tensor(out=OT.t[:, 4 * g:4 * g + 4, tq],
                                                                                 in0=psap(bo)[0:64, :].rearrange("p (h q) -> p h q", q=128),
                                                                                 in1=den.t[:, :].rearrange("p (h q) -> p h q", q=128), op=ALU.mult))

                    adt = smf["adt"]
                    A1(0)
                    (bG,) = cx.psum(1)
                    cx.pre("pe", [BT, CT], [bG])
                    for g in range(4):
                        ins = nc.tensor.matmul(psap(bG)[:, g * 128:(g + 1) * 128], lhsT=BT.t[:, g, tc], rhs=CT.t[:, g, tc], start=True, stop=True)
                    cx.post("pe", ins, [BT, CT], [bG])
                    pG = psap(bG).rearrange("p (g q) -> p g q", q=128)
                    cx.op("dve", [bG, cstf], [Gmf], lambda e: e.tensor_tensor(out=Gmf.t[:], in0=pG, in1=triu_f.unsqueeze(1).broadcast_to([128, 4, 128]), op=ALU.mult))
                    bG.pending = True
                    cx.op("dve", [bG, cstf], [Gmb], lambda e: e.tensor_tensor(out=Gmb.t[:], in0=pG, in1=tril_f.unsqueeze(1).broadcast_to([128, 4, 128]), op=ALU.mult))
                    ssd = {}

                    def R(g):
                        rhsf = cx.ring("rhsf")
                        rhsb = cx.ring("rhsb")
                        cx.op("dve", [adt, cstb], [rhsf], lambda e: e.tensor_tensor(out=rhsf.t[:], in0=cstb.t[:, 128:256].unsqueeze(1).broadcast_to([128, 8, 128]),
                                                                                     in1=bc(adt.t[:, 8 * g:8 * g + 8], 128), op=ALU.mult))
                        cx.op("dve", [adt, cstb], [rhsb], lambda e: e.tensor_tensor(out=rhsb.t[:], in0=cstb.t[:, 256:384].unsqueeze(1).broadcast_to([128, 8, 128]),
                                                                                     in1=bc(adt.t[:, 32 + 8 * g:32 + 8 * g + 8], 128), op=ALU.mult))
                        ssd["rhs", g] = (rhsf, rhsb)

                    def O(g):
                        gs = slice(g * 512, (g + 1) * 512)
                        (bo,) = cx.psum(1)
                        cx.op("pe", [CT, Sbf[g]], [bo], lambda e: e.matmul(psap(bo), lhsT=CT.t[:, g, tc], rhs=Sbf[g].t[:], start=True, stop=True))
                        cx.op("dve", [bo, smf["ee"]], [y32[g]], lambda e: e.tensor_tensor(out=v3(y32[g].t[:]), in0=v3(psap(bo)), in1=bc(smf["ee"].t[:, 8 * g:8 * g + 8], 64), op=ALU.mult))

                    def St(g):
                        gs = slice(g * 512, (g + 1) * 512)
                        (bs,) = cx.psum(1)
                        cx.op("pe", [btm, xdd], [bs], lambda e: e.matmul(psap(bs), lhsT=btm.t[:, g * 128:(g + 1) * 128], rhs=xdd.t[:, gs], start=True, stop=True))
                        cx.op("dve", [S32[g], smf["ee"]], [S32[g]], lambda e: e.tensor_tensor(out=v3(S32[g].t[:]), in0=v3(S32[g].t[:]),
                                                                                           in1=bc(smf["ee"].t[:, 32 + 8 * g:32 + 8 * g + 8], 64), op=ALU.mult))
                        cx.op("dve", [S32[g], bs], [S32[g]], lambda e: e.tensor_tensor(out=S32[g].t[:], in0=S32[g].t[:], in1=psap(bs), op=ALU.add))
                        cx.op("act", [S32[g]], [Sbf[g]], lambda e: e.activation(out=Sbf[g].t[:], in_=S32[g].t[:], func=AF.Copy))

                    def Y0(g):
                        gs = slice(g * 512, (g + 1) * 512)
                        (bY,) = cx.psum(1)
                        cx.op("pe", [ybl, cstb], [bY], lambda e: e.matmul(psap(bY), lhsT=ident_b, rhs=ybl.t[:, gs], start=True, stop=False))
                        ssd["bY", g] = bY

                    def Dm(g):
                        rhsf, rhsb = ssd["rhs", g]
                        bDs = cx.psum(1) + cx.psum(1)
                        cx.pre("pe", [rhsf, rhsb, cstb], bDs)
                        for hf in range(2):
                            nc.tensor.matmul(psap(bDs[hf]), lhsT=sl_b, rhs=rhsf.t[:, 4 * hf:4 * hf + 4, :], start=True, stop=False)
                            ins = nc.tensor.matmul(psap(bDs[hf]), lhsT=su_b, rhs=rhsb.t[:, 4 * hf:4 * hf + 4, :], start=False, stop=True)
                        cx.post("pe", ins, [rhsf, rhsb, cstb], bDs)
                        ssd["bD", g] = bDs

                    def EM(g):
                        bDs = ssd["bD", g]
                        E = cx.ring("E")
                        for hf in range(2):
                            cx.op("act", [bDs[hf]], [E], lambda e: e.activation(out=E.t[:, 4 * hf:4 * hf + 4, :], in_=psap(bDs[hf]).rearrange("p (h q) -> p h q", q=128), func=AF.Exp))
                        Mf = cx.ring("Mf")
                        Mb = cx.ring("Mb")
                        cx.op("dve", [E, Gmf], [Mf], lambda e: e.tensor_tensor(out=Mf.t[:], in0=E.t[:], in1=Gmf.t[:, g, :].unsqueeze(1).broadcast_to([128, 8, 128]), op=ALU.mult))
                        cx.op("dve", [E, Gmb], [Mb], lambda e: e.tensor_tensor(out=Mb.t[:], in0=E.t[:], in1=Gmb.t[:, g, :].unsqueeze(1).broadcast_to([128, 8, 128]), op=ALU.mult))
                        ssd["M", g] = (Mf, Mb)

                    def Y(g):
                        Mf, Mb = ssd["M", g]
                        bY = ssd["bY", g]
                        cx.pre("pe", [Mf, Mb, xdf, xdb, Dg, xs], [bY])
                        for hl in range(8):
                            h = 8 * g + hl
                            hc = hl * 64
                            nc.tensor.matmul(psap(bY)[:, hc:hc + 64], lhsT=Mf.t[:, hl, :], rhs=xdf.t[:, h * 64:(h + 1) * 64], start=False, stop=False)
                            nc.tensor.matmul(psap(bY)[:, hc:hc + 64], lhsT=Mb.t[:, hl, :], rhs=xdb.t[:, h * 64:(h + 1) * 64], start=False, stop=False)
                            ins = nc.tensor.matmul(psap(bY)[:, hc:hc + 64], lhsT=Dg.t[:, h, :], rhs=xs.t[:, h * 64:(h + 1) * 64], start=False, stop=(hl == 7))
                        cx.post("pe", ins, [Mf, Mb, xdf, xdb, Dg, xs], [bY])

                    def Cmb(g):
                        gs = slice(g * 512, (g + 1) * 512)
                        bY = ssd["bY", g]
                        cx.op("dve", [y32[g], bY], [y32[g]], lambda e: e.tensor_tensor(out=y32[g].t[:], in0=y32[g].t[:], in1=psap(bY), op=ALU.add))
                        cx.op("dve", [y32[g], szl], [y32[g]], lambda e: e.tensor_tensor(out=y32[g].t[:], in0=y32[g].t[:], in1=szl.t[:, gs], op=ALU.mult))
                        cx.op("act", [y32[g]], [yn, ssn4], lambda e: e.activation(out=yn.t[:, gs], in_=y32[g].t[:], func=AF.Square, accum_out=ssn4.t[:, g:g + 1]))

                    yn = cx.ring("yn")
                    R(0)
                    Dm(0)
                    for g in range(4):
                        if g < 3:
                            A1(g + 1)
                        A2(g)
                        O(g)
                        Y0(g)
                        EM(g)
                        A3(g)
                        St(g)
                        if g < 3:
                            R(g + 1)
                            Dm(g + 1)
                        A4a(g)
                        Y(g)
                        A4b(g)
                        Cmb(g)
                    if c + 1 < C:
                        nxtP = prepC(c + 1)
                    cx.op("dve", [ssn4], [ssn], lambda e: e.tensor_reduce(out=ssn.t[:, 0:1], in_=ssn4.t[:, 0:4], op=ALU.add, axis=mybir.AxisListType.X))
                    cx.op("act", [ssn, epsT], [ssn], lambda e: e.activation(out=ssn.t[:, 0:1], in_=ssn.t[:, 0:1], func=AF.Ln, scale=1.0 / 2048, bias=epsT.t[:, 0:1]))
                    cx.op("act", [ssn], [ssn], lambda e: e.activation(out=ssn.t[:, 1:2], in_=ssn.t[:, 0:1], func=AF.Exp, scale=-0.5))
                    for g in range(4):
                        cx.op("act", [y32[g], ssn], [yn], lambda e: e.activation(out=yn.t[:, g * 512:(g + 1) * 512], in_=y32[g].t[:], func=AF.Copy, scale=ssn.t[:, 1:2]))
                    for half in range(2):
                        (bt,) = cx.psum(1)
                        pv = psbf(bt).rearrange("p (k j) -> p k j", j=128)
                        cx.pre("pe", [yn, cstb], [bt])
                        for kc in range(8):
                            k2 = half * 8 + kc
                            ins = nc.tensor.transpose(out=pv[:, kc, :], in_=yn.t[:, k2 * 128:(k2 + 1) * 128], identity=ident_b)
                        cx.post("pe", ins, [yn, cstb], [bt])
                        cx.op("dve", [bt, gssd], [ssdT], lambda e: e.tensor_tensor(out=ssdT.t[:, half * 8:half * 8 + 8, :], in0=pv[:, 0:8, :],
                                                                                   in1=bc(gssd.t[:, l, half * 8:half * 8 + 8], 128), op=ALU.mult))
                    cx.dma("pool", sc["ssdT"][:, t0:t0 + 128].rearrange("(k p) t -> p k t", p=128), ssdT.t[:], [ssdT], [], ssdT, "st")
                    cx.dma("pool", sc["OT"][:, t0:t0 + 128].rearrange("(h d) t -> d h t", d=64), OT.t[:], [OT], [], OT, "st")
                cx.barrier()

        def phaseD(s, l):
            L = seq_lens[s]
            sc = S[s, l]
            last = (l == DEPTH - 1)
            with ExitStack() as pes:
                cx.mkring(pes, "wslot", 5, [128, 8, 512], BF16)
                cx.mkring(pes, "X", 2, [128, 4, 1024], F32)
                cx.mkring(pes, "xnr", 2, [128, 1024], BF16)
                cx.mkring(pes, "hT", 1, [128, 8, 512], BF16)
                cx.mkring(pes, "Pc", 3, [128, 2, 512], BF16)
                cx.mkring(pes, "rec", 3, [128, 512], F32)
                cx.mkring(pes, "sg", 4, [128, 512], BF16)
                cx.mkring(pes, "tmp", 1, [128, 512], F32)
                cx.mkring(pes, "outst", 1, [128, 1024], F32)
                cx.mkring(pes, "OTl", 1, [64, 16, 512], BF16)
                bufA = Buf(pes.enter_context(nc.sbuf_tensor(uq("bufA"), [128, 24, 512], BF16)))
                bufB = Buf(pes.enter_context(nc.sbuf_tensor(uq("bufB"), [128, 16, 512], BF16)))
                bufC = Buf(pes.enter_context(nc.sbuf_tensor(uq("bufC"), [128, 16, 512], BF16)))
                KcT = Buf(pes.enter_context(nc.sbuf_tensor(uq("KcT"), [128, 8, 256], BF16)))
                Vc = Buf(pes.enter_context(nc.sbuf_tensor(uq("Vc"), [128, 2, 1024], BF16)))
                ss = Buf(pes.enter_context(nc.sbuf_tensor(uq("ssD"), [128, 4], F32)))
                rstd = Buf(pes.enter_context(nc.sbuf_tensor(uq("rstdD"), [128, 4], F32)))
                junk = Buf(pes.enter_context(nc.sbuf_tensor(uq("junkD"), [128, 1024], BF16)))

                class View:
                    def __init__(self, buf, ap):
                        self.buf = buf
                        self.t = ap
                Xm = cx.ring("X")
                cx.dma("sp", Xm.t[:, 0:2, :], memin[s][:, :].rearrange("(ts p) d -> p ts d", p=128), [], [Xm], Xm)
                normT(Xm, 2, gmem.t[:, l, :], bufB, ss, rstd, "xnr", junk)

                def evac_k(o, bk, cg):
                    e = cx.alt()
                    if e == "act":
                        cx.op("act", [bk], [KcT], lambda e_: e_.activation(out=KcT.t[:, cg * 4 + o, :], in_=psap(bk)[:, 0:256], func=AF.Copy))
                    else:
                        cx.op("dve", [bk], [KcT], lambda e_: e_.tensor_copy(out=KcT.t[:, cg * 4 + o, :], in_=psap(bk)[:, 0:256]))

                def evac_v(o, bk, cg):
                    cx.op("dve", [bk], [Vc], lambda e_: e_.tensor_copy(out=Vc.t[:, o, cg * 512:(cg + 1) * 512], in_=psap(bk)))
                for cg in range(2):
                    gemm(bufB, 8, 128, "w_kv_cross", l, cg * 512, 512, "FM", 256, lambda o, bk, cg=cg: evac_k(o, bk, cg))
                for cg in range(2):
                    gemm(bufB, 8, 128, "w_kv_cross", l, 1024 + cg * 512, 512, "TM", 256, lambda o, bk, cg=cg: evac_v(o, bk, cg))

                for st_i in range(L // 512):
                    t0 = st_i * 512
                    X = cx.ring("X")
                    xsrc = xin[s] if l == 0 else S[s, l - 1]["xn"]
                    cx.dma("sp", X.t[:], xsrc[t0:t0 + 512, :].rearrange("(ts p) d -> p ts d", p=128), [], [X], X)
                    OTl = cx.ring("OTl")
                    cx.dma("sp", OTl.t[:], sc["OT"][:, t0:t0 + 512].rearrange("(h d) t -> d h t", d=64), [], [OTl], OTl)
                    cx.dma("sp", bufB.t[:], sc["ssdT"][:, t0:t0 + 512].rearrange("(k p) t -> p k t", p=128), [], [bufB], bufB)
                    cx.dma("sp", bufC.t[:], sc["gat"][:, t0:t0 + 512].rearrange("(k p) t -> p k t", p=128), [], [bufC], bufC)

                    def evac_pa(o, bk, cg):
                        oc = cg * 4 + o
                        cx.op("dve", [bk, bufC, bufA], [bufA], lambda e: e.tensor_tensor(out=bufA.t[:, 16 + oc, :], in0=psap(bk), in1=bufC.t[:, oc, :], op=ALU.mult))

                    def evac_ps(o, bk, cg):
                        oc = cg * 4 + o
                        tmp = cx.ring("tmp")
                        cx.op("dve", [bk, bufC], [tmp], lambda e: e.tensor_tensor(out=tmp.t[:], in0=psap(bk), in1=bufC.t[:, 8 + oc, :], op=ALU.mult))
                        cx.op("dve", [tmp, bufA], [bufA], lambda e: e.tensor_tensor(out=bufA.t[:, oc, :], in0=tmp.t[:], in1=bufA.t[:, 16 + oc, :], op=ALU.add))
                    for cg in range(2):
                        gemm(OTl, 16, 64, "w_proj_attn", l, cg * 512, 512, "FM", 512, lambda o, bk, cg=cg: evac_pa(o, bk, cg))
                    for cg in range(2):
                        gemm(bufB, 16, 128, "w_proj_ssd", l, cg * 512, 512, "FM", 512, lambda o, bk, cg=cg: evac_ps(o, bk, cg))

                    def evac_res(o, bk, cg):
                        cx.op("dve", [bk, X], [X], lambda e: e.tensor_tensor(out=X.t[:, o, cg * 512:(cg + 1) * 512], in0=psap(bk), in1=X.t[:, o, cg * 512:(cg + 1) * 512], op=ALU.add))
                    for cg in range(2):
                        gemm(bufA, 8, 128, "w_out", l, cg * 512, 512, "TM", 512, lambda o, bk, cg=cg: evac_res(o, bk, cg))
                    hT = cx.ring("hT")
                    normT(X, 4, gcross.t[:, l, :], hT, ss, rstd, "xnr", junk)

                    def evac_q(o, bk, cg):
                        e = cx.alt()
                        if e == "act":
                            cx.op("act", [bk], [bufB], lambda e_: e_.activation(out=bufB.t[:, cg * 4 + o, :], in_=psap(bk), func=AF.Copy))
                        else:
                            cx.op("dve", [bk], [bufB], lambda e_: e_.tensor_copy(out=bufB.t[:, cg * 4 + o, :], in_=psap(bk)))
                    for cg in range(2):
                        gemm(hT, 8, 128, "w_q_cross", l, cg * 512, 512, "FM", 512, lambda o, bk, cg=cg: evac_q(o, bk, cg))
                    xa = {}

                    def XL(hc):
                        Pc = cx.ring("Pc")
                        for mch in range(2):
                            (bk,) = cx.psum(1)
                            cx.pre("pe", [KcT, bufB], [bk])
                            for dc in range(2):
                                ins = nc.tensor.matmul(psap(bk), lhsT=KcT.t[:, hc * 2 + dc, mch * 128:(mch + 1) * 128], rhs=bufB.t[:, hc * 2 + dc, :],
                                                       start=(dc == 0), stop=(dc == 1))
                            cx.post("pe", ins, [KcT, bufB], [bk])
                            cx.op("act", [bk], [Pc], lambda e: e.activation(out=Pc.t[:, mch, :], in_=psap(bk), func=AF.Exp, scale=1.0 / 16.0))
                        xa[hc] = Pc

                    def XD(hc):
                        Pc = xa[hc]
                        (bd,) = cx.psum(1)
                        cx.pre("pe", [Pc, cstb], [bd])
                        for mch in range(2):
                            ins = nc.tensor.matmul(psap(bd), lhsT=ones_b, rhs=Pc.t[:, mch, :], start=(mch == 0), stop=(mch == 1))
                        cx.post("pe", ins, [Pc, cstb], [bd])
                        rec = cx.ring("rec")
                        cx.op("act", [bd], [rec], lambda e: e.activation(out=rec.t[:], in_=psap(bd), func=AF.Ln))
                        cx.op("act", [rec], [rec], lambda e: e.activation(out=rec.t[:], in_=rec.t[:], func=AF.Exp, scale=-1.0))
                        bks = []
                        for dc in range(2):
                            (bk,) = cx.psum(1)
                            cx.pre("pe", [Vc, Pc], [bk])
                            for mch in range(2):
                                ins = nc.tensor.matmul(psap(bk), lhsT=Vc.t[:, mch, (hc * 2 + dc) * 128:(hc * 2 + dc + 1) * 128], rhs=Pc.t[:, mch, :],
                                                       start=(mch == 0), stop=(mch == 1))
                            cx.post("pe", ins, [Vc, Pc], [bk])
                            bks.append(bk)
                        xa[hc] = (rec, bks)

                    def XE(hc):
                        rec, bks = xa[hc]
                        for dc in range(2):
                            bk = bks[dc]
                            cx.op("dve", [bk, rec, bufB], [bufB], lambda e: e.tensor_tensor(out=bufB.t[:, 8 + hc * 2 + dc, :], in0=psap(bk), in1=rec.t[:], op=ALU.mult))
                    XL(0)
                    for hc in range(4):
                        if hc < 3:
                            XL(hc + 1)
                        XD(hc)
                        if hc > 0:
                            XE(hc - 1)
                    XE(3)
                    ocv = View(bufB, bufB.t[:, 8:16, :])
                    for cg in range(2):
                        gemm(ocv, 8, 128, "w_o_cross", l, cg * 512, 512, "TM", 512, lambda o, bk, cg=cg: evac_res(o, bk, cg))
                    hT = cx.ring("hT")
                    normT(X, 4, gffn.t[:, l, :], hT, ss, rstd, "xnr", junk)
                    for cg in range(6):
                        nw = 512 if cg < 5 else 256
                        held = {}

                        def evac_g(o, bk, held=held):
                            sg = cx.ring("sg")
                            cx.op("act", [bk], [sg], lambda e: e.activation(out=sg.t[:], in_=psap(bk), func=AF.Silu))
                            held[o] = sg

                        def evac_u(o, bk, cg=cg, held=held):
                            sg = held[o]
                            cx.op("dve", [bk, sg, bufA], [bufA], lambda e: e.tensor_tensor(out=bufA.t[:, cg * 4 + o, :], in0=psap(bk), in1=sg.t[:], op=ALU.mult))
                        gemm(hT, 8, 128, "w_gate_up", l, cg * 512, nw, "FM", 512, evac_g)
                        gemm(hT, 8, 128, "w_gate_up", l, 2816 + cg * 512, nw, "FM", 512, evac_u)
                    for cg in range(2):
                        gemm(bufA, 22, 128, "w_down", l, cg * 512, 512, "TM", 512, lambda o, bk, cg=cg: evac_res(o, bk, cg))
                    if last:
                        for ts in range(4):
                            cx.op("act", [X], [junk, ss], lambda e: e.activation(out=junk.t[:], in_=X.t[:, ts, :], func=AF.Square, accum_out=ss.t[:, ts:ts + 1]))
                        cx.op("act", [ss, epsT], [ss], lambda e: e.activation(out=ss.t[:, 0:4], in_=ss.t[:, 0:4], func=AF.Ln, scale=1.0 / 1024, bias=epsT.t[:, 0:1]))
                        cx.op("act", [ss], [rstd], lambda e: e.activation(out=rstd.t[:, 0:4], in_=ss.t[:, 0:4], func=AF.Exp, scale=-0.5))
                        for ts in range(4):
                            ot = cx.ring("outst")
                            cx.op("dve", [X, rstd, gfin], [ot], lambda e: e.scalar_tensor_tensor(out=ot.t[:], in0=X.t[:, ts, :], scalar=rstd.t[:, ts:ts + 1], in1=gfin.t[:],
                                                                                                op0=ALU.mult, op1=ALU.mult))
                            cx.dma("pool", yout[s][t0 + ts * 128:t0 + (ts + 1) * 128, :], ot.t[:], [ot], [], ot, "st")
                    else:
                        cx.dma("pool", sc["xn"][t0:t0 + 512, :].rearrange("(ts p) d -> p ts d", p=128), X.t[:], [X], [], X, "st")
                cx.barrier()

        first = True
        nph = 0
        for s in range(NS):
            for l in range(DEPTH):
                for ph in ((phaseA, phaseA2, phaseB, phaseC, phaseD) if not os.environ.get("K_MERGE") else (phaseA, phaseA2B, phaseC, phaseD)):
                    if stop is not None and nph >= stop:
                        break
                    cx.begin_phase()
                    ph(s, l)
                    cx.end_phase()
                    nph += 1
                    if first and stop is None:
                        cast_weights(1)
                        first = False
        cx.barrier()
    return nc


_CACHE = {}


def kernel(**inputs):
    seq_lens = [2048, 4096, 4096]
    if "nc" not in _CACHE:
        _CACHE["nc"] = build(seq_lens)
    nc = _CACHE["nc"]
    cst, onehot = host_consts()
    xp = np.asarray(inputs["x_prompt"], dtype=np.float32)
    xsm = np.asarray(inputs["x_sample"], dtype=np.float32)
    mp = np.asarray(inputs["mem_prompt"], dtype=np.float32)
    ms = np.asarray(inputs["mem_sample"], dtype=np.float32)
    shared = {k: np.ascontiguousarray(np.asarray(inputs[k], dtype=np.float32)) for k in list(W_SHAPES) + list(SMALL)}
    shared["cst"] = cst
    shared["onehot"] = onehot
    in_maps = []
    for c in range(8):
        m = dict(shared)
        m["x0"] = np.ascontiguousarray(xp[c])
        m["x1"] = np.ascontiguousarray(xsm[2 * c])
        m["x2"] = np.ascontiguousarray(xsm[2 * c + 1])
        m["mem0"] = np.ascontiguousarray(mp[c])
        m["mem1"] = np.ascontiguousarray(ms[2 * c])
        m["mem2"] = np.ascontiguousarray(ms[2 * c + 1])
        in_maps.append(m)
    res = run_bass_kernel_spmd(nc, in_maps, core_ids=list(range(8)))
    y_prompt = np.stack([np.asarray(res.results[c]["y0"], dtype=np.float32) for c in range(8)], axis=0)
    ys = []
    for c in range(8):
        ys.append(np.asarray(res.results[c]["y1"], dtype=np.float32))
        ys.append(np.asarray(res.results[c]["y2"], dtype=np.float32))
    y_sample = np.stack(ys, axis=0)
    return (y_prompt, y_sample)
```

```python
import math
import os
from contextlib import ExitStack
import numpy as np
import concourse.bass as bass
import concourse.mybir as mybir
from concourse.bass_utils import run_bass_kernel_spmd

F32 = mybir.dt.float32
BF16 = mybir.dt.bfloat16
AF = mybir.ActivationFunctionType
ALU = mybir.AluOpType

D = 1024
DEPTH = 2
EPS = 1e-6
W_SHAPES = {
    "w_in": (1024, 8768), "w_proj_attn": (1024, 1024), "w_proj_ssd": (2048, 1024), "w_out": (1024, 1024),
    "w_q_cross": (1024, 1024), "w_kv_cross": (1024, 2048), "w_o_cross": (1024, 1024),
    "w_gate_up": (1024, 5632), "w_down": (2816, 1024),
}
W_ORDER = ["w_in", "w_proj_attn", "w_proj_ssd", "w_out", "w_kv_cross", "w_q_cross", "w_o_cross", "w_gate_up", "w_down"]
SMALL = {
    "rel_bias": (32, 16), "norm_mix": (2, 1024), "attn_sink": (2, 16), "conv_w": (2, 5, 3072), "conv_b": (2, 3072),
    "dt_bias": (2, 2, 32), "a_log": (2, 2, 32), "d_skip": (2, 32), "norm_ssd": (2, 2048), "norm_cross": (2, 1024),
    "norm_mem": (2, 1024), "norm_ffn": (2, 1024), "norm_final": (1024,),
}


def _bucket(rel):
    nb = 16
    max_exact = 8
    ret = np.where(rel > 0, nb, 0)
    n = np.abs(rel)
    nf = np.maximum(n, 1).astype(np.float32)
    large = max_exact + (np.log(nf / np.float32(max_exact)) / np.float32(math.log(16.0))
                         * np.float32(nb - max_exact)).astype(np.int32)
    large = np.minimum(large, nb - 1)
    return ret + np.where(n < max_exact, n, large)


def host_consts():
    i = np.arange(128)
    ident = np.eye(128, dtype=np.float32)
    triu = (i[:, None] <= i[None, :]).astype(np.float32)
    tril = (i[:, None] >= i[None, :]).astype(np.float32)
    sl = (i[:, None] > i[None, :]).astype(np.float32)
    su = (i[:, None] < i[None, :]).astype(np.float32)
    ones = np.ones((128, 128), np.float32)
    m = np.arange(512)
    rel = 255 - m
    win = ((np.abs(rel) <= 128) & (m < 511)).astype(np.float32)
    cst = np.concatenate([ident, triu, tril, sl, su, ones, np.tile(win[None, :], (128, 1))], axis=1)
    b = _bucket(rel.astype(np.int32))
    onehot = np.zeros((32, 512), np.float32)
    for mm in range(511):
        onehot[b[mm], mm] = 1.0
    return np.ascontiguousarray(cst), onehot


class Tok:
    __slots__ = ("key", "sem", "val")

    def __init__(self, key, sem, val):
        self.key, self.sem, self.val = key, sem, val


class Buf:
    def __init__(self, t=None, name=""):
        self.t = t
        self.name = name
        self.w = {}
        self.r = {}
        self.sems = {}
        self.pending = False


class Ctx:
    def __init__(self, nc, es):
        self.nc = nc
        self.es = es
        self.E = {"pe": nc.tensor, "act": nc.scalar, "dve": nc.vector, "pool": nc.gpsimd, "sp": nc.sync}
        self.sem = {}
        self.cnt = {}
        self.seen = {}
        self.last = {}
        for e in self.E:
            self.sem[e] = es.enter_context(nc.semaphore("sem_" + e))
            self.cnt[e] = 0
            self.seen[e] = {}
        self.dma_last = {}
        self.nsem = 5
        self.rings = {}
        self.ps_ptr = 0
        self.flip = 0
        self.sempool = []
        self.phase_sems = []
        self.phase_open = False

    def begin_phase(self):
        self.phase_open = True
        self.wcache = None

    def end_phase(self):
        self.barrier()
        self.sempool.extend(self.phase_sems)
        self.phase_sems = []
        self.phase_open = False

    def wait(self, e, tok):
        if tok is None:
            return
        if tok.key == e and e == "pe":
            return
        seen = self.seen[e]
        if seen.get(tok.key, 0) >= tok.val:
            return
        self.E[e].wait_ge(tok.sem, tok.val)
        seen[tok.key] = tok.val

    def pre(self, e, reads, writes):
        for b in reads:
            for t in b.w.values():
                self.wait(e, t)
        for b in writes:
            for t in b.w.values():
                self.wait(e, t)
            for t in b.r.values():
                self.wait(e, t)

    def _book(self, tok, reads, writes):
        for b in reads:
            b.r[tok.key] = tok
            if tok.key != "pe":
                b.pending = False
        for b in writes:
            if b.r:
                b.w = {}
                b.r = {}
            b.w[tok.key] = tok
            if tok.key == "pe":
                b.pending = True

    def post(self, e, ins, reads, writes):
        self.cnt[e] += 1
        ins.then_inc(self.sem[e], 1)
        tok = Tok(e, self.sem[e], self.cnt[e])
        self.last[e] = tok
        self._book(tok, reads, writes)
        return tok

    def op(self, e, reads, writes, fn):
        self.pre(e, reads, writes)
        ins = fn(self.E[e])
        self.post(e, ins, reads, writes)
        return ins

    def dma(self, q, out, in_, reads, writes, semb, kind="ld", **kw):
        self.pre(q, reads, writes)
        ins = self.E[q].dma_start(out=out, in_=in_, **kw)
        if kind not in semb.sems:
            if self.sempool:
                ent = self.sempool.pop()
            else:
                ent = [self.es.enter_context(self.nc.semaphore("d%d" % self.nsem)), 0, self.nsem]
                self.nsem += 1
            semb.sems[kind] = ent
            if self.phase_open:
                self.phase_sems.append(ent)
        rec = semb.sems[kind]
        rec[1] += 16
        ins.then_inc(rec[0], 16)
        tok = Tok(("d", rec[2]), rec[0], rec[1])
        self.dma_last[tok.key] = tok
        self._book(tok, reads, writes)
        return tok

    def barrier(self):
        nobar = getattr(self, "nobar", set())
        toks = list(self.last.values()) + [t for k, t in self.dma_last.items() if k not in nobar]
        for e in ("pe", "act", "dve", "pool", "sp"):
            for t in toks:
                if t.key == e:
                    continue
                self.wait(e, t)

    def tile(self, name, shape, dt):
        t = self.es.enter_context(self.nc.sbuf_tensor(name, list(shape), dt))
        return Buf(t, name)

    def mkring(self, es, name, n, shape, dt):
        self.uid = getattr(self, "uid", 0) + 1
        self.rings[name] = [[Buf(es.enter_context(self.nc.sbuf_tensor("%s_%d_%d" % (name, self.uid, i), list(shape), dt)),
                                 "%s%d" % (name, i)) for i in range(n)], 0]

    def ring(self, name):
        r = self.rings[name]
        b = r[0][r[1] % len(r[0])]
        r[1] += 1
        return b

    def psum(self, n=1):
        if not hasattr(self, "lru"):
            self.lru = list(range(8))
        for st in self.lru:
            if st + n <= 8 and all(not self.PSB[st + i].pending for i in range(n)):
                out = [self.PSB[st + i] for i in range(n)]
                for i in range(n):
                    self.lru.remove(st + i)
                    self.lru.append(st + i)
                    self.PSB[st + i].pending = True
                return out
        raise AssertionError("no free psum banks for n=%d: pending=%s" % (n, [b.pending for b in self.PSB]))

    def alt(self):
        self.flip ^= 1
        return "act" if self.flip else "dve"


_UQ = [0]


def uq(name):
    _UQ[0] += 1
    return "%s_u%d" % (name, _UQ[0])


def bc(ap2, m):
    p, n = ap2.shape
    return ap2.unsqueeze(2).broadcast_to([p, n, m])


def build(seq_lens, dbg=False, stop=None):
    nc = bass.Bass("TRN2", target_bir_lowering=False)
    NS = len(seq_lens)
    xin = [nc.dram_tensor("x%d" % s, [seq_lens[s], D], F32, kind="ExternalInput").ap() for s in range(NS)]
    memin = [nc.dram_tensor("mem%d" % s, [256, D], F32, kind="ExternalInput").ap() for s in range(NS)]
    Wf = {k: nc.dram_tensor(k, [DEPTH] + list(v), F32, kind="ExternalInput").ap() for k, v in W_SHAPES.items()}
    Sm = {k: nc.dram_tensor(k, list(v), F32, kind="ExternalInput").ap() for k, v in SMALL.items()}
    cst_d = nc.dram_tensor("cst", [128, 1280], F32, kind="ExternalInput").ap()
    oh_d = nc.dram_tensor("onehot", [32, 512], F32, kind="ExternalInput").ap()
    yout = [nc.dram_tensor("y%d" % s, [seq_lens[s], D], F32, kind="ExternalOutput").ap() for s in range(NS)]

    def scr(name, shape, dt):
        return nc.dram_tensor(name, list(shape), dt, kind=("ExternalOutput" if (dbg and not name.startswith("wb_")) else "Internal")).ap()

    Wb = {k: [scr("wb_%s%d" % (k, l), v, BF16) for l in range(DEPTH)] for k, v in W_SHAPES.items()}
    Rd = scr("Rd", [16, 129 * 512], BF16)
    S = {}
    for s in range(NS):
        L = seq_lens[s]
        for l in range(DEPTH):
            S[s, l] = dict(
                qT=scr("qT%d%d" % (s, l), [1024, L], BF16), kT=scr("kT%d%d" % (s, l), [256, L], BF16),
                v=scr("v%d%d" % (s, l), [L, 256], BF16), sz=scr("sz%d%d" % (s, l), [L, 2048], BF16),
                xbc=scr("xbc%d%d" % (s, l), [3072, L + 4], BF16), dt=scr("dt%d%d" % (s, l), [L, 64], F32),
                gat=scr("gat%d%d" % (s, l), [2048, L], BF16), xs=scr("xs%d%d" % (s, l), [L, 2048], BF16),
                B=scr("B%d%d" % (s, l), [L, 512], BF16), BT=scr("BT%d%d" % (s, l), [512, L], BF16),
                CT=scr("CT%d%d" % (s, l), [512, L], BF16), yb=scr("yb%d%d" % (s, l), [L, 2048], BF16),
                ssdT=scr("ssdT%d%d" % (s, l), [2048, L], BF16), OT=scr("OT%d%d" % (s, l), [1024, L], BF16),
                xn=scr("xn%d%d" % (s, l), [L, D], F32),
            )

    with ExitStack() as es:
        cx = Ctx(nc, es)
        PS = es.enter_context(nc.psum_tensor("psall", [128, 8, 512], F32))
        cx.PSB = [Buf(None, "ps%d" % i) for i in range(8)]

        def psap(b):
            return PS[:, cx.PSB.index(b), :]

        def psbf(b):
            return PS[:, cx.PSB.index(b), :].bitcast(BF16)

        cstf = cx.tile("cstf", [128, 1280], F32)
        cstb = cx.tile("cstb", [128, 768], BF16)
        ident_f = cstf.t[:, 0:128]
        triu_f = cstf.t[:, 128:256]
        tril_f = cstf.t[:, 256:384]
        ones_f = cstf.t[:, 640:768]
        win_f = cstf.t[:, 768:1280]
        ident_b = cstb.t[:, 0:128]
        sl_b = cstb.t[:, 384:512]
        su_b = cstb.t[:, 512:640]
        ones_b = cstb.t[:, 640:768]
        gmix = cx.tile("gmix", [128, 2, 8], F32)
        gcross = cx.tile("gcross", [128, 2, 8], F32)
        gmem = cx.tile("gmem", [128, 2, 8], F32)
        gffn = cx.tile("gffn", [128, 2, 8], F32)
        gssd = cx.tile("gssd", [128, 2, 16], F32)
        gfin = cx.tile("gfin", [128, 1024], F32)
        dtb = cx.tile("dtb", [128, 128], F32)
        aneg = cx.tile("aneg", [128, 128], F32)
        dsk = cx.tile("dsk", [128, 64], F32)
        esink = cx.tile("esink", [128, 32], F32)
        cw = cx.tile("cw", [128, 2, 5, 24], F32)
        cb = cx.tile("cb", [128, 2, 24], F32)
        epsT = cx.tile("epsT", [128, 1], F32)
        oneT = cx.tile("oneT", [128, 1], F32)

        pl = Buf(None, "pl")
        with nc.allow_non_contiguous_dma(reason="small parameter loads"):
            cx.dma("sp", cstf.t[:], cst_d[:, :], [], [cstf], pl)
            for tl, nm in ((gmix, "norm_mix"), (gcross, "norm_cross"), (gmem, "norm_mem"), (gffn, "norm_ffn")):
                for l in range(DEPTH):
                    cx.dma("sp", tl.t[:, l, :], Sm[nm][l, :].rearrange("(c p) -> p c", p=128), [], [tl], pl)
            for l in range(DEPTH):
                cx.dma("sp", gssd.t[:, l, :], Sm["norm_ssd"][l, :].rearrange("(c p) -> p c", p=128), [], [gssd], pl)
                for k in range(5):
                    cx.dma("sp", cw.t[:, l, k, :], Sm["conv_w"][l, k, :].rearrange("(c p) -> p c", p=128), [], [cw], pl)
                cx.dma("sp", cb.t[:, l, :], Sm["conv_b"][l, :].rearrange("(c p) -> p c", p=128), [], [cb], pl)
            cx.dma("sp", gfin.t[:], Sm["norm_final"].partition_broadcast(128), [], [gfin], pl)
            cx.dma("sp", dtb.t[:], Sm["dt_bias"].rearrange("l a h -> (l a h)").partition_broadcast(128), [], [dtb], pl)
            cx.dma("sp", aneg.t[:], Sm["a_log"].rearrange("l a h -> (l a h)").partition_broadcast(128), [], [aneg], pl)
            cx.dma("sp", dsk.t[:], Sm["d_skip"].rearrange("l h -> (l h)").partition_broadcast(128), [], [dsk], pl)
            cx.dma("sp", esink.t[:], Sm["attn_sink"].rearrange("l h -> (l h)").partition_broadcast(128), [], [esink], pl)

        cx.barrier()
        WB = {}

        def cast_weights(l):
            for k in W_ORDER:
                K, N = W_SHAPES[k]
                b = N
                while b > 2048 or N % b:
                    b -= 1
                    while N % b:
                        b -= 1
                wbuf = Buf(None, "wb_%s%d" % (k, l))
                WB[k, l] = wbuf
                for r0 in range(0, K, 128):
                    r1 = min(K, r0 + 128)
                    tk = cx.dma("pool", Wb[k][l][r0:r1, :].rearrange("r (a b) -> r a b", b=b),
                                Wf[k][l, r0:r1, :].rearrange("r (a b) -> r a b", b=b), [], [wbuf], wbuf)
                    if not hasattr(cx, "nobar"):
                        cx.nobar = set()
                    cx.nobar.add(tk.key)

        cast_weights(0)

        cx.op("dve", [cstf], [cstb], lambda e: e.tensor_copy(out=cstb.t[:], in_=cstf.t[:, 0:768]))
        cx.op("dve", [], [epsT], lambda e: e.memset(epsT.t[:], EPS))
        cx.op("dve", [], [oneT], lambda e: e.memset(oneT.t[:], 1.0))
        cx.op("act", [aneg], [aneg], lambda e: e.activation(out=aneg.t[:], in_=aneg.t[:], func=AF.Exp))
        cx.op("dve", [aneg], [aneg], lambda e: e.tensor_scalar(out=aneg.t[:], in0=aneg.t[:], scalar1=-1.0, scalar2=None, op0=ALU.mult))
        cx.op("act", [esink], [esink], lambda e: e.activation(out=esink.t[:], in_=esink.t[:], func=AF.Exp))

        with ExitStack() as pes:
            cx.mkring(pes, "Rt", 2, [128, 512], BF16)
            win8 = Buf(pes.enter_context(nc.sbuf_tensor(uq("win8"), [128, 512], F32)))
            negm = Buf(pes.enter_context(nc.sbuf_tensor(uq("negm"), [128, 512], F32)))
            rtmp = Buf(pes.enter_context(nc.sbuf_tensor(uq("rtmp"), [128, 512], F32)))
            cx.op("dve", [cstf], [win8], lambda e: e.tensor_scalar(out=win8.t[:], in0=win_f, scalar1=8.0, scalar2=None, op0=ALU.mult))
            cx.op("dve", [cstf], [negm], lambda e: e.tensor_scalar(out=negm.t[:], in0=win_f, scalar1=-1.0, scalar2=30000.0, op0=ALU.add, op1=ALU.mult))
            oh = Buf(pes.enter_context(nc.sbuf_tensor(uq("oh"), [32, 512], F32)))
            rb = Buf(pes.enter_context(nc.sbuf_tensor(uq("rb"), [32, 16], F32)))
            rbrep = Buf(pes.enter_context(nc.sbuf_tensor(uq("rbrep"), [32, 16, 128], F32)))
            cx.dma("sp", oh.t[:], oh_d[:, :], [], [oh], pl)
            cx.dma("sp", rb.t[:], Sm["rel_bias"][:, :], [], [rb], pl)
            cx.barrier()
            cx.op("dve", [rb], [rbrep], lambda e: e.tensor_copy(out=rbrep.t[:], in_=bc(rb.t[:, :], 128)))
            for h in range(16):
                (bk,) = cx.psum(1)
                cx.op("pe", [rbrep, oh], [bk], lambda e: e.matmul(psap(bk), lhsT=rbrep.t[:, h, :], rhs=oh.t[:, :], start=True, stop=True))
                Rt = cx.ring("Rt")
                cx.op("dve", [bk, win8], [rtmp], lambda e: e.tensor_tensor(out=rtmp.t[:], in0=psap(bk), in1=win8.t[:], op=ALU.mult))
                cx.op("dve", [rtmp, negm], [Rt], lambda e: e.tensor_tensor(out=Rt.t[:], in0=rtmp.t[:], in1=negm.t[:], op=ALU.add))
                cx.dma("pool", Rd[h, 0:65536].rearrange("(p m) -> p m", m=512), Rt.t[:], [Rt], [], Rt, "st")
            cx.barrier()

        def load_tatt(Tatt):
            for h in range(16):
                for j in range(3):
                    off = 383 - 128 * j
                    src = Rd[h, off:off + 128 * 511].rearrange("(p m) -> p m", m=511)[:, 0:128]
                    cx.dma("sp", Tatt.t[:, h, j, :], src, [], [Tatt], Tatt)

        def normT(X, nts, g_ap, hT, ss, rstd, xnr, junk):
            for ts in range(nts):
                cx.op("act", [X], [junk, ss], lambda e: e.activation(out=junk.t[:], in_=X.t[:, ts, :], func=AF.Square,
                                                                      accum_out=ss.t[:, ts:ts + 1]))
            cx.op("act", [ss, epsT], [ss], lambda e: e.activation(out=ss.t[:, 0:nts], in_=ss.t[:, 0:nts], func=AF.Ln,
                                                                   scale=1.0 / 1024, bias=epsT.t[:, 0:1]))
            cx.op("act", [ss], [rstd], lambda e: e.activation(out=rstd.t[:, 0:nts], in_=ss.t[:, 0:nts], func=AF.Exp, scale=-0.5))
            for ts in range(nts):
                xn = cx.ring(xnr)
                cx.op("dve", [X, rstd], [xn], lambda e: e.tensor_scalar(out=xn.t[:], in0=X.t[:, ts, :], scalar1=rstd.t[:, ts:ts + 1],
                                                                         scalar2=None, op0=ALU.mult))
                (bk,) = cx.psum(1)
                pv = psbf(bk).rearrange("p (k j) -> p k j", j=128)
                cx.pre("pe", [xn, cstb], [bk])
                for kc in range(8):
                    ins = nc.tensor.transpose(out=pv[:, kc, :], in_=xn.t[:, kc * 128:(kc + 1) * 128], identity=ident_b)
                cx.post("pe", ins, [xn, cstb], [bk])
                cx.op("dve", [bk], [hT], lambda e: e.tensor_tensor(out=hT.t[:, 0:8, ts * 128:(ts + 1) * 128], in0=pv[:, 0:8, :],
                                                                    in1=bc(g_ap, 128), op=ALU.mult))

        def gemm(A, KC, kp, wkey, l, c0, nw, mode, T, evac):
            W = Wb[wkey][l]
            Ab = getattr(A, "buf", A)
            segs = [(k0, min(k0 + 8, KC)) for k0 in range(0, KC, 8)]
            nout = nw // 128 if mode == "FM" else T // 128
            banks = cx.psum(nout)
            slotw = cx.rings["wslot"][0][0].t.shape[2]
            wide = (len(segs) == 1 and slotw > 512)
            Ncols = W_SHAPES[wkey][1]
            for (k0, k1) in segs:
                ent = getattr(cx, "wcache", None) if wide else None
                if ent is not None and ent[0] == (wkey, l) and ent[2] <= c0 and c0 + nw <= ent[3]:
                    wt, cb = ent[1], ent[2]
                else:
                    wt = cx.ring("wslot")
                    cb = c0
                    cw = min(slotw, Ncols - c0) if wide else nw
                    cx.dma("sp", wt.t[0:kp, 0:k1 - k0, 0:cw], W[k0 * kp:k1 * kp, cb:cb + cw].rearrange("(k p) n -> p k n", p=kp),
                           [WB[wkey, l]], [wt], wt)
                    if wide:
                        cx.wcache = ((wkey, l), wt, cb, cb + cw)
                off = c0 - cb
                for o in range(nout):
                    cx.pre("pe", [wt, Ab], [banks[o]])
                    for k in range(k0, k1):
                        if mode == "FM":
                            ins = nc.tensor.matmul(psap(banks[o])[:, 0:T], lhsT=wt.t[0:kp, k - k0, off + o * 128:off + (o + 1) * 128],
                                                   rhs=A.t[0:kp, k, 0:T], start=(k == 0), stop=(k == KC - 1))
                        else:
                            ins = nc.tensor.matmul(psap(banks[o])[:, 0:nw], lhsT=A.t[0:kp, k, o * 128:(o + 1) * 128],
                                                   rhs=wt.t[0:kp, k - k0, off:off + nw], start=(k == 0), stop=(k == KC - 1))
                    cx.post("pe", ins, [wt, Ab], [banks[o]])
            for o in range(nout):
                evac(o, banks[o])

        def evac_store(bk, ncols, dst, func=None, ring="stb", eng=None):
            st = cx.ring(ring)
            if func is None:
                e = eng or cx.alt()
                if e == "act":
                    cx.op("act", [bk], [st], lambda e_: e_.activation(out=st.t[:, 0:ncols], in_=psap(bk)[:, 0:ncols], func=AF.Copy))
                else:
                    cx.op("dve", [bk], [st], lambda e_: e_.tensor_copy(out=st.t[:, 0:ncols], in_=psap(bk)[:, 0:ncols]))
            else:
                cx.op("act", [bk], [st], lambda e_: e_.activation(out=st.t[:, 0:ncols], in_=psap(bk)[:, 0:ncols], func=func))
            cx.dma("pool", dst, st.t[:, 0:ncols], [st], [], st, "st")

        def phaseA(s, l):
            L = seq_lens[s]
            sc = S[s, l]
            xsrc = xin[s] if l == 0 else S[s, l - 1]["xn"]
            with ExitStack() as pes:
                cx.mkring(pes, "X", 2, [128, 4, 1024], F32)
                cx.mkring(pes, "hT", 2, [128, 8, 512], BF16)
                cx.mkring(pes, "xnr", 2, [128, 1024], BF16)
                cx.mkring(pes, "wslot", 4, [128, 8, 1024], BF16)
                cx.mkring(pes, "stb", 8, [128, 512], BF16)
                cx.mkring(pes, "stf", 3, [128, 64], F32)
                ss = Buf(pes.enter_context(nc.sbuf_tensor(uq("ssA"), [128, 4], F32)))
                rstd = Buf(pes.enter_context(nc.sbuf_tensor(uq("rstdA"), [128, 4], F32)))
                junk = Buf(pes.enter_context(nc.sbuf_tensor(uq("junkA"), [128, 1024], BF16)))
                zt = Buf(pes.enter_context(nc.sbuf_tensor(uq("zt"), [128, 24, 2], BF16)))
                cx.op("dve", [], [zt], lambda e: e.memset(zt.t[:], 0.0))
                with nc.allow_non_contiguous_dma(reason="zero halo"):
                    cx.dma("pool", sc["xbc"][:, 0:2].rearrange("(c p) t -> p c t", p=128), zt.t[:], [zt], [], zt, "st")
                    cx.dma("pool", sc["xbc"][:, L + 2:L + 4].rearrange("(c p) t -> p c t", p=128), zt.t[:], [zt], [], zt, "st")
                def load_norm(st_j):
                    tj = st_j * 512
                    Xj = cx.ring("X")
                    cx.dma("sp", Xj.t[:], xsrc[tj:tj + 512, :].rearrange("(ts p) d -> p ts d", p=128), [], [Xj], Xj)
                    hTj = cx.ring("hT")
                    normT(Xj, 4, gmix.t[:, l, :], hTj, ss, rstd, "xnr", junk)
                    return hTj
                hT_next = load_norm(0)
                for st_i in range(L // 512):
                    t0 = st_i * 512
                    hT = hT_next
                    for cg in range(2):
                        gemm(hT, 8, 128, "w_in", l, cg * 512, 512, "FM", 512,
                             lambda o, bk, cg=cg: evac_store(bk, 512, sc["qT"][(cg * 4 + o) * 128:(cg * 4 + o + 1) * 128, t0:t0 + 512]))
                    gemm(hT, 8, 128, "w_in", l, 1024, 256, "FM", 512,
                         lambda o, bk: evac_store(bk, 512, sc["kT"][o * 128:(o + 1) * 128, t0:t0 + 512]))
                    gemm(hT, 8, 128, "w_in", l, 1280, 256, "TM", 512,
                         lambda o, bk: evac_store(bk, 256, sc["v"][t0 + o * 128:t0 + (o + 1) * 128, :]))
                    for cg in range(4):
                        gemm(hT, 8, 128, "w_in", l, 1536 + cg * 512, 512, "TM", 512,
                             lambda o, bk, cg=cg: evac_store(bk, 512, sc["sz"][t0 + o * 128:t0 + (o + 1) * 128, cg * 512:(cg + 1) * 512], func=AF.Silu))
                    for cg in range(6):
                        gemm(hT, 8, 128, "w_in", l, 3584 + cg * 512, 512, "FM", 512,
                             lambda o, bk, cg=cg: evac_store(bk, 512, sc["xbc"][(cg * 4 + o) * 128:(cg * 4 + o + 1) * 128, 2 + t0:2 + t0 + 512]))

                    def evac_dt(o, bk):
                        st = cx.ring("stf")
                        cx.op("dve", [bk, dtb], [st], lambda e: e.tensor_tensor(out=st.t[:], in0=psap(bk)[:, 0:64], in1=dtb.t[:, l * 64:(l + 1) * 64], op=ALU.add))
                        cx.op("act", [st], [st], lambda e: e.activation(out=st.t[:], in_=st.t[:], func=AF.Exp))
                        cx.op("act", [st, oneT], [st], lambda e: e.activation(out=st.t[:], in_=st.t[:], func=AF.Ln, bias=oneT.t[:, 0:1], scale=1.0))
                        cx.dma("pool", sc["dt"][t0 + o * 128:t0 + (o + 1) * 128, :], st.t[:], [st], [], st, "st")
                    if st_i + 1 < L // 512:
                        hT_next = load_norm(st_i + 1)
                    gemm(hT, 8, 128, "w_in", l, 6656, 64, "TM", 512, evac_dt)
                    for cg in range(4):
                        gemm(hT, 8, 128, "w_in", l, 6720 + cg * 512, 512, "FM", 512,
                             lambda o, bk, cg=cg: evac_store(bk, 512, sc["gat"][(cg * 4 + o) * 128:(cg * 4 + o + 1) * 128, t0:t0 + 512], func=AF.Sigmoid))
                cx.barrier()

        def phaseA2(s, l):
            L = seq_lens[s]
            sc = S[s, l]
            with ExitStack() as pes:
                dg = Buf(pes.enter_context(nc.sbuf_tensor(uq("dg"), [128, 24, 5, 128], BF16)))
                cx.mkring(pes, "xci", 2, [128, 24, 516], BF16)
                cx.mkring(pes, "cv", 6, [128, 512], BF16)
                cx.mkring(pes, "xstm", 2, [128, 4, 2048], BF16)
                cx.mkring(pes, "btm", 2, [128, 4, 512], BF16)
                for cc in range(24):
                    for k in range(5):
                        cx.op("dve", [cstf, cw], [dg], lambda e: e.tensor_scalar(out=dg.t[:, cc, k, :], in0=ident_f, scalar1=cw.t[:, l, k, cc:cc + 1],
                                                                                 scalar2=None, op0=ALU.mult))
                for st_i in range(L // 512):
                    t0 = st_i * 512
                    xi = cx.ring("xci")
                    cx.dma("sp", xi.t[:], sc["xbc"][:, t0:t0 + 516].rearrange("(c p) t -> p c t", p=128), [], [xi], xi)
                    xstm = cx.ring("xstm")
                    btm = cx.ring("btm")
                    def conv(cc):
                        (bk,) = cx.psum(1)
                        cx.pre("pe", [dg, xi], [bk])
                        for k in range(5):
                            ins = nc.tensor.matmul(psap(bk), lhsT=dg.t[:, cc, k, :], rhs=xi.t[:, cc, k:k + 512], start=(k == 0), stop=(k == 4))
                        cx.post("pe", ins, [dg, xi], [bk])
                        cv = cx.ring("cv")
                        cx.op("act", [bk, cb], [cv], lambda e: e.activation(out=cv.t[:], in_=psap(bk), func=AF.Silu, bias=cb.t[:, l, cc:cc + 1], scale=1.0))
                        if cc >= 16:
                            dst = sc["BT"] if cc < 20 else sc["CT"]
                            r0 = (cc - 16) * 128 if cc < 20 else (cc - 20) * 128
                            cx.dma("pool", dst[r0:r0 + 128, t0:t0 + 512], cv.t[:], [cv], [], cv, "st")
                        return cv

                    def trans(cc, cv):
                        if cc >= 20:
                            return
                        (bt,) = cx.psum(1)
                        pv = psbf(bt).rearrange("p (k j) -> p k j", j=128)
                        cx.pre("pe", [cv, cstb], [bt])
                        for ts in range(4):
                            ins = nc.tensor.transpose(out=pv[:, ts, :], in_=cv.t[:, ts * 128:(ts + 1) * 128], identity=ident_b)
                        cx.post("pe", ins, [cv, cstb], [bt])
                        if cc < 16:
                            cx.op("dve", [bt], [xstm], lambda e: e.tensor_copy(out=xstm.t[:, :, cc * 128:(cc + 1) * 128], in_=pv[:, 0:4, :]))
                        else:
                            cx.op("dve", [bt], [btm], lambda e: e.tensor_copy(out=btm.t[:, :, (cc - 16) * 128:(cc - 15) * 128], in_=pv[:, 0:4, :]))
                    cvs = {0: conv(0), 1: conv(1)}
                    for cc in range(24):
                        if cc + 2 < 24:
                            cvs[cc + 2] = conv(cc + 2)
                        trans(cc, cvs.pop(cc))
                    cx.dma("pool", sc["xs"][t0:t0 + 512, :].rearrange("(ts p) c -> p ts c", p=128), xstm.t[:], [xstm], [], xstm, "st")
                    cx.dma("pool", sc["B"][t0:t0 + 512, :].rearrange("(ts p) c -> p ts c", p=128), btm.t[:], [btm], [], btm, "st")
                cx.barrier()

        def phaseA2B(s, l):
            L = seq_lens[s]
            sc = S[s, l]
            nst = L // 512
            with ExitStack() as pes:
                dg = Buf(pes.enter_context(nc.sbuf_tensor(uq("dg"), [128, 24, 5, 128], BF16)))
                cx.mkring(pes, "xci", 2, [128, 24, 516], BF16)
                cx.mkring(pes, "cv", 6, [128, 512], BF16)
                cx.mkring(pes, "xstm", 2, [128, 4, 2048], BF16)
                cx.mkring(pes, "btm", 2, [128, 4, 512], BF16)
                cx.mkring(pes, "CTt", 2, [128, 4, 512], BF16)
                cx.mkring(pes, "dt", 3, [128, 64], F32)
                cx.mkring(pes, "xdd", 2, [128, 2048], BF16)
                cx.mkring(pes, "ybt", 2, [128, 2048], BF16)
                S32 = [Buf(pes.enter_context(nc.sbuf_tensor(uq("S32b"), [128, 512], F32))) for _ in range(4)]
                Sbf = [Buf(pes.enter_context(nc.sbuf_tensor(uq("Sbfb"), [128, 512], BF16))) for _ in range(4)]
                sms = [{k: Buf(pes.enter_context(nc.sbuf_tensor(uq("smb_" + k), [128, n], F32))) for k, n in
                        (("adt", 64), ("cs", 64), ("ee", 64), ("dd", 32), ("w", 32))} for _ in range(2)]
                for g in range(4):
                    cx.op("dve", [], [S32[g]], lambda e: e.memset(S32[g].t[:], 0.0))
                    cx.op("dve", [], [Sbf[g]], lambda e: e.memset(Sbf[g].t[:], 0.0))
                for cc in range(24):
                    for k in range(5):
                        cx.op("dve", [cstf, cw], [dg], lambda e: e.tensor_scalar(out=dg.t[:, cc, k, :], in0=ident_f, scalar1=cw.t[:, l, k, cc:cc + 1],
                                                                                 scalar2=None, op0=ALU.mult))

                def scan_chunk(c, xstm, btm, CTt):
                    t0 = c * 128
                    ci = c % 4
                    tc = slice(ci * 128, (ci + 1) * 128)
                    sm = sms[c % 2]
                    dtt = cx.ring("dt")
                    cx.dma("sp", dtt.t[:], sc["dt"][t0:t0 + 128, :], [], [dtt], dtt)
                    cx.op("dve", [dtt, aneg], [sm["adt"]], lambda e: e.tensor_tensor(out=sm["adt"].t[:], in0=dtt.t[:], in1=aneg.t[:, l * 64:(l + 1) * 64], op=ALU.mult))
                    ssd_decay(dtt, l, 1, sm)
                    xdd = cx.ring("xdd")
                    cx.op("dve", [xstm, sm["w"]], [xdd], lambda e: e.tensor_tensor(out=v3(xdd.t[:]), in0=v3(xstm.t[:, ci, :]), in1=bc(sm["w"].t[:, :], 64), op=ALU.mult))
                    ybt = cx.ring("ybt")
                    for g in range(4):
                        (bk,) = cx.psum(1)
                        cx.op("pe", [CTt, Sbf[g]], [bk], lambda e: e.matmul(psap(bk), lhsT=CTt.t[:, g, tc], rhs=Sbf[g].t[:], start=True, stop=True))
                        cx.op("dve", [bk, sm["ee"]], [ybt], lambda e: e.tensor_tensor(out=v3(ybt.t[:, g * 512:(g + 1) * 512]), in0=v3(psap(bk)),
                                                                                       in1=bc(sm["ee"].t[:, 8 * g:8 * g + 8], 64), op=ALU.mult))
                        (bs,) = cx.psum(1)
                        cx.op("pe", [btm, xdd], [bs], lambda e: e.matmul(psap(bs), lhsT=btm.t[:, ci, g * 128:(g + 1) * 128], rhs=xdd.t[:, g * 512:(g + 1) * 512], start=True, stop=True))
                        cx.op("dve", [S32[g], sm["ee"]], [S32[g]], lambda e: e.tensor_tensor(out=v3(S32[g].t[:]), in0=v3(S32[g].t[:]),
                                                                                          in1=bc(sm["ee"].t[:, 32 + 8 * g:32 + 8 * g + 8], 64), op=ALU.mult))
                        cx.op("dve", [S32[g], bs], [S32[g]], lambda e: e.tensor_tensor(out=S32[g].t[:], in0=S32[g].t[:], in1=psap(bs), op=ALU.add))
                        cx.op("act", [S32[g]], [Sbf[g]], lambda e: e.activation(out=Sbf[g].t[:], in_=S32[g].t[:], func=AF.Copy))
                    cx.dma("pool", sc["yb"][t0:t0 + 128, :], ybt.t[:], [ybt], [], ybt, "st")

                pending = []
                for st_i in range(nst - 1, -1, -1):
                    t0 = st_i * 512
                    xi = cx.ring("xci")
                    cx.dma("sp", xi.t[:], sc["xbc"][:, t0:t0 + 516].rearrange("(c p) t -> p c t", p=128), [], [xi], xi)
                    xstm = cx.ring("xstm")
                    btm = cx.ring("btm")
                    CTt = cx.ring("CTt")

                    def conv(cc):
                        (bk,) = cx.psum(1)
                        cx.pre("pe", [dg, xi], [bk])
                        for k in range(5):
                            ins = nc.tensor.matmul(psap(bk), lhsT=dg.t[:, cc, k, :], rhs=xi.t[:, cc, k:k + 512], start=(k == 0), stop=(k == 4))
                        cx.post("pe", ins, [dg, xi], [bk])
                        if cc >= 20:
                            cx.op("act", [bk, cb], [CTt], lambda e: e.activation(out=CTt.t[:, cc - 20, :], in_=psap(bk), func=AF.Silu, bias=cb.t[:, l, cc:cc + 1], scale=1.0))
                            return None
                        cv = cx.ring("cv")
                        cx.op("act", [bk, cb], [cv], lambda e: e.activation(out=cv.t[:], in_=psap(bk), func=AF.Silu, bias=cb.t[:, l, cc:cc + 1], scale=1.0))
                        if cc >= 16:
                            r0 = (cc - 16) * 128
                            cx.dma("pool", sc["BT"][r0:r0 + 128, t0:t0 + 512], cv.t[:], [cv], [], cv, "st")
                        return cv

                    def trans(cc, cv):
                        if cc >= 20:
                            return
                        (bt,) = cx.psum(1)
                        pv = psbf(bt).rearrange("p (k j) -> p k j", j=128)
                        cx.pre("pe", [cv, cstb], [bt])
                        for ts in range(4):
                            ins = nc.tensor.transpose(out=pv[:, ts, :], in_=cv.t[:, ts * 128:(ts + 1) * 128], identity=ident_b)
                        cx.post("pe", ins, [cv, cstb], [bt])
                        if cc < 16:
                            cx.op("dve", [bt], [xstm], lambda e: e.tensor_copy(out=xstm.t[:, :, cc * 128:(cc + 1) * 128], in_=pv[:, 0:4, :]))
                        else:
                            cx.op("dve", [bt], [btm], lambda e: e.tensor_copy(out=btm.t[:, :, (cc - 16) * 128:(cc - 15) * 128], in_=pv[:, 0:4, :]))
                    cvs = {0: conv(0), 1: conv(1)}
                    for cc in range(24):
                        if cc + 2 < 24:
                            cvs[cc + 2] = conv(cc + 2)
                        trans(cc, cvs.pop(cc))
                        if cc in (3, 9, 15, 21) and pending:
                            pending.pop(0)()
                    while pending:
                        pending.pop(0)()
                    cx.dma("pool", sc["xs"][t0:t0 + 512, :].rearrange("(ts p) c -> p ts c", p=128), xstm.t[:], [xstm], [], xstm, "st")
                    cx.dma("pool", sc["B"][t0:t0 + 512, :].rearrange("(ts p) c -> p ts c", p=128), btm.t[:], [btm], [], btm, "st")
                    cx.dma("pool", sc["CT"][:, t0:t0 + 512].rearrange("(g n) t -> n g t", n=128), CTt.t[:], [CTt], [], CTt, "st")
                    for ci in (3, 2, 1, 0):
                        pending.append(lambda c=st_i * 4 + ci, a=xstm, b_=btm, d=CTt: scan_chunk(c, a, b_, d))
                while pending:
                    pending.pop(0)()
                cx.barrier()

        def ssd_decay(dtt, l, direction, sm):
            d0 = direction * 32
            tri = triu_f if direction == 0 else tril_f
            (bk,) = cx.psum(1)
            cx.pre("pe", [sm["adt"], cstf], [bk])
            nc.tensor.matmul(psap(bk)[:, 0:32], lhsT=tri, rhs=sm["adt"].t[:, d0:d0 + 32], start=True, stop=True)
            ins = nc.tensor.matmul(psap(bk)[:, 32:64], lhsT=ones_f, rhs=sm["adt"].t[:, d0:d0 + 32], start=True, stop=True)
            cx.post("pe", ins, [sm["adt"], cstf], [bk])
            cs, ee, dd, w = sm["cs"], sm["ee"], sm["dd"], sm["w"]
            cx.op("act", [bk], [cs], lambda e: e.activation(out=cs.t[:], in_=psap(bk)[:, 0:64], func=AF.Copy))
            cx.op("act", [cs], [ee], lambda e: e.activation(out=ee.t[:], in_=cs.t[:], func=AF.Exp))
            cx.op("dve", [cs], [dd], lambda e: e.tensor_tensor(out=dd.t[:], in0=cs.t[:, 32:64], in1=cs.t[:, 0:32], op=ALU.subtract))
            cx.op("act", [dd], [dd], lambda e: e.activation(out=dd.t[:], in_=dd.t[:], func=AF.Exp))
            cx.op("dve", [dd, dtt], [w], lambda e: e.tensor_tensor(out=w.t[:], in0=dd.t[:], in1=dtt.t[:, d0:d0 + 32], op=ALU.mult))

        def v3(ap, n=64):
            return ap.rearrange("p (h d) -> p h d", d=n)

        def phaseB(s, l):
            L = seq_lens[s]
            sc = S[s, l]
            C = L // 128
            with ExitStack() as pes:
                cx.mkring(pes, "xs", 3, [128, 2048], BF16)
                cx.mkring(pes, "btm", 3, [128, 512], BF16)
                cx.mkring(pes, "dt", 3, [128, 64], F32)
                cx.mkring(pes, "CT", 2, [128, 4, 512], BF16)
                cx.mkring(pes, "xdd", 2, [128, 2048], BF16)
                cx.mkring(pes, "ybt", 2, [128, 2048], BF16)
                S32 = [Buf(pes.enter_context(nc.sbuf_tensor(uq("S32b"), [128, 512], F32))) for _ in range(4)]
                Sbf = [Buf(pes.enter_context(nc.sbuf_tensor(uq("Sbfb"), [128, 512], BF16))) for _ in range(4)]
                sms = [{k: Buf(pes.enter_context(nc.sbuf_tensor(uq("smb_" + k), [128, n], F32))) for k, n in
                        (("adt", 64), ("cs", 64), ("ee", 64), ("dd", 32), ("w", 32))} for _ in range(2)]
                for g in range(4):
                    cx.op("dve", [], [S32[g]], lambda e: e.memset(S32[g].t[:], 0.0))
                    cx.op("dve", [], [Sbf[g]], lambda e: e.memset(Sbf[g].t[:], 0.0))
                state = {"CT": None}

                def prep(c):
                    t0 = c * 128
                    sm = sms[c % 2]
                    if state["CT"] is None or c % 4 == 3:
                        CTn = cx.ring("CT")
                        s0 = (c // 4) * 512
                        cx.dma("sp", CTn.t[:], sc["CT"][:, s0:s0 + 512].rearrange("(g n) t -> n g t", n=128), [], [CTn], CTn)
                        state["CT"] = CTn
                    xs = cx.ring("xs")
                    cx.dma("sp", xs.t[:], sc["xs"][t0:t0 + 128, :], [], [xs], xs)
                    btm = cx.ring("btm")
                    cx.dma("sp", btm.t[:], sc["B"][t0:t0 + 128, :], [], [btm], btm)
                    dtt = cx.ring("dt")
                    cx.dma("sp", dtt.t[:], sc["dt"][t0:t0 + 128, :], [], [dtt], dtt)
                    cx.op("dve", [dtt, aneg], [sm["adt"]], lambda e: e.tensor_tensor(out=sm["adt"].t[:], in0=dtt.t[:], in1=aneg.t[:, l * 64:(l + 1) * 64], op=ALU.mult))
                    ssd_decay(dtt, l, 1, sm)
                    xdd = cx.ring("xdd")
                    cx.op("dve", [xs, sm["w"]], [xdd], lambda e: e.tensor_tensor(out=v3(xdd.t[:]), in0=v3(xs.t[:]), in1=bc(sm["w"].t[:, :], 64), op=ALU.mult))
                    return dict(CT=state["CT"], btm=btm, sm=sm, xdd=xdd)
                nxt = prep(C - 1)
                for c in range(C - 1, -1, -1):
                    t0 = c * 128
                    P = nxt
                    CT, btm, sm, xdd = P["CT"], P["btm"], P["sm"], P["xdd"]
                    tc = slice((c % 4) * 128, (c % 4 + 1) * 128)
                    ybt = cx.ring("ybt")
                    for g in range(4):
                        (bk,) = cx.psum(1)
                        cx.op("pe", [CT, Sbf[g]], [bk], lambda e: e.matmul(psap(bk), lhsT=CT.t[:, g, tc], rhs=Sbf[g].t[:], start=True, stop=True))
                        cx.op("dve", [bk, sm["ee"]], [ybt], lambda e: e.tensor_tensor(out=v3(ybt.t[:, g * 512:(g + 1) * 512]), in0=v3(psap(bk)),
                                                                                       in1=bc(sm["ee"].t[:, 8 * g:8 * g + 8], 64), op=ALU.mult))
                        (bs,) = cx.psum(1)
                        cx.op("pe", [btm, xdd], [bs], lambda e: e.matmul(psap(bs), lhsT=btm.t[:, g * 128:(g + 1) * 128], rhs=xdd.t[:, g * 512:(g + 1) * 512], start=True, stop=True))
                        cx.op("dve", [S32[g], sm["ee"]], [S32[g]], lambda e: e.tensor_tensor(out=v3(S32[g].t[:]), in0=v3(S32[g].t[:]),
                                                                                          in1=bc(sm["ee"].t[:, 32 + 8 * g:32 + 8 * g + 8], 64), op=ALU.mult))
                        cx.op("dve", [S32[g], bs], [S32[g]], lambda e: e.tensor_tensor(out=S32[g].t[:], in0=S32[g].t[:], in1=psap(bs), op=ALU.add))
                        cx.op("act", [S32[g]], [Sbf[g]], lambda e: e.activation(out=Sbf[g].t[:], in_=S32[g].t[:], func=AF.Copy))
                        if g == 1 and c > 0:
                            nxt = prep(c - 1)
                    cx.dma("pool", sc["yb"][t0:t0 + 128, :], ybt.t[:], [ybt], [], ybt, "st")
                cx.barrier()

        def phaseC(s, l):
            L = seq_lens[s]
            sc = S[s, l]
            C = L // 128
            scale = 64 ** -0.5
            with ExitStack() as pes:
                cx.mkring(pes, "xs", 2, [128, 2048], BF16)
                cx.mkring(pes, "btm", 2, [128, 512], BF16)
                cx.mkring(pes, "dt", 2, [128, 64], F32)
                cx.mkring(pes, "CT", 1, [128, 4, 512], BF16)
                cx.mkring(pes, "BT", 1, [128, 4, 512], BF16)
                cx.mkring(pes, "ybl", 2, [128, 2048], BF16)
                cx.mkring(pes, "szl", 2, [128, 2048], BF16)
                cx.mkring(pes, "xdd", 2, [128, 2048], BF16)
                cx.mkring(pes, "xdf", 2, [128, 2048], BF16)
                cx.mkring(pes, "xdb", 2, [128, 2048], BF16)
                cx.mkring(pes, "rhsf", 2, [128, 8, 128], BF16)
                cx.mkring(pes, "rhsb", 2, [128, 8, 128], BF16)
                cx.mkring(pes, "E", 2, [128, 8, 128], BF16)
                cx.mkring(pes, "Mf", 2, [128, 8, 128], BF16)
                cx.mkring(pes, "Mb", 2, [128, 8, 128], BF16)
                cx.mkring(pes, "yn", 1, [128, 2048], BF16)
                cx.mkring(pes, "ssdT", 2, [128, 16, 128], BF16)
                cx.mkring(pes, "qT", 1, [128, 8, 512], BF16)
                kTz = [Buf(pes.enter_context(nc.sbuf_tensor(uq("kTz"), [128, 4, 768], BF16))) for _ in range(2)]
                for kz in kTz:
                    cx.op("dve", [], [kz], lambda e: e.memset(kz.t[:], 0.0))
                cx.mkring(pes, "V", 1, [128, 6, 256], BF16)
                cx.mkring(pes, "Pa", 2, [128, 3, 512], BF16)
                cx.mkring(pes, "den", 2, [64, 512], F32)
                cx.mkring(pes, "OT", 2, [64, 16, 128], BF16)
                Tatt = Buf(pes.enter_context(nc.sbuf_tensor(uq("Tatt"), [128, 16, 3, 128], BF16)))
                load_tatt(Tatt)
                Dg = Buf(pes.enter_context(nc.sbuf_tensor(uq("Dg"), [128, 32, 128], BF16)))
                for h in range(32):
                    cx.op("dve", [cstf, dsk], [Dg], lambda e: e.tensor_scalar(out=Dg.t[:, h, :], in0=ident_f, scalar1=dsk.t[:, l * 32 + h:l * 32 + h + 1],
                                                                               scalar2=None, op0=ALU.mult))
                Gmf = Buf(pes.enter_context(nc.sbuf_tensor(uq("Gmf"), [128, 4, 128], BF16)))
                Gmb = Buf(pes.enter_context(nc.sbuf_tensor(uq("Gmb"), [128, 4, 128], BF16)))
                S32 = [Buf(pes.enter_context(nc.sbuf_tensor(uq("S32f"), [128, 512], F32))) for _ in range(4)]
                Sbf = [Buf(pes.enter_context(nc.sbuf_tensor(uq("Sbff"), [128, 512], BF16))) for _ in range(4)]
                y32 = [Buf(pes.enter_context(nc.sbuf_tensor(uq("y32g"), [128, 512], F32))) for _ in range(4)]
                ssn4 = Buf(pes.enter_context(nc.sbuf_tensor(uq("ssn4"), [128, 4], F32)))
                ssn = Buf(pes.enter_context(nc.sbuf_tensor(uq("ssn"), [128, 2], F32)))
                smfs = [{k: Buf(pes.enter_context(nc.sbuf_tensor(uq("smf_" + k), [128, n], F32))) for k, n in
                         (("adt", 64), ("cs", 64), ("ee", 64), ("dd", 32), ("w", 32))} for _ in range(2)]
                for g in range(4):
                    cx.op("dve", [], [S32[g]], lambda e: e.memset(S32[g].t[:], 0.0))
                    cx.op("dve", [], [Sbf[g]], lambda e: e.memset(Sbf[g].t[:], 0.0))
                def prepC(cn):
                    tn = cn * 128
                    smf_ = smfs[cn % 2]
                    xs_ = cx.ring("xs")
                    cx.dma("sp", xs_.t[:], sc["xs"][tn:tn + 128, :], [], [xs_], xs_)
                    btm_ = cx.ring("btm")
                    cx.dma("sp", btm_.t[:], sc["B"][tn:tn + 128, :], [], [btm_], btm_)
                    dtt_ = cx.ring("dt")
                    cx.dma("sp", dtt_.t[:], sc["dt"][tn:tn + 128, :], [], [dtt_], dtt_)
                    ybl_ = cx.ring("ybl")
                    cx.dma("sp", ybl_.t[:], sc["yb"][tn:tn + 128, :], [], [ybl_], ybl_)
                    szl_ = cx.ring("szl")
                    cx.dma("sp", szl_.t[:], sc["sz"][tn:tn + 128, :], [], [szl_], szl_)
                    cx.op("dve", [dtt_, aneg], [smf_["adt"]], lambda e: e.tensor_tensor(out=smf_["adt"].t[:], in0=dtt_.t[:], in1=aneg.t[:, l * 64:(l + 1) * 64], op=ALU.mult))
                    ssd_decay(dtt_, l, 0, smf_)
                    xdd_ = cx.ring("xdd")
                    xdf_ = cx.ring("xdf")
                    xdb_ = cx.ring("xdb")
                    cx.op("dve", [xs_, smf_["w"]], [xdd_], lambda e: e.tensor_tensor(out=v3(xdd_.t[:]), in0=v3(xs_.t[:]), in1=bc(smf_["w"].t[:, :], 64), op=ALU.mult))
                    cx.op("dve", [xs_, dtt_], [xdf_], lambda e: e.tensor_tensor(out=v3(xdf_.t[:]), in0=v3(xs_.t[:]), in1=bc(dtt_.t[:, 0:32], 64), op=ALU.mult))
                    cx.op("dve", [xs_, dtt_], [xdb_], lambda e: e.tensor_tensor(out=v3(xdb_.t[:]), in0=v3(xs_.t[:]), in1=bc(dtt_.t[:, 32:64], 64), op=ALU.mult))
                    return dict(xs=xs_, btm=btm_, dtt=dtt_, ybl=ybl_, szl=szl_, smf=smf_, xdd=xdd_, xdf=xdf_, xdb=xdb_)
                nxtP = None
                for c in range(C):
                    t0 = c * 128
                    I = c // 4
                    ci = c % 4
                    tc = slice(ci * 128, (ci + 1) * 128)
                    if ci == 0:
                        s0 = I * 512
                        CT = cx.ring("CT")
                        cx.dma("sp", CT.t[:], sc["CT"][:, s0:s0 + 512].rearrange("(g n) t -> n g t", n=128), [], [CT], CT)
                        BT = cx.ring("BT")
                        cx.dma("sp", BT.t[:], sc["BT"][:, s0:s0 + 512].rearrange("(g n) t -> n g t", n=128), [], [BT], BT)
                        qT = cx.ring("qT")
                        cx.dma("sp", qT.t[:], sc["qT"][:, s0:s0 + 512].rearrange("(c p) t -> p c t", p=128), [], [qT], qT)
                        blo = max(0, 4 * I - 1)
                        bhi = min(C, 4 * I + 5)
                        sl0 = blo - (4 * I - 1)
                        nb = bhi - blo
                        for g in range(4):
                            for half in range(2):
                                r0 = (g // 2) * 128 + (g % 2) * 64
                                cx.dma("sp", kTz[half].t[half * 64:(half + 1) * 64, g, sl0 * 128:(sl0 + nb) * 128],
                                       sc["kT"][r0:r0 + 64, blo * 128:bhi * 128], [], [kTz[half]], kTz[half])
                        V = cx.ring("V")
                        cx.dma("sp", V.t[:, sl0:sl0 + nb, :], sc["v"][blo * 128:bhi * 128, :].rearrange("(b p) c -> p b c", p=128), [], [V], V)
                    ssdT = cx.ring("ssdT")
                    OT = cx.ring("OT")
                    tq = slice(0, 128)
                    if c == 0:
                        nxtP = prepC(0)
                    P = nxtP
                    xs, btm, dtt, ybl, szl, smf, xdd, xdf, xdb = (P[k] for k in ("xs", "btm", "dtt", "ybl", "szl", "smf", "xdd", "xdf", "xdb"))

                    jv = [j for j in range(3) if 0 <= c + j - 1 < C]
                    j0, j1 = jv[0], jv[-1] + 1
                    att = {}

                    def A1(g):
                        bks = [None, None, None]
                        for j in jv:
                            (bks[j],) = cx.psum(1)
                        bsel = bks[j0:j1]
                        cx.pre("pe", [kTz[0], kTz[1], qT, Tatt, cstb], bsel)
                        for j in jv:
                            slot = ci + j
                            nc.tensor.matmul(psap(bks[j]), lhsT=ident_b, rhs=Tatt.t[:, 4 * g:4 * g + 4, j, :], start=True, stop=False)
                            for hl in range(4):
                                h = 4 * g + hl
                                e2 = h % 2
                                ins = nc.tensor.matmul(psap(bks[j])[:, hl * 128:(hl + 1) * 128],
                                                       lhsT=kTz[e2].t[:, g, slot * 128:(slot + 1) * 128],
                                                       rhs=qT.t[:, h // 2, tc], start=False, stop=(hl == 3))
                        cx.post("pe", ins, [kTz[0], kTz[1], qT, Tatt, cstb], bsel)
                        att[g] = (bks, bsel)

                    def A2(g):
                        bks, bsel = att[g]
                        Pa = cx.ring("Pa")
                        for j in jv:
                            cx.op("act", [bks[j]], [Pa], lambda e: e.activation(out=Pa.t[:, j, :], in_=psap(bks[j]), func=AF.Exp, scale=scale))
                        att[g] = Pa

                    def A3(g):
                        Pa = att[g]
                        (bo,) = cx.psum(1)
                        (bd,) = cx.psum(1)
                        cx.pre("pe", [V, Pa, cstb], [bo, bd])
                        for j in jv:
                            slot = ci + j
                            nc.tensor.matmul(psap(bo)[0:64, :], lhsT=V.t[:, slot, g * 64:(g + 1) * 64], rhs=Pa.t[:, j, :], start=(j == j0), stop=(j == j1 - 1))
                        for j in jv:
                            ins = nc.tensor.matmul(psap(bd)[0:64, :], lhsT=ones_b[:, 0:64], rhs=Pa.t[:, j, :], start=(j == j0), stop=(j == j1 - 1))
                        cx.post("pe", ins, [V, Pa, cstb], [bo, bd])
                        att[g] = (bo, bd)

                    def A4a(g):
                        bo, bd = att[g]
                        den = cx.ring("den")
                        cx.op("dve", [bd, esink], [den], lambda e: e.tensor_tensor(out=den.t[:, :].rearrange("p (h q) -> p h q", q=128),
                                                                                    in0=psap(bd)[0:64, :].rearrange("p (h q) -> p h q", q=128),
                                                                                    in1=bc(esink.t[0:64, l * 16 + 4 * g:l * 16 + 4 * g + 4], 128), op=ALU.add))
                        cx.op("act", [den], [den], lambda e: e.activation(out=den.t[:], in_=den.t[:], func=AF.Ln))
                        cx.op("act", [den], [den], lambda e: e.activation(out=den.t[:], in_=den.t[:], func=AF.Exp, scale=-1.0))
                        att[g] = (bo, den)

                    def A4b(g):
                        bo, den = att[g]
                        cx.op("dve", [bo, den], [OT], lambda e: e.tensor_tensor(out=OT.t[:, 4 * g:4 * g + 4, tq],
                                                                                 in0=psap(bo)[0:64, :].rearrange("p (h q) -> p h q", q=128),
                                                                                 in1=den.t[:, :].rearrange("p (h q) -> p h q", q=128), op=ALU.mult))

                    adt = smf["adt"]
                    A1(0)
                    (bG,) = cx.psum(1)
                    cx.pre("pe", [BT, CT], [bG])
                    for g in range(4):
                        ins = nc.tensor.matmul(psap(bG)[:, g * 128:(g + 1) * 128], lhsT=BT.t[:, g, tc], rhs=CT.t[:, g, tc], start=True, stop=True)
                    cx.post("pe", ins, [BT, CT], [bG])
                    pG = psap(bG).rearrange("p (g q) -> p g q", q=128)
                    cx.op("dve", [bG, cstf], [Gmf], lambda e: e.tensor_tensor(out=Gmf.t[:], in0=pG, in1=triu_f.unsqueeze(1).broadcast_to([128, 4, 128]), op=ALU.mult))
                    bG.pending = True
                    cx.op("dve", [bG, cstf], [Gmb], lambda e: e.tensor_tensor(out=Gmb.t[:], in0=pG, in1=tril_f.unsqueeze(1).broadcast_to([128, 4, 128]), op=ALU.mult))
                    ssd = {}

                    def R(g):
                        rhsf = cx.ring("rhsf")
                        rhsb = cx.ring("rhsb")
                        cx.op("dve", [adt, cstb], [rhsf], lambda e: e.tensor_tensor(out=rhsf.t[:], in0=cstb.t[:, 128:256].unsqueeze(1).broadcast_to([128, 8, 128]),
                                                                                     in1=bc(adt.t[:, 8 * g:8 * g + 8], 128), op=ALU.mult))
                        cx.op("dve", [adt, cstb], [rhsb], lambda e: e.tensor_tensor(out=rhsb.t[:], in0=cstb.t[:, 256:384].unsqueeze(1).broadcast_to([128, 8, 128]),
                                                                                     in1=bc(adt.t[:, 32 + 8 * g:32 + 8 * g + 8], 128), op=ALU.mult))
                        ssd["rhs", g] = (rhsf, rhsb)

                    def O(g):
                        gs = slice(g * 512, (g + 1) * 512)
                        (bo,) = cx.psum(1)
                        cx.op("pe", [CT, Sbf[g]], [bo], lambda e: e.matmul(psap(bo), lhsT=CT.t[:, g, tc], rhs=Sbf[g].t[:], start=True, stop=True))
                        cx.op("dve", [bo, smf["ee"]], [y32[g]], lambda e: e.tensor_tensor(out=v3(y32[g].t[:]), in0=v3(psap(bo)), in1=bc(smf["ee"].t[:, 8 * g:8 * g + 8], 64), op=ALU.mult))

                    def St(g):
                        gs = slice(g * 512, (g + 1) * 512)
                        (bs,) = cx.psum(1)
                        cx.op("pe", [btm, xdd], [bs], lambda e: e.matmul(psap(bs), lhsT=btm.t[:, g * 128:(g + 1) * 128], rhs=xdd.t[:, gs], start=True, stop=True))
                        cx.op("dve", [S32[g], smf["ee"]], [S32[g]], lambda e: e.tensor_tensor(out=v3(S32[g].t[:]), in0=v3(S32[g].t[:]),
                                                                                           in1=bc(smf["ee"].t[:, 32 + 8 * g:32 + 8 * g + 8], 64), op=ALU.mult))
                        cx.op("dve", [S32[g], bs], [S32[g]], lambda e: e.tensor_tensor(out=S32[g].t[:], in0=S32[g].t[:], in1=psap(bs), op=ALU.add))
                        cx.op("act", [S32[g]], [Sbf[g]], lambda e: e.activation(out=Sbf[g].t[:], in_=S32[g].t[:], func=AF.Copy))

                    def Y0(g):
                        gs = slice(g * 512, (g + 1) * 512)
                        (bY,) = cx.psum(1)
                        cx.op("pe", [ybl, cstb], [bY], lambda e: e.matmul(psap(bY), lhsT=ident_b, rhs=ybl.t[:, gs], start=True, stop=False))
                        ssd["bY", g] = bY

                    def Dm(g):
                        rhsf, rhsb = ssd["rhs", g]
                        bDs = cx.psum(1) + cx.psum(1)
                        cx.pre("pe", [rhsf, rhsb, cstb], bDs)
                        for hf in range(2):
                            nc.tensor.matmul(psap(bDs[hf]), lhsT=sl_b, rhs=rhsf.t[:, 4 * hf:4 * hf + 4, :], start=True, stop=False)
                            ins = nc.tensor.matmul(psap(bDs[hf]), lhsT=su_b, rhs=rhsb.t[:, 4 * hf:4 * hf + 4, :], start=False, stop=True)
                        cx.post("pe", ins, [rhsf, rhsb, cstb], bDs)
                        ssd["bD", g] = bDs

                    def EM(g):
                        bDs = ssd["bD", g]
                        E = cx.ring("E")
                        for hf in range(2):
                            cx.op("act", [bDs[hf]], [E], lambda e: e.activation(out=E.t[:, 4 * hf:4 * hf + 4, :], in_=psap(bDs[hf]).rearrange("p (h q) -> p h q", q=128), func=AF.Exp))
                        Mf = cx.ring("Mf")
                        Mb = cx.ring("Mb")
                        cx.op("dve", [E, Gmf], [Mf], lambda e: e.tensor_tensor(out=Mf.t[:], in0=E.t[:], in1=Gmf.t[:, g, :].unsqueeze(1).broadcast_to([128, 8, 128]), op=ALU.mult))
                        cx.op("dve", [E, Gmb], [Mb], lambda e: e.tensor_tensor(out=Mb.t[:], in0=E.t[:], in1=Gmb.t[:, g, :].unsqueeze(1).broadcast_to([128, 8, 128]), op=ALU.mult))
                        ssd["M", g] = (Mf, Mb)

                    def Y(g):
                        Mf, Mb = ssd["M", g]
                        bY = ssd["bY", g]
                        cx.pre("pe", [Mf, Mb, xdf, xdb, Dg, xs], [bY])
                        for hl in range(8):
                            h = 8 * g + hl
                            hc = hl * 64
                            nc.tensor.matmul(psap(bY)[:, hc:hc + 64], lhsT=Mf.t[:, hl, :], rhs=xdf.t[:, h * 64:(h + 1) * 64], start=False, stop=False)
                            nc.tensor.matmul(psap(bY)[:, hc:hc + 64], lhsT=Mb.t[:, hl, :], rhs=xdb.t[:, h * 64:(h + 1) * 64], start=False, stop=False)
                            ins = nc.tensor.matmul(psap(bY)[:, hc:hc + 64], lhsT=Dg.t[:, h, :], rhs=xs.t[:, h * 64:(h + 1) * 64], start=False, stop=(hl == 7))
                        cx.post("pe", ins, [Mf, Mb, xdf, xdb, Dg, xs], [bY])

                    def Cmb(g):
                        gs = slice(g * 512, (g + 1) * 512)
                        bY = ssd["bY", g]
                        cx.op("dve", [y32[g], bY], [y32[g]], lambda e: e.tensor_tensor(out=y32[g].t[:], in0=y32[g].t[:], in1=psap(bY), op=ALU.add))
                        cx.op("dve", [y32[g], szl], [y32[g]], lambda e: e.tensor_tensor(out=y32[g].t[:], in0=y32[g].t[:], in1=szl.t[:, gs], op=ALU.mult))
                        cx.op("act", [y32[g]], [yn, ssn4], lambda e: e.activation(out=yn.t[:, gs], in_=y32[g].t[:], func=AF.Square, accum_out=ssn4.t[:, g:g + 1]))

                    yn = cx.ring("yn")
                    R(0)
                    Dm(0)
                    for g in range(4):
                        if g < 3:
                            A1(g + 1)
                        A2(g)
                        O(g)
                        Y0(g)
                        EM(g)
                        A3(g)
                        St(g)
                        if g < 3:
                            R(g + 1)
                            Dm(g + 1)
                        A4a(g)
                        Y(g)
                        A4b(g)
                        Cmb(g)
                    if c + 1 < C:
                        nxtP = prepC(c + 1)
                    cx.op("dve", [ssn4], [ssn], lambda e: e.tensor_reduce(out=ssn.t[:, 0:1], in_=ssn4.t[:, 0:4], op=ALU.add, axis=mybir.AxisListType.X))
                    cx.op("act", [ssn, epsT], [ssn], lambda e: e.activation(out=ssn.t[:, 0:1], in_=ssn.t[:, 0:1], func=AF.Ln, scale=1.0 / 2048, bias=epsT.t[:, 0:1]))
                    cx.op("act", [ssn], [ssn], lambda e: e.activation(out=ssn.t[:, 1:2], in_=ssn.t[:, 0:1], func=AF.Exp, scale=-0.5))
                    for g in range(4):
                        cx.op("act", [y32[g], ssn], [yn], lambda e: e.activation(out=yn.t[:, g * 512:(g + 1) * 512], in_=y32[g].t[:], func=AF.Copy, scale=ssn.t[:, 1:2]))
                    for half in range(2):
                        (bt,) = cx.psum(1)
                        pv = psbf(bt).rearrange("p (k j) -> p k j", j=128)
                        cx.pre("pe", [yn, cstb], [bt])
                        for kc in range(8):
                            k2 = half * 8 + kc
                            ins = nc.tensor.transpose(out=pv[:, kc, :], in_=yn.t[:, k2 * 128:(k2 + 1) * 128], identity=ident_b)
                        cx.post("pe", ins, [yn, cstb], [bt])
                        cx.op("dve", [bt, gssd], [ssdT], lambda e: e.tensor_tensor(out=ssdT.t[:, half * 8:half * 8 + 8, :], in0=pv[:, 0:8, :],
                                                                                   in1=bc(gssd.t[:, l, half * 8:half * 8 + 8], 128), op=ALU.mult))
                    cx.dma("pool", sc["ssdT"][:, t0:t0 + 128].rearrange("(k p) t -> p k t", p=128), ssdT.t[:], [ssdT], [], ssdT, "st")
                    cx.dma("pool", sc["OT"][:, t0:t0 + 128].rearrange("(h d) t -> d h t", d=64), OT.t[:], [OT], [], OT, "st")
                cx.barrier()

        def phaseD(s, l):
            L = seq_lens[s]
            sc = S[s, l]
            last = (l == DEPTH - 1)
            with ExitStack() as pes:
                cx.mkring(pes, "wslot", 4, [128, 8, 512], BF16)
                cx.mkring(pes, "X", 2, [128, 4, 1024], F32)
                cx.mkring(pes, "xnr", 2, [128, 1024], BF16)
                cx.mkring(pes, "hT", 2, [128, 8, 512], BF16)
                cx.mkring(pes, "Pc", 3, [128, 2, 512], BF16)
                cx.mkring(pes, "rec", 3, [128, 512], F32)
                cx.mkring(pes, "sg", 4, [128, 512], BF16)
                cx.mkring(pes, "tmp", 1, [128, 512], F32)
                cx.mkring(pes, "outst", 1, [128, 1024], F32)
                cx.mkring(pes, "OTl", 1, [64, 16, 512], BF16)
                bufA = Buf(pes.enter_context(nc.sbuf_tensor(uq("bufA"), [128, 24, 512], BF16)))
                bufB = Buf(pes.enter_context(nc.sbuf_tensor(uq("bufB"), [128, 16, 512], BF16)))
                bufC = Buf(pes.enter_context(nc.sbuf_tensor(uq("bufC"), [128, 16, 512], BF16)))
                KcT = Buf(pes.enter_context(nc.sbuf_tensor(uq("KcT"), [128, 8, 256], BF16)))
                Vc = Buf(pes.enter_context(nc.sbuf_tensor(uq("Vc"), [128, 2, 1024], BF16)))
                ss = Buf(pes.enter_context(nc.sbuf_tensor(uq("ssD"), [128, 4], F32)))
                rstd = Buf(pes.enter_context(nc.sbuf_tensor(uq("rstdD"), [128, 4], F32)))
                junk = Buf(pes.enter_context(nc.sbuf_tensor(uq("junkD"), [128, 1024], BF16)))

                class View:
                    def __init__(self, buf, ap):
                        self.buf = buf
                        self.t = ap
                Xm = cx.ring("X")
                cx.dma("sp", Xm.t[:, 0:2, :], memin[s][:, :].rearrange("(ts p) d -> p ts d", p=128), [], [Xm], Xm)
                normT(Xm, 2, gmem.t[:, l, :], bufB, ss, rstd, "xnr", junk)

                def evac_k(o, bk, cg):
                    e = cx.alt()
                    if e == "act":
                        cx.op("act", [bk], [KcT], lambda e_: e_.activation(out=KcT.t[:, cg * 4 + o, :], in_=psap(bk)[:, 0:256], func=AF.Copy))
                    else:
                        cx.op("dve", [bk], [KcT], lambda e_: e_.tensor_copy(out=KcT.t[:, cg * 4 + o, :], in_=psap(bk)[:, 0:256]))

                def evac_v(o, bk, cg):
                    cx.op("dve", [bk], [Vc], lambda e_: e_.tensor_copy(out=Vc.t[:, o, cg * 512:(cg + 1) * 512], in_=psap(bk)))
                for cg in range(2):
                    gemm(bufB, 8, 128, "w_kv_cross", l, cg * 512, 512, "FM", 256, lambda o, bk, cg=cg: evac_k(o, bk, cg))
                for cg in range(2):
                    gemm(bufB, 8, 128, "w_kv_cross", l, 1024 + cg * 512, 512, "TM", 256, lambda o, bk, cg=cg: evac_v(o, bk, cg))

                for st_i in range(L // 512):
                    t0 = st_i * 512
                    X = cx.ring("X")
                    xsrc = xin[s] if l == 0 else S[s, l - 1]["xn"]
                    cx.dma("sp", X.t[:], xsrc[t0:t0 + 512, :].rearrange("(ts p) d -> p ts d", p=128), [], [X], X)
                    OTl = cx.ring("OTl")
                    cx.dma("sp", OTl.t[:], sc["OT"][:, t0:t0 + 512].rearrange("(h d) t -> d h t", d=64), [], [OTl], OTl)
                    cx.dma("sp", bufB.t[:], sc["ssdT"][:, t0:t0 + 512].rearrange("(k p) t -> p k t", p=128), [], [bufB], bufB)
                    cx.dma("sp", bufC.t[:], sc["gat"][:, t0:t0 + 512].rearrange("(k p) t -> p k t", p=128), [], [bufC], bufC)

                    def evac_pa(o, bk, cg):
                        oc = cg * 4 + o
                        cx.op("dve", [bk, bufC, bufA], [bufA], lambda e: e.tensor_tensor(out=bufA.t[:, 16 + oc, :], in0=psap(bk), in1=bufC.t[:, oc, :], op=ALU.mult))

                    def evac_ps(o, bk, cg):
                        oc = cg * 4 + o
                        tmp = cx.ring("tmp")
                        cx.op("dve", [bk, bufC], [tmp], lambda e: e.tensor_tensor(out=tmp.t[:], in0=psap(bk), in1=bufC.t[:, 8 + oc, :], op=ALU.mult))
                        cx.op("dve", [tmp, bufA], [bufA], lambda e: e.tensor_tensor(out=bufA.t[:, oc, :], in0=tmp.t[:], in1=bufA.t[:, 16 + oc, :], op=ALU.add))
                    for cg in range(2):
                        gemm(OTl, 16, 64, "w_proj_attn", l, cg * 512, 512, "FM", 512, lambda o, bk, cg=cg: evac_pa(o, bk, cg))
                    for cg in range(2):
                        gemm(bufB, 16, 128, "w_proj_ssd", l, cg * 512, 512, "FM", 512, lambda o, bk, cg=cg: evac_ps(o, bk, cg))

                    def evac_res(o, bk, cg):
                        cx.op("dve", [bk, X], [X], lambda e: e.tensor_tensor(out=X.t[:, o, cg * 512:(cg + 1) * 512], in0=psap(bk), in1=X.t[:, o, cg * 512:(cg + 1) * 512], op=ALU.add))
                    for cg in range(2):
                        gemm(bufA, 8, 128, "w_out", l, cg * 512, 512, "TM", 512, lambda o, bk, cg=cg: evac_res(o, bk, cg))
                    hT = cx.ring("hT")
                    normT(X, 4, gcross.t[:, l, :], hT, ss, rstd, "xnr", junk)

                    def evac_q(o, bk, cg):
                        e = cx.alt()
                        if e == "act":
                            cx.op("act", [bk], [bufB], lambda e_: e_.activation(out=bufB.t[:, cg * 4 + o, :], in_=psap(bk), func=AF.Copy))
                        else:
                            cx.op("dve", [bk], [bufB], lambda e_: e_.tensor_copy(out=bufB.t[:, cg * 4 + o, :], in_=psap(bk)))
                    for cg in range(2):
                        gemm(hT, 8, 128, "w_q_cross", l, cg * 512, 512, "FM", 512, lambda o, bk, cg=cg: evac_q(o, bk, cg))
                    xa = {}

                    def XL(hc):
                        Pc = cx.ring("Pc")
                        for mch in range(2):
                            (bk,) = cx.psum(1)
                            cx.pre("pe", [KcT, bufB], [bk])
                            for dc in range(2):
                                ins = nc.tensor.matmul(psap(bk), lhsT=KcT.t[:, hc * 2 + dc, mch * 128:(mch + 1) * 128], rhs=bufB.t[:, hc * 2 + dc, :],
                                                       start=(dc == 0), stop=(dc == 1))
                            cx.post("pe", ins, [KcT, bufB], [bk])
                            cx.op("act", [bk], [Pc], lambda e: e.activation(out=Pc.t[:, mch, :], in_=psap(bk), func=AF.Exp, scale=1.0 / 16.0))
                        xa[hc] = Pc

                    def XD(hc):
                        Pc = xa[hc]
                        (bd,) = cx.psum(1)
                        cx.pre("pe", [Pc, cstb], [bd])
                        for mch in range(2):
                            ins = nc.tensor.matmul(psap(bd), lhsT=ones_b, rhs=Pc.t[:, mch, :], start=(mch == 0), stop=(mch == 1))
                        cx.post("pe", ins, [Pc, cstb], [bd])
                        rec = cx.ring("rec")
                        cx.op("act", [bd], [rec], lambda e: e.activation(out=rec.t[:], in_=psap(bd), func=AF.Ln))
                        cx.op("act", [rec], [rec], lambda e: e.activation(out=rec.t[:], in_=rec.t[:], func=AF.Exp, scale=-1.0))
                        bks = []
                        for dc in range(2):
                            (bk,) = cx.psum(1)
                            cx.pre("pe", [Vc, Pc], [bk])
                            for mch in range(2):
                                ins = nc.tensor.matmul(psap(bk), lhsT=Vc.t[:, mch, (hc * 2 + dc) * 128:(hc * 2 + dc + 1) * 128], rhs=Pc.t[:, mch, :],
                                                       start=(mch == 0), stop=(mch == 1))
                            cx.post("pe", ins, [Vc, Pc], [bk])
                            bks.append(bk)
                        xa[hc] = (rec, bks)

                    def XE(hc):
                        rec, bks = xa[hc]
                        for dc in range(2):
                            bk = bks[dc]
                            cx.op("dve", [bk, rec, bufB], [bufB], lambda e: e.tensor_tensor(out=bufB.t[:, 8 + hc * 2 + dc, :], in0=psap(bk), in1=rec.t[:], op=ALU.mult))
                    XL(0)
                    for hc in range(4):
                        if hc < 3:
                            XL(hc + 1)
                        XD(hc)
                        if hc > 0:
                            XE(hc - 1)
                    XE(3)
                    ocv = View(bufB, bufB.t[:, 8:16, :])
                    for cg in range(2):
                        gemm(ocv, 8, 128, "w_o_cross", l, cg * 512, 512, "TM", 512, lambda o, bk, cg=cg: evac_res(o, bk, cg))
                    hT = cx.ring("hT")
                    normT(X, 4, gffn.t[:, l, :], hT, ss, rstd, "xnr", junk)
                    for cg in range(6):
                        nw = 512 if cg < 5 else 256
                        held = {}

                        def evac_g(o, bk, held=held):
                            sg = cx.ring("sg")
                            cx.op("act", [bk], [sg], lambda e: e.activation(out=sg.t[:], in_=psap(bk), func=AF.Silu))
                            held[o] = sg

                        def evac_u(o, bk, cg=cg, held=held):
                            sg = held[o]
                            cx.op("dve", [bk, sg, bufA], [bufA], lambda e: e.tensor_tensor(out=bufA.t[:, cg * 4 + o, :], in0=psap(bk), in1=sg.t[:], op=ALU.mult))
                        gemm(hT, 8, 128, "w_gate_up", l, cg * 512, nw, "FM", 512, evac_g)
                        gemm(hT, 8, 128, "w_gate_up", l, 2816 + cg * 512, nw, "FM", 512, evac_u)
                    for cg in range(2):
                        gemm(bufA, 22, 128, "w_down", l, cg * 512, 512, "TM", 512, lambda o, bk, cg=cg: evac_res(o, bk, cg))
                    if last:
                        for ts in range(4):
                            cx.op("act", [X], [junk, ss], lambda e: e.activation(out=junk.t[:], in_=X.t[:, ts, :], func=AF.Square, accum_out=ss.t[:, ts:ts + 1]))
                        cx.op("act", [ss, epsT], [ss], lambda e: e.activation(out=ss.t[:, 0:4], in_=ss.t[:, 0:4], func=AF.Ln, scale=1.0 / 1024, bias=epsT.t[:, 0:1]))
                        cx.op("act", [ss], [rstd], lambda e: e.activation(out=rstd.t[:, 0:4], in_=ss.t[:, 0:4], func=AF.Exp, scale=-0.5))
                        for ts in range(4):
                            ot = cx.ring("outst")
                            cx.op("dve", [X, rstd, gfin], [ot], lambda e: e.scalar_tensor_tensor(out=ot.t[:], in0=X.t[:, ts, :], scalar=rstd.t[:, ts:ts + 1], in1=gfin.t[:],
                                                                                                op0=ALU.mult, op1=ALU.mult))
                            cx.dma("pool", yout[s][t0 + ts * 128:t0 + (ts + 1) * 128, :], ot.t[:], [ot], [], ot, "st")
                    else:
                        cx.dma("pool", sc["xn"][t0:t0 + 512, :].rearrange("(ts p) d -> p ts d", p=128), X.t[:], [X], [], X, "st")
                cx.barrier()

        first = True
        nph = 0
        for s in range(NS):
            for l in range(DEPTH):
                for ph in ((phaseA, phaseA2, phaseB, phaseC, phaseD) if not os.environ.get("K_MERGE") else (phaseA, phaseA2B, phaseC, phaseD)):
                    if stop is not None and nph >= stop:
                        break
                    cx.begin_phase()
                    ph(s, l)
                    cx.end_phase()
                    nph += 1
                    if first and stop is None:
                        cast_weights(1)
                        first = False
        cx.barrier()
    return nc


_CACHE = {}


def kernel(**inputs):
    seq_lens = [2048, 4096, 4096]
    if "nc" not in _CACHE:
        _CACHE["nc"] = build(seq_lens)
    nc = _CACHE["nc"]
    cst, onehot = host_consts()
    xp = np.asarray(inputs["x_prompt"], dtype=np.float32)
    xsm = np.asarray(inputs["x_sample"], dtype=np.float32)
    mp = np.asarray(inputs["mem_prompt"], dtype=np.float32)
    ms = np.asarray(inputs["mem_sample"], dtype=np.float32)
    shared = {k: np.ascontiguousarray(np.asarray(inputs[k], dtype=np.float32)) for k in list(W_SHAPES) + list(SMALL)}
    shared["cst"] = cst
    shared["onehot"] = onehot
    in_maps = []
    for c in range(8):
        m = dict(shared)
        m["x0"] = np.ascontiguousarray(xp[c])
        m["x1"] = np.ascontiguousarray(xsm[2 * c])
        m["x2"] = np.ascontiguousarray(xsm[2 * c + 1])
        m["mem0"] = np.ascontiguousarray(mp[c])
        m["mem1"] = np.ascontiguousarray(ms[2 * c])
        m["mem2"] = np.ascontiguousarray(ms[2 * c + 1])
        in_maps.append(m)
    res = run_bass_kernel_spmd(nc, in_maps, core_ids=list(range(8)))
    y_prompt = np.stack([np.asarray(res.results[c]["y0"], dtype=np.float32) for c in range(8)], axis=0)
    ys = []
    for c in range(8):
        ys.append(np.asarray(res.results[c]["y1"], dtype=np.float32))
        ys.append(np.asarray(res.results[c]["y2"], dtype=np.float32))
    y_sample = np.stack(ys, axis=0)
    return (y_prompt, y_sample)
```
